# Optimizing a Trainium2 kernel written in Bass

```python
import jax, jax.numpy as jnp
from jax import lax
import numpy as np

D_MODEL = 1024
BATCH = 4
SEQ = 4096
DEPTH = 2

MIX_WIDTH = D_MODEL
N_MIXERS = 4
GROUP_WIDTH = MIX_WIDTH // N_MIXERS
N_GROUP_HEADS = 4
HEAD_DIM = GROUP_WIDTH // N_GROUP_HEADS
SHORT_CONV_WIDTH = 3
CHUNK = 128
IDX_HEADS = 4
IDX_DIM = 64
TOPK_MAX = 256
CONF_CONV_WIDTH = 31
Q_BLOCK = 128
ROPE_THETA = 10000.0
FFN_HIDDEN = -(-8 * D_MODEL // (3 * 256)) * 256
NORM_EPS = 1e-6
LN_EPS = 1e-5
NEG = -1e30
SPLIT_SIZES = (GROUP_WIDTH, GROUP_WIDTH, GROUP_WIDTH,
               GROUP_WIDTH, GROUP_WIDTH,
               GROUP_WIDTH, GROUP_WIDTH, GROUP_WIDTH,
               IDX_HEADS * IDX_DIM, IDX_DIM, IDX_HEADS,
               GROUP_WIDTH, GROUP_WIDTH)
IN_COLS = sum(SPLIT_SIZES)

kernel_name = "hymba_style_four_mixer_hybrid"


def rmsnorm(x, g):
    xf = x.astype(jnp.float32)
    y = xf * lax.rsqrt(jnp.mean(xf * xf, axis=-1, keepdims=True) + NORM_EPS)
    return (y * g.astype(jnp.float32)).astype(x.dtype)


def layernorm(x, g, b):
    xf = x.astype(jnp.float32)
    mu = jnp.mean(xf, axis=-1, keepdims=True)
    var = jnp.mean(jnp.square(xf - mu), axis=-1, keepdims=True)
    y = (xf - mu) * lax.rsqrt(var + LN_EPS)
    return (y * g.astype(jnp.float32) + b.astype(jnp.float32)).astype(x.dtype)


def causal_dwconv(x, w):
    width, chans = w.shape
    return lax.conv_general_dilated(
        x, w[:, None, :].astype(x.dtype), window_strides=(1,),
        padding=[(width - 1, 0)], dimension_numbers=('NWC', 'WIO', 'NWC'),
        feature_group_count=chans)


def rope_tables(seq, dim):
    inv_freq = ROPE_THETA ** (-jnp.arange(0, dim, 2, dtype=jnp.float32) / dim)
    ang = jnp.arange(seq, dtype=jnp.float32)[:, None] * inv_freq[None, :]
    return jnp.cos(ang), jnp.sin(ang)


def apply_rope(x, cos, sin):
    half = x.shape[-1] // 2
    c = cos[None, :, None, :].astype(x.dtype)
    s = sin[None, :, None, :].astype(x.dtype)
    x1, x2 = x[..., :half], x[..., half:]
    return jnp.concatenate([x1 * c - x2 * s, x2 * c + x1 * s], axis=-1)


def short_conv_mixer(b, c, h, w_conv):
    return b * causal_dwconv(c * h, w_conv)


def gmlp_mixer(u, v, ln_g, ln_b, w_s, b_s):
    bsz, seq, _ = v.shape
    vn = layernorm(v, ln_g, ln_b)
    vc = vn.reshape(bsz, seq // CHUNK, CHUNK, N_GROUP_HEADS, HEAD_DIM)
    mask = jnp.tril(jnp.ones((CHUNK, CHUNK), dtype=bool))
    ws = jnp.where(mask[None], w_s, 0).astype(v.dtype)
    mixed = jnp.einsum('hts,bnshd->bnthd', ws, vc)
    mixed = mixed + b_s.T[None, None, :, :, None].astype(v.dtype)
    return u * mixed.reshape(bsz, seq, GROUP_WIDTH)


def dsa_mixer(q, k, v, q_idx, k_idx, w_idx):
    bsz, seq, _ = q.shape
    topk = min(TOPK_MAX, seq // 4)
    cos_a, sin_a = rope_tables(seq, HEAD_DIM)
    cos_i, sin_i = rope_tables(seq, IDX_DIM)
    q = apply_rope(q.reshape(bsz, seq, N_GROUP_HEADS, HEAD_DIM), cos_a, sin_a)
    k = apply_rope(k.reshape(bsz, seq, N_GROUP_HEADS, HEAD_DIM), cos_a, sin_a)
    v = v.reshape(bsz, seq, N_GROUP_HEADS, HEAD_DIM)
    qi = apply_rope(q_idx.reshape(bsz, seq, IDX_HEADS, IDX_DIM), cos_i, sin_i).astype(jnp.float32)
    ki = apply_rope(k_idx[:, :, None, :], cos_i, sin_i)[:, :, 0, :].astype(jnp.float32)
    wi = w_idx.astype(jnp.float32) * (IDX_HEADS ** -0.5) * (IDX_DIM ** -0.5)
    scale = HEAD_DIM ** -0.5
    nblk = seq // Q_BLOCK

    def to_blocks(t):
        return jnp.swapaxes(t.reshape((bsz, nblk, Q_BLOCK) + t.shape[2:]), 0, 1)

    kpos = jnp.arange(seq)
    tpos = kpos.reshape(nblk, Q_BLOCK)

    def block(args):
        qb, qib, wib, tb = args
        rel = jax.nn.relu(jnp.einsum('bqjd,bsd->bqjs', qib, ki))
        score = jnp.einsum('bqjs,bqj->bqs', rel, wib)
        admissible = kpos[None, :] <= tb[:, None]
        score = jnp.where(admissible[None], score, NEG)
        _, idx = lax.top_k(score, topk)
        valid = idx <= tb[None, :, None]
        kg = jax.vmap(lambda kk, ii: kk[ii])(k, idx)
        vg = jax.vmap(lambda vv, ii: vv[ii])(v, idx)
        logits = jnp.einsum('bqhd,bqkhd->bqhk', qb, kg).astype(jnp.float32) * scale
        logits = jnp.where(valid[:, :, None, :], logits, NEG)
        p = jax.nn.softmax(logits, axis=-1).astype(v.dtype)
        return jnp.einsum('bqhk,bqkhd->bqhd', p, vg)

    out = lax.map(block, (to_blocks(q), to_blocks(qi), to_blocks(wi), tpos))
    return jnp.swapaxes(out, 0, 1).reshape(bsz, seq, GROUP_WIDTH)


def conformer_conv_mixer(a, gate, w_dw, b_dw, ln_g, ln_b):
    y = a * jax.nn.sigmoid(gate)
    y = causal_dwconv(y, w_dw) + b_dw.astype(y.dtype)
    y = layernorm(y, ln_g, ln_b)
    return jax.nn.silu(y)


def hybrid_layer(x, g_mix, w_in, w_conv_a, gmlp_ln_g, gmlp_ln_b, w_s, b_s,
                 w_conf, b_conf, conf_ln_g, conf_ln_b, w_out,
                 g_ffn, w_gate, w_up, w_down):
    h = rmsnorm(x, g_mix)
    z = h @ w_in
    cuts, acc = [], 0
    for sz in SPLIT_SIZES[:-1]:
        acc += sz
        cuts.append(acc)
    (a_b, a_c, a_h, g_u, g_v, q, k, v, q_idx, k_idx, w_idx, d_a, d_g) = jnp.split(z, cuts, axis=-1)
    y_a = short_conv_mixer(a_b, a_c, a_h, w_conv_a)
    y_b = gmlp_mixer(g_u, g_v, gmlp_ln_g, gmlp_ln_b, w_s, b_s)
    y_c = dsa_mixer(q, k, v, q_idx, k_idx, w_idx)
    y_d = conformer_conv_mixer(d_a, d_g, w_conf, b_conf, conf_ln_g, conf_ln_b)
    x = x + jnp.concatenate([y_a, y_b, y_c, y_d], axis=-1) @ w_out
    hf = rmsnorm(x, g_ffn)
    return x + (jax.nn.silu(hf @ w_gate) * (hf @ w_up)) @ w_down


def setup_inputs(seed: int = 0) -> dict:
    key = jax.random.key(seed)
    ks = jax.random.split(key, 20)
    f32 = jnp.float32
    G, H = GROUP_WIDTH, N_GROUP_HEADS

    def nrm(k, shape, scale):
        return jax.random.normal(k, shape, f32) * scale

    def gain(k, shape):
        return 1.0 + 0.05 * jax.random.normal(k, shape, f32)

    return {
        "x": jax.random.normal(ks[0], (BATCH, SEQ, D_MODEL), f32),
        "g_mix": gain(ks[1], (DEPTH, D_MODEL)),
        "w_in": nrm(ks[2], (DEPTH, D_MODEL, IN_COLS), D_MODEL ** -0.5),
        "w_conv_a": nrm(ks[3], (DEPTH, SHORT_CONV_WIDTH, G), SHORT_CONV_WIDTH ** -0.5),
        "gmlp_ln_g": gain(ks[4], (DEPTH, G)),
        "gmlp_ln_b": nrm(ks[5], (DEPTH, G), 0.02),
        "w_s": nrm(ks[6], (DEPTH, H, CHUNK, CHUNK), 0.5 * CHUNK ** -0.5),
        "b_s": 1.0 + nrm(ks[7], (DEPTH, H, CHUNK), 0.1),
        "w_conf": nrm(ks[8], (DEPTH, CONF_CONV_WIDTH, G), CONF_CONV_WIDTH ** -0.5),
        "b_conf": nrm(ks[9], (DEPTH, G), 0.02),
        "conf_ln_g": gain(ks[10], (DEPTH, G)),
        "conf_ln_b": nrm(ks[11], (DEPTH, G), 0.02),
        "w_out": nrm(ks[12], (DEPTH, MIX_WIDTH, D_MODEL), 0.5 * MIX_WIDTH ** -0.5),
        "g_ffn": gain(ks[13], (DEPTH, D_MODEL)),
        "w_gate": nrm(ks[14], (DEPTH, D_MODEL, FFN_HIDDEN), D_MODEL ** -0.5),
        "w_up": nrm(ks[15], (DEPTH, D_MODEL, FFN_HIDDEN), D_MODEL ** -0.5),
        "w_down": nrm(ks[16], (DEPTH, FFN_HIDDEN, D_MODEL), 0.5 * FFN_HIDDEN ** -0.5),
        "g_final": gain(ks[17], (D_MODEL,)),
    }


def reference(x, g_mix, w_in, w_conv_a, gmlp_ln_g, gmlp_ln_b, w_s, b_s,
              w_conf, b_conf, conf_ln_g, conf_ln_b, w_out,
              g_ffn, w_gate, w_up, w_down, g_final):
    for l in range(DEPTH):
        x = hybrid_layer(x, g_mix[l], w_in[l], w_conv_a[l], gmlp_ln_g[l], gmlp_ln_b[l],
                         w_s[l], b_s[l], w_conf[l], b_conf[l], conf_ln_g[l], conf_ln_b[l],
                         w_out[l], g_ffn[l], w_gate[l], w_up[l], w_down[l])
    return rmsnorm(x, g_final)
```

```python
import numpy as np
from contextlib import ExitStack
import concourse.bass as bass
import concourse.mybir as mybir
from concourse.bass_utils import run_bass_kernel_spmd

F32 = mybir.dt.float32
BF16 = mybir.dt.bfloat16
ALU = mybir.AluOpType
AF = mybir.ActivationFunctionType
AX = mybir.AxisListType

D = 1024
G = 256
HID = 2816
INC = 2884
NBIS = 15
ENGS = ("pe", "act", "dve", "pool", "sp")


class Sched:
    def __init__(self, nc, stack):
        self.nc = nc
        self.stack = stack
        self.q = {e: [] for e in ENGS}
        self.sem = {e: stack.enter_context(nc.semaphore("sem_" + e)) for e in ENGS}
        self.cnt = {e: 0 for e in ENGS}
        self.waited = {e: {} for e in ENGS}
        self.last_write = {}
        self.readers = {}
        self.dsem = {}
        self.dcnt = {}

    def _dsem(self, key):
        if key not in self.dsem:
            self.dsem[key] = self.stack.enter_context(self.nc.semaphore("dsem_%d" % len(self.dsem)))
            self.dcnt[key] = 0
        return self.dsem[key]

    def op(self, eng, fn, reads=(), writes=(), dma=None):
        deps = []
        for r in reads:
            t = self.last_write.get(r)
            if t is not None:
                deps.append(t)
        for w in writes:
            t = self.last_write.get(w)
            if t is not None:
                deps.append(t)
            deps.extend(self.readers.get(w, ()))
        wd = self.waited[eng]
        m = {}
        for (sem, val, deng) in deps:
            if deng == "pe" and eng == "pe" and dma is None:
                continue
            k = id(sem)
            if wd.get(k, 0) >= val:
                continue
            if k not in m or m[k][1] < val:
                m[k] = (sem, val)
        for k, (sem, val) in m.items():
            wd[k] = val
        waits = list(m.values())
        if dma is not None:
            sem = self._dsem(dma)
            self.dcnt[dma] += 1
            tok = (sem, 16 * self.dcnt[dma], "dma")
            inc = 16
        else:
            self.cnt[eng] += 1
            tok = (self.sem[eng], self.cnt[eng], eng)
            inc = 1
        self.q[eng].append((waits, fn, tok[0], inc))
        for w in writes:
            self.last_write[w] = tok
            self.readers[w] = []
        for r in reads:
            if r in writes:
                continue
            lst = self.readers.setdefault(r, [])
            lst[:] = [x for x in lst if not (x[0] is tok[0])]
            lst.append(tok)
        return tok

    def final_wait(self, eng, keys):
        waits = []
        for k in keys:
            t = self.last_write.get(k)
            if t is not None:
                waits.append((t[0], t[1]))
        self.q[eng].append((waits, None, None, 0))

    def emit(self):
        nc = self.nc
        import os
        if os.environ.get("KDUMP"):
            names = {id(v): k for k, v in self.sem.items()}
            names.update({id(v): "d:" + str(k) for k, v in self.dsem.items()})
            for e in ENGS:
                print("ENG", e, "n=", len(self.q[e]), "cnt=", self.cnt[e])
                for (waits, fn, sem, inc) in self.q[e][-6:]:
                    print("   waits", [(names.get(id(s_), "?"), v) for (s_, v) in waits], "inc", names.get(id(sem)), inc)
        with nc.Block() as block:
            def run(eng_name):
                def body(e):
                    for (waits, fn, sem, inc) in self.q[eng_name]:
                        for (s, v) in waits:
                            e.wait_ge(s, v)
                        if fn is not None:
                            fn(e).then_inc(sem, inc)
                return body
            block.tensor(run("pe"))
            block.scalar(run("act"))
            block.vector(run("dve"))
            block.gpsimd(run("pool"))
            block.sync(run("sp"))


C_B, C_C, C_H, C_U, C_V, C_Q, C_K, C_VV, C_QI, C_KI, C_WI, C_DA, C_DG = (
    0, 256, 512, 768, 1024, 1280, 1536, 1792, 2048, 2304, 2368, 2372, 2628)
def chunk_table():
    t = []
    t.append((0, 1024, [("w_in", C_B, 512, 0)]))
    t.append((0, 1024, [("w_in", C_H, 256, 0), ("w_in", C_DA, 256, 256)]))
    t.append((0, 1024, [("w_in", C_DG, 256, 0), ("w_in", C_KI, 68, 256)]))
    t.append((0, 1024, [("w_in", C_U, 512, 0)]))
    t.append((0, 1024, [("w_in", C_Q, 512, 0)]))
    t.append((0, 1024, [("w_in", C_VV, 512, 0)]))
    t.append((0, 1024, [("w_out", 0, 512, 0)]))
    t.append((0, 1024, [("w_out", 512, 512, 0)]))
    for j in range(11):
        t.append((0, 1024, [("w_gate", j * 256, 256, 0), ("w_up", j * 256, 256, 256)]))
    for cg in range(2):
        for g in range(3):
            nr = 1024 if g < 2 else 768
            t.append((g * 1024, nr, [("w_down", cg * 512, 512, 0)]))
    return t


NCHUNK = 25


class _Stop(Exception):
    pass


def build(SEQ, DEPTH, dbg=False, stop=None):
    nc = bass.Bass("TRN2", target_bir_lowering=False)

    def stage(n):
        if stop is not None and n > stop:
            raise _Stop()
    NTILE = SEQ // 512
    NB = SEQ // 128

    def din(name, shape):
        return nc.dram_tensor(name, list(shape), F32, kind="ExternalInput").ap()

    x_in = din("x", [SEQ, D])
    P = {}
    for name, shape in [("g_mix", [DEPTH, D]), ("w_in", [DEPTH, D, INC]), ("w_conv_a", [DEPTH, 3, G]),
                        ("gmlp_ln_g", [DEPTH, G]), ("gmlp_ln_b", [DEPTH, G]), ("w_s", [DEPTH, 4, 128, 128]),
                        ("b_s", [DEPTH, 4, 128]), ("w_conf", [DEPTH, 31, G]), ("b_conf", [DEPTH, G]),
                        ("conf_ln_g", [DEPTH, G]), ("conf_ln_b", [DEPTH, G]), ("w_out", [DEPTH, D, D]),
                        ("g_ffn", [DEPTH, D]), ("w_gate", [DEPTH, D, HID]), ("w_up", [DEPTH, D, HID]),
                        ("w_down", [DEPTH, HID, D]), ("g_final", [D])]:
        P[name] = din(name, shape)
    c_ident = din("c_ident", [128, 128])
    c_tril = din("c_tril", [128, 128])
    c_cb = din("c_cb", [128, 128])
    c_cos = din("c_cos", [128, NB, 32])
    c_sin = din("c_sin", [128, NB, 32])
    c_pow2 = din("c_pow2", [128, NBIS + 2])
    out = nc.dram_tensor("out", [SEQ, D], F32, kind="ExternalOutput").ap()
    wsc = nc.dram_tensor("wsc", [DEPTH, NCHUNK, 1024, 512], BF16, kind="Internal").ap()
    xs = nc.dram_tensor("xs", [SEQ, D], F32, kind="Internal").ap()
    dbg_outs = {}

    with ExitStack() as st:
        S = Sched(nc, st)

        def sb(name, shape, dt):
            return st.enter_context(nc.sbuf_tensor(name, list(shape), dt))

        def psum(name, shape, dt):
            return st.enter_context(nc.psum_tensor(name, list(shape), dt))

        wring = [sb("wring%d" % i, [128, 8, 512], BF16) for i in range(3)]
        xt = sb("xt", [128, 4, 1024], F32)
        hy = sb("hy", [128, 4096], BF16)
        hn = hy[:, :].rearrange("p (s d) -> p s d", s=4)
        yT = hy[:, :].rearrange("p (k t) -> p k t", k=8)
        hT = sb("hT", [128, 8, 512], BF16)
        hTb = sb("hTb", [128, 8, 512], BF16)
        bT = sb("bT", [128, 2, 512], BF16)
        cT = sb("cT", [128, 2, 512], BF16)
        ch = sb("ch", [128, 2, 514], BF16)
        accA = sb("accA", [128, 512], F32)
        daT = sb("daT", [128, 2, 512], BF16)
        sig = sb("sig", [128, 512], BF16)
        dT = sb("dT", [128, 2, 544], BF16)
        kraw = sb("kraw", [128, 4, 64], F32)
        wraw = sb("wraw", [128, 4, 4], F32)
        wabs = sb("wabs", [128, 4, 4], F32)
        sgn = sb("sgn", [128, 4, 4], F32)
        ub = sb("ub", [128, 256], BF16)
        vn32 = sb("vn32", [128, 256], F32)
        vnb = sb("vnb", [128, 256], BF16)
        v32 = sb("v32", [128, 256], F32)
        qk32 = sb("qk32", [128, 512], F32)
        o32 = sb("o32", [128, 260], F32)
        wsf = qk32[:, :].rearrange("p (h s) -> p h s", h=4)
        yb = sb("yb", [128, 256], BF16)
        ropeA = sb("ropeA", [128, 512], F32)
        ropeB = sb("ropeB", [128, 512], F32)
        junk = ropeA[:, :].bitcast(BF16)
        qkr = sb("qkr", [128, 512], BF16)
        qbd = sb("qbd", [128, 2, 4, 256], BF16)
        jk = sb("jk", [128, 4], BF16)
        qiT_all = sb("qiT_all", [128, 2, 512], BF16)
        qib = sb("qib", [128, 256], BF16)
        kib = sb("kib", [128, 128], BF16)
        dsgn = sb("dsgn", [128, 4, 128], BF16)
        kT = sb("kT", [128, 2, SEQ], BF16)
        kiT2 = sb("kiT2", [128, SEQ], BF16)
        V = sb("V", [128, NB, 260], BF16)
        big = sb("big", [128, 12288], BF16)
        rbuf = sb("rbuf", [128, 4, 512], BF16)
        PT = [sb("PT%d" % i, [128, 512], BF16) for i in range(2)]
        yc = sb("yc", [128, 256], BF16)
        rec = sb("rec", [128, 4], F32)
        t1 = sb("t1", [128, 256], F32)
        yd = sb("yd", [128, 256], BF16)
        cosT = sb("cosT", [128, 4, 32], F32)
        sinT = sb("sinT", [128, 4, 32], F32)
        sg = [sb("sg%d" % i, [128, 512], BF16) for i in range(2)]
        diag = sb("diag", [128, 2, 31, 128], BF16)
        wsT = sb("wsT", [128, 4, 128], BF16)
        wsm = sb("wsm", [128, 4, 128], BF16)
        identf = sb("identf", [128, 128], F32)
        identb = sb("identb", [128, 128], BF16)
        I4 = sb("I4", [128, 4, 128], BF16)
        tril = sb("tril", [128, 128], F32)
        cb = sb("cb", [128, 128], F32)
        pow2 = sb("pow2", [128, NBIS + 2], F32)
        lnB = sb("lnB", [128, 5, 256], F32)
        gfinB = sb("gfinB", [128, 1024], F32)
        gT = sb("gT", [128, 2, 8], F32)
        wcaT = sb("wcaT", [128, 2, 3], F32)
        wcfT = sb("wcfT", [128, 2, 31], F32)
        bsT = sb("bsT", [128, 4], F32)
        ssq = sb("ssq", [128, 4], F32)
        rstd = sb("rstd", [128, 4], F32)
        st6 = sb("st6", [128, 6], F32)
        mv = sb("mv", [128, 2], F32)
        lrs = sb("lrs", [128, 1], F32)
        amax = sb("amax", [128, 1], F32)
        Wt = sb("Wt", [128, NBIS + 2], F32)
        cntb = sb("cntb", [128, NBIS + 2], F32)
        mid = [sb("mid%d" % i, [128, 1], F32) for i in range(2)]
        tq = sb("tq", [128, 1], F32)
        thr = sb("thr", [128, 1], F32)
        score_t = big[:, 0:2 * SEQ].bitcast(F32)
        mb = big[:, 8192:8192 + SEQ]
        qi32 = sb("qi32", [128, 256], F32)
        ki32 = sb("ki32", [128, 64], F32)
        aT = big[:, :].rearrange("p (k t) -> p k t", k=24)

        pT = [psum("pT%d" % i, [128, 8, 128], BF16) for i in range(2)]
        pM = [psum("pM%d" % i, [128, 512], F32) for i in range(4)]
        pO = psum("pO", [128, 512], F32)
        pD = psum("pD", [128, 512], F32)

        def K_ps(i):
            return ("pM", i)

        def act(out_, in_, func, r, w, **kw):
            S.op("act", lambda e: e.activation(out=out_, in_=in_, func=func, **kw), r, w)

        def tt(out_, in0, in1, op, r, w, eng="dve"):
            S.op(eng, lambda e: e.tensor_tensor(out=out_, in0=in0, in1=in1, op=op), r, w)

        def ts(out_, in0, s1, s2, op0, op1, r, w, eng="dve", **kw):
            if op1 is None:
                S.op(eng, lambda e: e.tensor_scalar(out=out_, in0=in0, scalar1=s1, scalar2=None, op0=op0, **kw), r, w)
            else:
                S.op(eng, lambda e: e.tensor_scalar(out=out_, in0=in0, scalar1=s1, scalar2=s2, op0=op0, op1=op1, **kw), r, w)

        def stt(out_, in0, scalar, in1, op0, op1, r, w, eng="dve"):
            S.op(eng, lambda e: e.scalar_tensor_tensor(out=out_, in0=in0, scalar=scalar, in1=in1, op0=op0, op1=op1), r, w)

        def cp(out_, in_, r, w, eng="dve"):
            S.op(eng, lambda e: e.tensor_copy(out=out_, in_=in_), r, w)

        def mm(out_, lhsT, rhs, start, stop, r, w):
            S.op("pe", lambda e: e.matmul(out_, lhsT=lhsT, rhs=rhs, start=start, stop=stop, skip_group_check=True), r, w)

        def tr(out_, in_, r, w):
            S.op("pe", lambda e: e.transpose(out=out_, in_=in_, identity=identb[:, :]), list(r) + ["identb"], w)

        def dma(eng, out_, in_, r, w, key, slow=False):
            if slow:
                S.op(eng, lambda e: e.dma_start(out=out_, in_=in_, allow_slow_non_contiguous=True), r, w, dma=key)
            else:
                S.op(eng, lambda e: e.dma_start(out=out_, in_=in_), r, w, dma=key)

        ntap = [0]

        def tap(name, ap, shape, rkeys):
            if not dbg:
                return
            d = nc.dram_tensor("dbg_" + name, list(shape), ap.dtype, kind="ExternalOutput").ap()
            dbg_outs[name] = d
            ntap[0] += 1
            dma("sp", d, ap, rkeys, ["dbg_" + name], ("dbg", ntap[0]))

        table = chunk_table()
        import os
        NCAST = int(os.environ.get("KNCAST", "1000"))
        for l in range(DEPTH):
            for ci, (r0, nr, cols) in enumerate(table):
                if ci >= NCAST:
                    continue
                for (src, c0, n, d0) in cols:
                    dma("pool", wsc[l, ci, 0:nr, d0:d0 + n], P[src][l, r0:r0 + nr, c0:c0 + n],
                        [], [("wsc", l, ci)], ("cast", l, ci))

        dma("sp", identf[:, :], c_ident, [], ["identf"], "c0")
        dma("sp", tril[:, :], c_tril, [], ["tril"], "c1")
        dma("sp", cb[:, :], c_cb, [], ["cb"], "c2")
        dma("sp", pow2[:, :], c_pow2, [], ["pow2"], "c5")
        dma("sp", gfinB[:, :], P["g_final"].partition_broadcast(128), [], ["gfinB"], "c6")
        cp(identb[:, :], identf[:, :], ["identf"], ["identb"])
        for h in range(4):
            ts(I4[:, h, :], identf[:, :], 30000.0, None, ALU.mult, None, ["identf"], ["I4"])
        Vv = V[:, :, :].rearrange("p b (h e) -> p b h e", h=4)
        S.op("dve", lambda e: e.memset(V[:, :, :], 1.0), [], ["V"])
        S.op("dve", lambda e: e.memset(qbd[:, :, :, :], 0.0), [], ["qbd"])

        ring_ctr = [0]

        grp_ctr = {"ffn": 0}

        def load_chunk(l, ci, nkc=8, group=None):
            if group == "ffn":
                slot = grp_ctr["ffn"] % 2
                grp_ctr["ffn"] += 1
            elif group == "front":
                slot = 2
            else:
                slot = ring_ctr[0] % 3
                ring_ctr[0] += 1
            ncol = max(d0 + n for (src_, c0, n, d0) in table[ci][2])
            src = wsc[l, ci].rearrange("(k p) n -> p k n", p=128)
            dma("sp", wring[slot][:, 0:nkc, 0:ncol], src[:, 0:nkc, 0:ncol], [("wsc", l, ci)], [("wr", slot)], ("wr", slot))
            return slot, wring[slot]

        tcount = [0]

        def transpose2(src_aps, src_keys, dst_ap, dst_keys, evac="dve"):
            i = tcount[0] % 2
            tcount[0] += 1
            n = len(src_aps)
            for j, a in enumerate(src_aps):
                tr(pT[i][:, j, :], a, src_keys, [("pT", i)])
            srcp = pT[i][:, 0:n, :] if n > 1 else pT[i][:, 0, :]
            if evac == "act":
                act(dst_ap, srcp, AF.Copy, [("pT", i)], dst_keys)
            else:
                cp(dst_ap, srcp, [("pT", i)], dst_keys)

        def rmsnorm_to_hT(gidx):
            S.op("dve", lambda e: e.memset(ssq[:, :], 0.0), [], [("ssq", i) for i in range(4)])
            for s in range(4):
                act(junk[:, :], xt[:, s, :], AF.Square, [("xt", s)], ["ropeA", ("ssq", s)], accum_out=ssq[:, s:s + 1])
            for s in range(4):
                ts(rstd[:, s:s + 1], ssq[:, s:s + 1], 1.0 / D, 1e-6, ALU.mult, ALU.add, [("ssq", s)], [("rstd", s)])
                act(rstd[:, s:s + 1], rstd[:, s:s + 1], AF.Sqrt, [("rstd", s)], [("rstd", s)])
                S.op("dve", lambda e, s=s: e.reciprocal(out=rstd[:, s:s + 1], in_=rstd[:, s:s + 1]),
                     [("rstd", s)], [("rstd", s)])
                act(hn[:, s, :], xt[:, s, :], AF.Copy, [("xt", s), ("rstd", s)], [("hy", s)], scale=rstd[:, s:s + 1])
            for kc in range(8):
                i = tcount[0] % 2
                tcount[0] += 1
                for s in range(4):
                    tr(pT[i][:, s, :], hn[:, s, kc * 128:(kc + 1) * 128], [("hy", s)], [("pT", i)])
                src = pT[i][:, 0:4, :].rearrange("p s t -> p (s t)")
                if kc % 2 == 0:
                    ts(hTb[:, kc, :], src, gT[:, gidx, kc:kc + 1], None, ALU.mult, None, [("pT", i), "gT"], [("hTb", kc)])
                else:
                    act(hTb[:, kc, :], src, AF.Copy, [("pT", i), "gT"], [("hTb", kc)], scale=gT[:, gidx, kc:kc + 1])

        def run(g, n=1):
            for _ in range(n):
                try:
                    next(g)
                except StopIteration:
                    return False
            return True

        pfc = [0]

        def nb():
            i = pfc[0] % 2
            pfc[0] += 1
            return (pD, "pD") if i == 0 else (pO, "pO")

        ssq2 = sb("ssq2", [128, 4], F32)
        rstd2 = sb("rstd2", [128, 4], F32)
        xstg = rbuf[:, :, :].rearrange("p a b -> p (a b)").bitcast(F32)
        RB_ALL = [("rbuf", j) for j in range(4)]
        XT_ALL = [("xt", s_) for s_ in range(4)]
        HT_ALL = [("hT", k) for k in range(8)]
        HTB_ALL = [("hTb", k) for k in range(8)]
        HY_ALL = [("hy", k) for k in range(4)]
        pmc = [0]

        def next_pm():
            i = pmc[0] % 4
            pmc[0] += 1
            return i

        def rope(dst, src_ps, src_keys, ngrp, blk, dst_keys):
            n = ngrp * 64
            if ngrp == 1:
                rk = list(src_keys) + ["cosT", "sinT"]
                cs = cosT[:, blk % 4, :]
                sn = sinT[:, blk % 4, :]
                tt(ropeA[:, 0:32], src_ps[:, 0:32], cs, ALU.mult, rk, ["ropeA"])
                tt(ropeA[:, 32:64], src_ps[:, 32:64], cs, ALU.mult, rk, ["ropeA"])
                tt(ropeB[:, 0:32], src_ps[:, 32:64], sn, ALU.mult, rk, ["ropeB"])
                tt(ropeB[:, 32:64], src_ps[:, 0:32], sn, ALU.mult, rk, ["ropeB"])
                tt(dst[:, 0:32], ropeA[:, 0:32], ropeB[:, 0:32], ALU.subtract, ["ropeA", "ropeB"], dst_keys)
                tt(dst[:, 32:64], ropeA[:, 32:64], ropeB[:, 32:64], ALU.add, ["ropeA", "ropeB"], dst_keys)
                return
            v = src_ps.rearrange("p (g h d) -> p g h d", g=ngrp, h=2)
            A = ropeA[:, 0:n].rearrange("p (g h d) -> p g h d", g=ngrp, h=2)
            B = ropeB[:, 0:n].rearrange("p (g h d) -> p g h d", g=ngrp, h=2)
            cs = cosT[:, blk % 4:blk % 4 + 1, :].to_broadcast([128, ngrp, 32])
            sn = sinT[:, blk % 4:blk % 4 + 1, :].to_broadcast([128, ngrp, 32])
            rk = list(src_keys) + ["cosT", "sinT"]
            tt(A[:, :, 0, :], v[:, :, 0, :], cs, ALU.mult, rk, ["ropeA"])
            tt(A[:, :, 1, :], v[:, :, 1, :], cs, ALU.mult, rk, ["ropeA"])
            tt(B[:, :, 0, :], v[:, :, 1, :], sn, ALU.mult, rk, ["ropeB"])
            tt(B[:, :, 1, :], v[:, :, 0, :], sn, ALU.mult, rk, ["ropeB"])
            dv = dst.rearrange("p (g h d) -> p g h d", g=ngrp, h=2)
            tt(dv[:, :, 0, :], A[:, :, 0, :], B[:, :, 0, :], ALU.subtract, ["ropeA", "ropeB"], dst_keys)
            tt(dv[:, :, 1, :], A[:, :, 1, :], B[:, :, 1, :], ALU.add, ["ropeA", "ropeB"], dst_keys)

        def layernorm_tok(src, src_keys, dst, dst_keys, gi, bi):
            S.op("dve", lambda e: e.bn_stats(out=st6[:, :], in_=src), src_keys, ["st6"])
            S.op("dve", lambda e: e.bn_aggr(out=mv[:, :], in_=st6[:, :]), ["st6"], ["mv"])
            ts(lrs[:, :], mv[:, 1:2], 1e-5, None, ALU.add, None, ["mv"], ["lrs"])
            act(lrs[:, :], lrs[:, :], AF.Sqrt, ["lrs"], ["lrs"])
            S.op("dve", lambda e: e.reciprocal(out=lrs[:, :], in_=lrs[:, :]), ["lrs"], ["lrs"])
            ts(vn32[:, :], src, mv[:, 0:1], lrs[:, 0:1], ALU.subtract, ALU.mult, list(src_keys) + ["mv", "lrs"], ["vn32"])
            tt(vn32[:, :], vn32[:, :], lnB[:, gi, :], ALU.mult, ["vn32", "lnB"], ["vn32"])
            tt(dst, vn32[:, :], lnB[:, bi, :], ALU.add, ["vn32", "lnB"], dst_keys)

        try:
          stage(1)
          for l in range(DEPTH):
              x_src = x_in if l == 0 else xs
              last = (l == DEPTH - 1)
              pk = ("lp", l)
              dma("sp", gT[:, 0, :], P["g_mix"][l].rearrange("(k p) -> p k", p=128), [], ["gT"], "lp0", slow=True)
              dma("sp", gT[:, 1, :], P["g_ffn"][l].rearrange("(k p) -> p k", p=128), [], ["gT"], "lp0", slow=True)
              for cc in range(2):
                  dma("sp", wcaT[:, cc, :], P["w_conv_a"][l][:, cc * 128:(cc + 1) * 128].rearrange("t p -> p t"),
                      [], ["wcaT"], "lp1", slow=True)
                  dma("sp", wcfT[:, cc, :], P["w_conf"][l][:, cc * 128:(cc + 1) * 128].rearrange("t p -> p t"),
                      [], ["wcfT"], "lp2", slow=True)
              dma("sp", bsT[:, :], P["b_s"][l].rearrange("h t -> t h"), [], ["bsT"], "lp3", slow=True)
              for i, nm in enumerate(["gmlp_ln_g", "gmlp_ln_b", "conf_ln_g", "conf_ln_b", "b_conf"]):
                  dma("sp", lnB[:, i, :], P[nm][l].partition_broadcast(128), [], ["lnB"], "lp4")
              dma("sp", wsf[:, :, :], P["w_s"][l].rearrange("h t s -> t h s"), [], ["qk32"], "lp5")
              for h in range(4):
                  tt(wsm[:, h, :], wsf[:, h, :], tril[:, :], ALU.mult, ["qk32", "tril"], ["wsm"])
              transpose2([wsm[:, h, :] for h in range(4)], ["wsm"], wsT[:, :, :], ["wsT"])
              for cc in range(2):
                  for tp in range(31):
                      ts(diag[:, cc, tp, :], identb[:, :], wcfT[:, cc, tp:tp + 1], None, ALU.mult, None,
                         ["identb", "wcfT"], ["diag"])
              stage(2)
              S.op("dve", lambda e: e.memset(ch[:, :, 0:2], 0.0), [], ["ch"])
              S.op("dve", lambda e: e.memset(dT[:, :, 0:32], 0.0), [], ["dT"])

              def gen_front(t):
                  dma("pool", cosT[:, :, :], c_cos[:, t * 4:(t + 1) * 4, :], [], ["cosT"], "c3")
                  dma("pool", sinT[:, :, :], c_sin[:, t * 4:(t + 1) * 4, :], [], ["sinT"], "c4")
                  S.op("dve", lambda e: e.memset(ssq2[:, :], 0.0), [], [("ssq2", i) for i in range(4)])
                  for s in range(4):
                      r0 = (t * 4 + s) * 128
                      dma("pool", xstg, x_src[r0:r0 + 128, :], ["xs"] if l > 0 else [], RB_ALL, "xstg")
                      act(junk[:, :], xstg, AF.Square, RB_ALL, ["ropeA", ("ssq2", s)], accum_out=ssq2[:, s:s + 1])
                      ts(rstd2[:, s:s + 1], ssq2[:, s:s + 1], 1.0 / D, 1e-6, ALU.mult, ALU.add, [("ssq2", s)], [("rstd2", s)])
                      act(rstd2[:, s:s + 1], rstd2[:, s:s + 1], AF.Sqrt, [("rstd2", s)], [("rstd2", s)])
                      S.op("dve", lambda e, s=s: e.reciprocal(out=rstd2[:, s:s + 1], in_=rstd2[:, s:s + 1]),
                           [("rstd2", s)], [("rstd2", s)])
                      act(hn[:, s, :], xstg, AF.Copy, RB_ALL + [("rstd2", s)], [("hy", s)], scale=rstd2[:, s:s + 1])
                      yield
                  for kc in range(8):
                      yield "sync"
                      i = tcount[0] % 2
                      tcount[0] += 1
                      for s in range(4):
                          tr(pT[i][:, s, :], hn[:, s, kc * 128:(kc + 1) * 128], [("hy", s)], [("pT", i)])
                      src = pT[i][:, 0:4, :].rearrange("p s t -> p (s t)")
                      if kc % 2 == 0:
                          ts(hT[:, kc, :], src, gT[:, 0, kc:kc + 1], None, ALU.mult, None, [("pT", i), "gT"], [("hT", kc)])
                      else:
                          act(hT[:, kc, :], src, AF.Copy, [("pT", i), "gT"], [("hT", kc)], scale=gT[:, 0, kc:kc + 1])
                      yield
                  slot, W = load_chunk(l, 0, group="front")
                  for m in range(4):
                      pb, pk = nb()
                      for kc in range(8):
                          mm(pb[:, :], W[:, kc, m * 128:(m + 1) * 128], hT[:, kc, :], kc == 0, kc == 7,
                             [("wr", slot)] + HT_ALL, [pk])
                      dst = bT if m < 2 else cT
                      act(dst[:, m % 2, :], pb[:, :], AF.Copy, [pk], ["bT" if m < 2 else "cT"])
                      yield
                  slot, W = load_chunk(l, 1, group="front")
                  for m in range(4):
                      pb, pk = nb()
                      for kc in range(8):
                          mm(pb[:, :], W[:, kc, m * 128:(m + 1) * 128], hT[:, kc, :], kc == 0, kc == 7,
                             [("wr", slot)] + HT_ALL, [pk])
                      if m < 2:
                          tt(ch[:, m, 2:514], pb[:, :], cT[:, m, :], ALU.mult, [pk, "cT"], ["ch"])
                      else:
                          act(daT[:, m - 2, :], pb[:, :], AF.Copy, [pk], ["daT"])
                      yield
                  for cc in range(2):
                      ts(accA[:, :], ch[:, cc, 2:514], wcaT[:, cc, 2:3], None, ALU.mult, None, ["ch", "wcaT"], ["accA"])
                      stt(accA[:, :], ch[:, cc, 1:513], wcaT[:, cc, 1:2], accA[:, :], ALU.mult, ALU.add,
                          ["ch", "wcaT", "accA"], ["accA"])
                      stt(accA[:, :], ch[:, cc, 0:512], wcaT[:, cc, 0:1], accA[:, :], ALU.mult, ALU.add,
                          ["ch", "wcaT", "accA"], ["accA"])
                      tt(yT[:, cc, :], accA[:, :], bT[:, cc, :], ALU.mult, ["accA", "bT"], [("hy", 0)])
                      yield
                  cp(ch[:, :, 0:2], ch[:, :, 512:514], ["ch"], ["ch"])
                  stage(4)
                  slot, W = load_chunk(l, 2, group="front")
                  for m in range(2):
                      pb, pk = nb()
                      for kc in range(8):
                          mm(pb[:, :], W[:, kc, m * 128:(m + 1) * 128], hT[:, kc, :], kc == 0, kc == 7,
                             [("wr", slot)] + HT_ALL, [pk])
                      act(sig[:, :], pb[:, :], AF.Sigmoid, [pk], ["sig"])
                      tt(dT[:, m, 32:544], daT[:, m, :], sig[:, :], ALU.mult, ["daT", "sig"], ["dT"])
                      yield
                  for s in range(4):
                      pb, pk = nb()
                      for kc in range(8):
                          mm(pb[:, 0:68], hT[:, kc, s * 128:(s + 1) * 128], W[:, kc, 256:324], kc == 0, kc == 7,
                             [("wr", slot)] + HT_ALL, [pk])
                      cp(kraw[:, s, :], pb[:, 0:64], [pk], ["kraw"])
                      cp(wraw[:, s, :], pb[:, 64:68], [pk], ["wraw"])
                      yield
                  stage(4.3)
                  ts(sgn[:, :, :], wraw[:, :, :], 0.0, 2.0, ALU.is_ge, ALU.mult, ["wraw"], ["sgn"])
                  ts(sgn[:, :, :], sgn[:, :, :], -1.0, None, ALU.add, None, ["sgn"], ["sgn"])
                  stt(wabs[:, :, :], wraw[:, :, :], 1.0 / 16.0, sgn[:, :, :], ALU.mult, ALU.mult, ["wraw", "sgn"], ["wabs"])
                  slot, W = load_chunk(l, 5, group="front")
                  for s in range(4):
                      blk = t * 4 + s
                      pb, pk = nb()
                      for kc in range(8):
                          mm(pb[:, :], hT[:, kc, s * 128:(s + 1) * 128], W[:, kc, :], kc == 0, kc == 7,
                             [("wr", slot)] + HT_ALL, [pk])
                      for h in range(4):
                          act(V[:, blk, h * 65:h * 65 + 64], pb[:, h * 64:(h + 1) * 64], AF.Copy,
                              [pk], ["V"])
                      stage(6.5)
                      yield
                      act(qk32[:, 0:256], pb[:, 256:512], AF.Copy, [pk], ["qk32"])
                      rope(qi32[:, :], qk32[:, 0:256], ["qk32"], 4, blk, ["qi32"])
                      for h in range(4):
                          ts(qib[:, h * 64:(h + 1) * 64], qi32[:, h * 64:(h + 1) * 64], wabs[:, s, h:h + 1], None,
                             ALU.mult, None, ["qi32", "wabs"], ["qib"])
                      stage(6.7)
                      yield
                      yield "sync"
                      transpose2([qib[:, 0:128], qib[:, 128:256]], ["qib"],
                                 qiT_all[:, :, s * 128:(s + 1) * 128], ["qiT_all"], evac="act")
                      stage(6.8)
                      rope(ki32[:, :], kraw[:, s, :], ["kraw"], 1, blk, ["ki32"])
                      stage(6.9)
                      yield
                      cp(kib[:, 0:64], ki32[:, :], ["ki32"], ["kib"])
                      cp(kib[:, 64:128], ki32[:, :], ["ki32"], ["kib"])
                      stage(6.91 + 0.02 * s)
                      yield "sync"
                      transpose2([kib[:, :]], ["kib"], kiT2[:, blk * 128:(blk + 1) * 128], ["kiT2"], evac="act")
                      stage(6.92 + 0.02 * s)
                      yield

              def gen_gateup(t):
                  for j in range(11):
                      slw, Wgu = load_chunk(l, 8 + j, group="ffn")
                      for m in range(2):
                          pg = next_pm()
                          pu = next_pm()
                          for kc in range(8):
                              mm(pM[pg][:, :], Wgu[:, kc, m * 128:(m + 1) * 128], hTb[:, kc, :], kc == 0, kc == 7,
                                 [("wr", slw)] + HTB_ALL, [K_ps(pg)])
                          for kc in range(8):
                              mm(pM[pu][:, :], Wgu[:, kc, 256 + m * 128:256 + (m + 1) * 128], hTb[:, kc, :], kc == 0, kc == 7,
                                 [("wr", slw)] + HTB_ALL, [K_ps(pu)])
                          ai = j * 2 + m
                          sgi = ai % 2
                          act(sg[sgi][:, :], pM[pg][:, :], AF.Silu, [K_ps(pg)], [("sg", sgi)])
                          tt(aT[:, ai, :], pM[pu][:, :], sg[sgi][:, :], ALU.mult, [K_ps(pu), ("sg", sgi)],
                             [("aT", ai), "score" if ai < 16 else "mb"])
                          yield

              def gen_down(t):
                  AT_ALL = [("aT", i) for i in range(22)] + ["score", "mb"]
                  for cg in range(2):
                      pis = [next_pm() for _ in range(4)]
                      for g in range(3):
                          nkc = 8 if g < 2 else 6
                          slot, W = load_chunk(l, 19 + cg * 3 + g, nkc, group="ffn")
                          for s in range(4):
                              for kc in range(nkc):
                                  kk = g * 8 + kc
                                  mm(pM[pis[s]][:, :], aT[:, kk, s * 128:(s + 1) * 128], W[:, kc, :],
                                     kk == 0, kk == 21, [("wr", slot)] + AT_ALL, [K_ps(pis[s])])
                          yield
                      for s in range(4):
                          tt(xt[:, s, cg * 512:(cg + 1) * 512], pM[pis[s]][:, :], xt[:, s, cg * 512:(cg + 1) * 512],
                             ALU.add, [K_ps(pis[s]), ("xt", s)], [("xt", s)])
                      yield
                  stage(10)
                  if last:
                      S.op("dve", lambda e: e.memset(ssq[:, :], 0.0), [], [("ssq", i) for i in range(4)])
                      for s in range(4):
                          act(junk[:, :], xt[:, s, :], AF.Square, [("xt", s)], ["ropeA", ("ssq", s)],
                              accum_out=ssq[:, s:s + 1])
                      for s in range(4):
                          ts(rstd[:, s:s + 1], ssq[:, s:s + 1], 1.0 / D, 1e-6, ALU.mult, ALU.add, [("ssq", s)], [("rstd", s)])
                          act(rstd[:, s:s + 1], rstd[:, s:s + 1], AF.Sqrt, [("rstd", s)], [("rstd", s)])
                          S.op("dve", lambda e, s=s: e.reciprocal(out=rstd[:, s:s + 1], in_=rstd[:, s:s + 1]),
                               [("rstd", s)], [("rstd", s)])
                          stt(xt[:, s, :], xt[:, s, :], rstd[:, s:s + 1], gfinB[:, :], ALU.mult, ALU.mult,
                              [("xt", s), ("rstd", s), "gfinB"], [("xt", s)])
                      dma("pool", out[t * 512:(t + 1) * 512, :].rearrange("(s p) d -> p s d", p=128), xt[:, :, :],
                          XT_ALL, ["out"], "ost")
                  else:
                      dma("pool", xs[t * 512:(t + 1) * 512, :].rearrange("(s p) d -> p s d", p=128), xt[:, :, :],
                          XT_ALL, ["xs"], "ost")

              def load_xt(t):
                  dma("pool", xt[:, :, :], x_src[t * 512:(t + 1) * 512, :].rearrange("(s p) d -> p s d", p=128),
                      ["xs"] if l > 0 else [], XT_ALL, "xt")

              gfr0 = gen_front(0)
              while run(gfr0):
                  pass
              load_xt(0)
              for t in range(NTILE):
                  def gen_filler():
                      stage(4.5)
                      cbanks = [(pD, "pD"), (pO, "pO"), (pM[2], K_ps(2)), (pM[3], K_ps(3))]
                      for s in range(4):
                          cbk, cky = cbanks[s]
                          for cc in range(2):
                              for tp in range(31):
                                  mm(cbk[:, cc * 128:(cc + 1) * 128], dT[:, cc, 2 + s * 128 + tp: 2 + s * 128 + tp + 128],
                                     diag[:, cc, tp, :], tp == 0, tp == 30, ["dT", "diag"], [cky])
                          yield
                      for s in range(4):
                          cbk, cky = cbanks[s]
                          tt(t1[:, :], cbk[:, 0:256], lnB[:, 4, :], ALU.add, [cky, "lnB"], ["t1"])
                          stage(4.75)
                          layernorm_tok(t1[:, :], ["t1"], t1[:, :], ["t1"], 2, 3)
                          yield
                          stage(4.8)
                          act(yd[:, :], t1[:, :], AF.Silu, ["t1"], ["yd"])
                          stage(4.85)
                          transpose2([yd[:, 0:128], yd[:, 128:256]], ["yd"],
                                     yT[:, 6:8, s * 128:(s + 1) * 128], [("hy", 3)], evac="act")
                          yield
                          stage(4.9 + 0.01 * s)
                      if os.environ.get("KHALO", "1") == "1":
                          cp(dT[:, :, 0:32], dT[:, :, 512:544], ["dT"], ["dT"])
                      elif os.environ.get("KHALO") == "2":
                          for cc in range(2):
                              act(dT[:, cc, 0:32], dT[:, cc, 512:544], AF.Copy, ["dT"], ["dT"])
                      stage(5)
                      slot, W = load_chunk(l, 3)
                      for s in range(4):
                          pi = next_pm()
                          for kc in range(8):
                              mm(pM[pi][:, :], hT[:, kc, s * 128:(s + 1) * 128], W[:, kc, :], kc == 0, kc == 7,
                                 [("wr", slot)] + HT_ALL, [K_ps(pi)])
                          act(ub[:, :], pM[pi][:, 0:256], AF.Copy, [K_ps(pi)], ["ub"])
                          act(v32[:, :], pM[pi][:, 256:512], AF.Copy, [K_ps(pi)], ["v32"])
                          layernorm_tok(v32[:, :], ["v32"], vnb[:, :], ["vnb"], 0, 1)
                          yield
                          stage(5.3)
                          gbk, gky = (pD, "pD") if s % 2 == 0 else (pO, "pO")
                          for h in range(4):
                              mm(gbk[:, h * 64:(h + 1) * 64], wsT[:, h, :], vnb[:, h * 64:(h + 1) * 64], True, True,
                                 ["wsT", "vnb"], [gky])
                          for h in range(4):
                              stt(yb[:, h * 64:(h + 1) * 64], gbk[:, h * 64:(h + 1) * 64], bsT[:, h:h + 1],
                                  ub[:, h * 64:(h + 1) * 64], ALU.add, ALU.mult, [gky, "bsT", "ub"], ["yb"])
                          transpose2([yb[:, 0:128], yb[:, 128:256]], ["yb"],
                                     yT[:, 2:4, s * 128:(s + 1) * 128], [("hy", 1)], evac="act")
                          yield
                      stage(6)
                      slot, W = load_chunk(l, 4)
                      for s in range(4):
                          blk = t * 4 + s
                          pi = next_pm()
                          for kc in range(8):
                              mm(pM[pi][:, :], hT[:, kc, s * 128:(s + 1) * 128], W[:, kc, :], kc == 0, kc == 7,
                                 [("wr", slot)] + HT_ALL, [K_ps(pi)])
                          act(qk32[:, :], pM[pi][:, :], AF.Copy, [K_ps(pi)], ["qk32"])
                          rope(qkr[:, :], qk32[:, :], ["qk32"], 8, blk, ["qkr"])
                          yield
                          i_ = tcount[0] % 2
                          tcount[0] += 1
                          tr(pT[i_][:, 0, :], qkr[:, 0:128], ["qkr"], [("pT", i_)])
                          tr(pT[i_][:, 1, :], qkr[:, 128:256], ["qkr"], [("pT", i_)])
                          act(qbd[0:64, :, s, 0:128], pT[i_][0:64, 0:2, :], AF.Copy, [("pT", i_)], ["qbd"])
                          act(qbd[64:128, :, s, 128:256], pT[i_][64:128, 0:2, :], AF.Copy, [("pT", i_)], ["qbd"])
                          transpose2([qkr[:, 256:384], qkr[:, 384:512]], ["qkr"],
                                     kT[:, :, blk * 128:(blk + 1) * 128], ["kT"], evac="act")
                          yield
                      return

                  stage(7)
                  I4f = I4[:, 0:2, :].rearrange("p h q -> p (h q)")

                  def gen_index(s):
                      qb = t * 4 + s
                      nk = (qb + 1) * 128
                      for j in range(4):
                          ts(dsgn[:, j, :], identb[:, :], sgn[:, s, j:j + 1], None, ALU.mult, None,
                             ["identb", "sgn"], ["dsgn"])
                      nch = (nk + 511) // 512
                      for c in range(nch):
                          k0 = c * 512
                          n = min(512, nk - k0)
                          for half in range(2):
                              for jj in range(2):
                                  j = half * 2 + jj
                                  base = (j % 2) * 64
                                  mm(pM[2 + jj][:, 0:n], qiT_all[base:base + 64, j // 2, s * 128:(s + 1) * 128],
                                     kiT2[base:base + 64, k0:k0 + n], True, True, ["qiT_all", "kiT2"], [K_ps(2 + jj)])
                              for jj in range(2):
                                  j = half * 2 + jj
                                  act(rbuf[:, j, 0:n], pM[2 + jj][:, 0:n], AF.Relu, [K_ps(2 + jj)], [("rbuf", j)])
                          yield
                          for j in range(4):
                              mm(pD[:, 0:n], dsgn[:, j, :], rbuf[:, j, 0:n], j == 0, j == 3, ["dsgn", ("rbuf", j)], ["pD"])
                          cp(score_t[:, k0:k0 + n], pD[:, 0:n], ["pD"], ["score"])
                          yield

                  def emit_bisect(s):
                      qb = t * 4 + s
                      nk = (qb + 1) * 128
                      jko = jk[:, 0:1].to_broadcast([128, nk])
                      if qb >= 2:
                          S.op("dve", lambda e, nk=nk: e.reduce_max(out=amax[:, :], in_=score_t[:, 0:nk], axis=AX.X,
                                                                   apply_absolute_value=True), ["score"], ["amax"])
                      tt(score_t[:, nk - 128:nk], score_t[:, nk - 128:nk], cb[:, :], ALU.add, ["score", "cb"], ["score"])
                      if qb >= 2:
                          ts(Wt[:, :], pow2[:, :], amax[:, 0:1], None, ALU.mult, None, ["pow2", "amax"], ["Wt"])
                          S.op("dve", lambda e: e.memset(cntb[:, :], 0.0), [], ["cntb"])
                          stt(mid[1][:, :], amax[:, :], -1.0, Wt[:, 1:2], ALU.mult, ALU.add, ["amax", "Wt"], ["mid1"])
                          for k in range(1, NBIS + 1):
                              mc = mid[k % 2]
                              mn = mid[(k + 1) % 2]
                              ts(jko, score_t[:, 0:nk], mc[:, 0:1], 0.0, ALU.is_ge, ALU.add,
                                 ["score", "mid%d" % (k % 2)], ["jk", "cntb"], accum_out=cntb[:, k:k + 1])
                              stt(tq[:, :], cntb[:, k:k + 1], 255.5, Wt[:, k:k + 1], ALU.is_ge, ALU.mult,
                                  ["cntb", "Wt"], ["tq"])
                              if k < NBIS:
                                  stt(mn[:, :], tq[:, :], Wt[:, k + 1:k + 2], mc[:, :], ALU.subtract, ALU.add,
                                      ["tq", "Wt", "mid%d" % (k % 2)], ["mid%d" % ((k + 1) % 2)])
                              else:
                                  stt(thr[:, :], tq[:, :], Wt[:, k:k + 1], mc[:, :], ALU.subtract, ALU.add,
                                      ["tq", "Wt", "mid%d" % (k % 2)], ["thr"])
                      else:
                          ts(thr[:, :], pow2[:, 0:1], 0.0, -1e29, ALU.mult, ALU.add, ["pow2"], ["thr"])

                  def emit_bisect_act(s, gf):
                      qb = t * 4 + s
                      nk = (qb + 1) * 128
                      S.op("dve", lambda e, nk=nk: e.reduce_max(out=amax[:, :], in_=score_t[:, 0:nk], axis=AX.X,
                                                               apply_absolute_value=True), ["score"], ["amax"])
                      tt(score_t[:, nk - 128:nk], score_t[:, nk - 128:nk], cb[:, :], ALU.add, ["score", "cb"], ["score"])
                      ts(Wt[:, :], pow2[:, :], amax[:, 0:1], None, ALU.mult, None, ["pow2", "amax"], ["Wt"])
                      S.op("dve", lambda e: e.memset(cntb[:, :], 0.0), [], ["cntb"])
                      stt(mid[1][:, :], amax[:, :], -1.0, Wt[:, 1:2], ALU.mult, ALU.add, ["amax", "Wt"], ["mid1"])
                      for k in range(1, NBIS + 1):
                          mc = mid[k % 2]
                          mn = mid[(k + 1) % 2]
                          S.op("act", lambda e, k=k, mc=mc, nk=nk: e.activation(
                              out=mb[:, 0:nk], in_=score_t[:, 0:nk], func=AF.Sign, bias=mc[:, 0:1], scale=-1.0,
                              accum_out=cntb[:, k:k + 1]), ["score", "mid%d" % (k % 2)], ["mb", "cntb"])
                          run(gf, 2)
                          stt(tq[:, :], cntb[:, k:k + 1], float(nk) - 511.0, Wt[:, k:k + 1], ALU.is_le, ALU.mult,
                              ["cntb", "Wt"], ["tq"])
                          if k < NBIS:
                              stt(mn[:, :], tq[:, :], Wt[:, k + 1:k + 2], mc[:, :], ALU.subtract, ALU.add,
                                  ["tq", "Wt", "mid%d" % (k % 2)], ["mid%d" % ((k + 1) % 2)])
                          else:
                              stt(thr[:, :], tq[:, :], Wt[:, k:k + 1], mc[:, :], ALU.subtract, ALU.add,
                                  ["tq", "Wt", "mid%d" % (k % 2)], ["thr"])

                  def emit_mask(s):
                      nk = (t * 4 + s + 1) * 128
                      ts(mb[:, 0:nk], score_t[:, 0:nk], thr[:, 0:1], -1.0, ALU.is_ge, ALU.add,
                         ["score", "thr"], ["mb"])

                  def gen_attn(s):
                      qb = t * 4 + s

                      def st_mm(kb):
                          pi = kb % 2
                          for p in range(2):
                              mm(pM[pi][:, p * 256:(p + 1) * 256], kT[:, p, kb * 128:(kb + 1) * 128], qbd[:, p, s, :],
                                 True, False, ["kT", "qbd"], [K_ps(pi)])
                              mm(pM[pi][:, p * 256:(p + 1) * 256], mb[:, kb * 128:(kb + 1) * 128], I4f,
                                 False, True, ["mb", "I4"], [K_ps(pi)])
                      st_mm(0)
                      for kb in range(qb + 1):
                          pi = kb % 2
                          if kb + 1 <= qb:
                              st_mm(kb + 1)
                          pt = PT[kb % 2]
                          act(pt[:, :], pM[pi][:, :], AF.Exp, [K_ps(pi)], [("PT", kb % 2)], scale=0.125)
                          for h in range(4):
                              mm(pO[:, h * 65:(h + 1) * 65], pt[:, h * 128:(h + 1) * 128], V[:, kb, h * 65:(h + 1) * 65],
                                 (kb == 0 and h == 0), kb == qb, [("PT", kb % 2), "V"], ["pO"])
                          yield

                  def emit_final(s):
                      act(o32[:, :], pO[:, 0:260], AF.Copy, ["pO"], ["o32"])
                      pOv = o32[:, :].rearrange("p (h e) -> p h e", h=4)
                      S.op("dve", lambda e, pOv=pOv: e.reciprocal(out=rec[:, :], in_=pOv[:, :, 64]), ["o32"], ["rec"])
                      for h in range(4):
                          ts(yc[:, h * 64:(h + 1) * 64], o32[:, h * 65:h * 65 + 64], rec[:, h:h + 1], None,
                             ALU.mult, None, ["o32", "rec"], ["yc"])
                      transpose2([yc[:, 0:128], yc[:, 128:256]], ["yc"],
                                 yT[:, 4:6, s * 128:(s + 1) * 128], [("hy", 2)], evac="act")

                  def run(g, n=1):
                      for _ in range(n):
                          try:
                              next(g)
                          except StopIteration:
                              return False
                      return True

                  g0 = gen_index(0)
                  while run(g0):
                      pass
                  gf = gen_filler()
                  if t * 4 >= 2:
                      emit_bisect_act(0, gf)
                  else:
                      emit_bisect(0)
                  while run(gf):
                      pass
                  emit_mask(0)
                  for s in range(4):
                      ga = gen_attn(s)
                      if s < 3:
                          gi = gen_index(s + 1)
                          alive = True
                          while alive:
                              alive = run(gi)
                              run(ga)
                          emit_bisect(s + 1)
                      while run(ga):
                          pass
                      emit_final(s)
                      if s < 3:
                          emit_mask(s + 1)
                  if dbg and l == 0 and t == 0:
                      tap("yT", yT[:, :, :], [128, 8, 512], HY_ALL)
                  stage(8)
                  for cg in range(2):
                      slot, W = load_chunk(l, 6 + cg)
                      for s in range(4):
                          pi = next_pm()
                          for kc in range(8):
                              mm(pM[pi][:, :], yT[:, kc, s * 128:(s + 1) * 128], W[:, kc, :], kc == 0, kc == 7,
                                 [("wr", slot)] + HY_ALL, [K_ps(pi)])
                          tt(xt[:, s, cg * 512:(cg + 1) * 512], pM[pi][:, :], xt[:, s, cg * 512:(cg + 1) * 512], ALU.add,
                             [K_ps(pi), ("xt", s)], [("xt", s)])
                  if dbg and l == 0 and t == 0:
                      tap("xmid", xt[:, :, :], [128, 4, 1024], XT_ALL)
                  stage(9)
                  rmsnorm_to_hT(1)
                  AT_ALL = [("aT", i) for i in range(22)] + ["score", "mb"]
                  def gen_ffn(t):
                      yield from gen_gateup(t)
                      yield from gen_down(t)
                  gd = gen_ffn(t)
                  if t + 1 < NTILE:
                      gfr = gen_front(t + 1)
                      a1 = a2_ = True
                      while a1 or a2_:
                          if a1:
                              a1 = run(gd)
                          k_ = 0
                          while a2_ and k_ < 3:
                              try:
                                  r_ = next(gfr)
                              except StopIteration:
                                  a2_ = False
                                  break
                              k_ += 1
                              if r_ == "sync" and a1:
                                  break
                      load_xt(t + 1)
                  else:
                      while run(gd):
                          pass
        except _Stop:
            dma("pool", out[0:512, :].rearrange("(s p) d -> p s d", p=128), xt[:, :, :],
                [("xt", s) for s in range(4)], ["out"], "ost")
        S.final_wait("pool", ["out"] + ["dbg_" + k for k in dbg_outs])
        S.final_wait("sp", ["out"] + ["dbg_" + k for k in dbg_outs])
        S.emit()
    return nc


def make_consts(SEQ):
    NB = SEQ // 128
    f32 = np.float32
    ident = np.eye(128, dtype=f32)
    tril = np.tril(np.ones((128, 128), dtype=f32))
    cbm = np.where(np.arange(128)[None, :] <= np.arange(128)[:, None], 0.0, -1e30).astype(f32)
    inv_freq = (f32(10000.0) ** (-(np.arange(0, 64, 2, dtype=f32)) / f32(64))).astype(f32)
    ang = (np.arange(SEQ, dtype=f32)[:, None] * inv_freq[None, :]).astype(f32)
    cos = np.cos(ang).astype(f32).reshape(NB, 128, 32).transpose(1, 0, 2)
    sin = np.sin(ang).astype(f32).reshape(NB, 128, 32).transpose(1, 0, 2)
    pw = (2.002 * 2.0 ** (-np.arange(NBIS + 2, dtype=np.float64))).astype(f32)
    pow2 = np.broadcast_to(pw[None, :], (128, NBIS + 2))
    return {"c_ident": ident, "c_tril": tril, "c_cb": cbm, "c_cos": np.ascontiguousarray(cos),
            "c_sin": np.ascontiguousarray(sin), "c_pow2": np.ascontiguousarray(pow2)}


_NC_CACHE = {}


def kernel(**inputs):
    x = np.asarray(inputs["x"], dtype=np.float32)
    B, SEQ, _ = x.shape
    DEPTH = inputs["w_in"].shape[0]
    key = (SEQ, DEPTH)
    if key not in _NC_CACHE:
        _NC_CACHE[key] = build(SEQ, DEPTH)
    nc = _NC_CACHE[key]
    consts = make_consts(SEQ)
    params = {k: np.ascontiguousarray(np.asarray(v, dtype=np.float32)) for k, v in inputs.items() if k != "x"}
    n = 8
    in_maps = []
    for c in range(n):
        m = {"x": np.ascontiguousarray(x[c % B])}
        m.update(params)
        m.update(consts)
        in_maps.append(m)
    res = run_bass_kernel_spmd(nc, in_maps, core_ids=list(range(n)))
    outp = np.stack([np.asarray(res.results[b]["out"], dtype=np.float32) for b in range(B)], axis=0)
    return outp
```

```python
import numpy as np
from contextlib import ExitStack
import concourse.bass as bass
import concourse.mybir as mybir
from concourse.bass_utils import run_bass_kernel_spmd

F32 = mybir.dt.float32
BF16 = mybir.dt.bfloat16
ALU = mybir.AluOpType
AF = mybir.ActivationFunctionType
AX = mybir.AxisListType

D = 1024
G = 256
HID = 2816
INC = 2884
NBIS = 15
ENGS = ("pe", "act", "dve", "pool", "sp")


class Sched:
    def __init__(self, nc, stack):
        self.nc = nc
        self.stack = stack
        self.q = {e: [] for e in ENGS}
        self.sem = {e: stack.enter_context(nc.semaphore("sem_" + e)) for e in ENGS}
        self.cnt = {e: 0 for e in ENGS}
        self.waited = {e: {} for e in ENGS}
        self.last_write = {}
        self.readers = {}
        self.dsem = {}
        self.dcnt = {}

    def _dsem(self, key):
        if key not in self.dsem:
            self.dsem[key] = self.stack.enter_context(self.nc.semaphore("dsem_%d" % len(self.dsem)))
            self.dcnt[key] = 0
        return self.dsem[key]

    def op(self, eng, fn, reads=(), writes=(), dma=None):
        deps = []
        for r in reads:
            t = self.last_write.get(r)
            if t is not None:
                deps.append(t)
        for w in writes:
            t = self.last_write.get(w)
            if t is not None:
                deps.append(t)
            deps.extend(self.readers.get(w, ()))
        wd = self.waited[eng]
        m = {}
        for (sem, val, deng) in deps:
            if deng == "pe" and eng == "pe" and dma is None:
                continue
            k = id(sem)
            if wd.get(k, 0) >= val:
                continue
            if k not in m or m[k][1] < val:
                m[k] = (sem, val)
        for k, (sem, val) in m.items():
            wd[k] = val
        waits = list(m.values())
        if dma is not None:
            sem = self._dsem(dma)
            self.dcnt[dma] += 1
            tok = (sem, 16 * self.dcnt[dma], "dma")
            inc = 16
        else:
            self.cnt[eng] += 1
            tok = (self.sem[eng], self.cnt[eng], eng)
            inc = 1
        self.q[eng].append((waits, fn, tok[0], inc))
        for w in writes:
            self.last_write[w] = tok
            self.readers[w] = []
        for r in reads:
            if r in writes:
                continue
            lst = self.readers.setdefault(r, [])
            lst[:] = [x for x in lst if not (x[0] is tok[0])]
            lst.append(tok)
        return tok

    def final_wait(self, eng, keys):
        waits = []
        for k in keys:
            t = self.last_write.get(k)
            if t is not None:
                waits.append((t[0], t[1]))
        self.q[eng].append((waits, None, None, 0))

    def emit(self):
        nc = self.nc
        import os
        if os.environ.get("KDUMP"):
            names = {id(v): k for k, v in self.sem.items()}
            names.update({id(v): "d:" + str(k) for k, v in self.dsem.items()})
            for e in ENGS:
                print("ENG", e, "n=", len(self.q[e]), "cnt=", self.cnt[e])
                for (waits, fn, sem, inc) in self.q[e][-6:]:
                    print("   waits", [(names.get(id(s_), "?"), v) for (s_, v) in waits], "inc", names.get(id(sem)), inc)
        with nc.Block() as block:
            def run(eng_name):
                def body(e):
                    for (waits, fn, sem, inc) in self.q[eng_name]:
                        for (s, v) in waits:
                            e.wait_ge(s, v)
                        if fn is not None:
                            fn(e).then_inc(sem, inc)
                return body
            block.tensor(run("pe"))
            block.scalar(run("act"))
            block.vector(run("dve"))
            block.gpsimd(run("pool"))
            block.sync(run("sp"))


C_B, C_C, C_H, C_U, C_V, C_Q, C_K, C_VV, C_QI, C_KI, C_WI, C_DA, C_DG = (
    0, 256, 512, 768, 1024, 1280, 1536, 1792, 2048, 2304, 2368, 2372, 2628)
def chunk_table():
    t = []
    t.append((0, 1024, [("w_in", C_B, 512, 0)]))
    t.append((0, 1024, [("w_in", C_H, 256, 0), ("w_in", C_DA, 256, 256)]))
    t.append((0, 1024, [("w_in", C_DG, 256, 0), ("w_in", C_KI, 68, 256)]))
    t.append((0, 1024, [("w_in", C_U, 512, 0)]))
    t.append((0, 1024, [("w_in", C_Q, 512, 0)]))
    t.append((0, 1024, [("w_in", C_VV, 512, 0)]))
    t.append((0, 1024, [("w_out", 0, 512, 0)]))
    t.append((0, 1024, [("w_out", 512, 512, 0)]))
    for j in range(11):
        t.append((0, 1024, [("w_gate", j * 256, 256, 0), ("w_up", j * 256, 256, 256)]))
    for cg in range(2):
        for g in range(3):
            nr = 1024 if g < 2 else 768
            t.append((g * 1024, nr, [("w_down", cg * 512, 512, 0)]))
    return t


NCHUNK = 25


class _Stop(Exception):
    pass


def build(SEQ, DEPTH, dbg=False, stop=None):
    nc = bass.Bass("TRN2", target_bir_lowering=False)

    def stage(n):
        if stop is not None and n > stop:
            raise _Stop()
    NTILE = SEQ // 512
    NB = SEQ // 128

    def din(name, shape):
        return nc.dram_tensor(name, list(shape), F32, kind="ExternalInput").ap()

    x_in = din("x", [SEQ, D])
    P = {}
    for name, shape in [("g_mix", [DEPTH, D]), ("w_in", [DEPTH, D, INC]), ("w_conv_a", [DEPTH, 3, G]),
                        ("gmlp_ln_g", [DEPTH, G]), ("gmlp_ln_b", [DEPTH, G]), ("w_s", [DEPTH, 4, 128, 128]),
                        ("b_s", [DEPTH, 4, 128]), ("w_conf", [DEPTH, 31, G]), ("b_conf", [DEPTH, G]),
                        ("conf_ln_g", [DEPTH, G]), ("conf_ln_b", [DEPTH, G]), ("w_out", [DEPTH, D, D]),
                        ("g_ffn", [DEPTH, D]), ("w_gate", [DEPTH, D, HID]), ("w_up", [DEPTH, D, HID]),
                        ("w_down", [DEPTH, HID, D]), ("g_final", [D])]:
        P[name] = din(name, shape)
    c_ident = din("c_ident", [128, 128])
    c_tril = din("c_tril", [128, 128])
    c_cb = din("c_cb", [128, 128])
    c_cos = din("c_cos", [128, NB, 32])
    c_sin = din("c_sin", [128, NB, 32])
    c_pow2 = din("c_pow2", [128, NBIS + 2])
    out = nc.dram_tensor("out", [SEQ, D], F32, kind="ExternalOutput").ap()
    wsc = nc.dram_tensor("wsc", [DEPTH, NCHUNK, 1024, 512], BF16, kind="Internal").ap()
    xs = nc.dram_tensor("xs", [SEQ, D], F32, kind="Internal").ap()
    dbg_outs = {}

    with ExitStack() as st:
        S = Sched(nc, st)

        def sb(name, shape, dt):
            return st.enter_context(nc.sbuf_tensor(name, list(shape), dt))

        def psum(name, shape, dt):
            return st.enter_context(nc.psum_tensor(name, list(shape), dt))

        wring_all = sb("wring", [128, 3, 8, 512], BF16)
        wring = [wring_all[:, i, :, :] for i in range(3)]
        xt = sb("xt", [128, 4, 1024], F32)
        hy = sb("hy", [128, 4096], BF16)
        hn = hy[:, :].rearrange("p (s d) -> p s d", s=4)
        yT = hy[:, :].rearrange("p (k t) -> p k t", k=8)
        hT = sb("hT", [128, 8, 512], BF16)
        hTb = sb("hTb", [128, 8, 512], BF16)
        bT = sb("bT", [128, 2, 512], BF16)
        cT = sb("cT", [128, 2, 512], BF16)
        ch = sb("ch", [128, 2, 514], BF16)
        accA = sb("accA", [128, 512], F32)
        daT = sb("daT", [128, 2, 512], BF16)
        sig = sb("sig", [128, 512], BF16)
        dT = sb("dT", [128, 2, 544], BF16)
        kraw = sb("kraw", [128, 4, 64], F32)
        wraw = sb("wraw", [128, 4, 4], F32)
        wabs = sb("wabs", [128, 4, 4], F32)
        sgn = sb("sgn", [128, 4, 4], F32)
        ub = sb("ub", [128, 256], BF16)
        vn32 = sb("vn32", [128, 256], F32)
        vnb = sb("vnb", [128, 256], BF16)
        v32 = sb("v32", [128, 256], F32)
        qk32 = sb("qk32", [128, 512], F32)
        o32 = sb("o32", [128, 260], F32)
        wsf = qk32[:, :].rearrange("p (h s) -> p h s", h=4)
        yb = sb("yb", [128, 256], BF16)
        ropeA = sb("ropeA", [128, 512], F32)
        ropeB = sb("ropeB", [128, 512], F32)
        junk = ropeA[:, :].bitcast(BF16)
        qkr = sb("qkr", [128, 512], BF16)
        qbd = sb("qbd", [128, 2, 4, 256], BF16)
        jk = sb("jk", [128, 4], BF16)
        qiT_all = sb("qiT_all", [128, 2, 512], BF16)
        qib = sb("qib", [128, 256], BF16)
        kib = sb("kib", [128, 128], BF16)
        dsgn = sb("dsgn", [128, 4, 128], BF16)
        kT = sb("kT", [128, 2, SEQ], BF16)
        kiT2 = sb("kiT2", [128, SEQ], BF16)
        V = sb("V", [128, NB, 260], BF16)
        big = sb("big", [128, 12288], BF16)
        rbuf = sb("rbuf", [128, 4, 512], BF16)
        PT = [sb("PT%d" % i, [128, 512], BF16) for i in range(2)]
        yc = sb("yc", [128, 256], BF16)
        rec = sb("rec", [128, 4], F32)
        t1 = sb("t1", [128, 256], F32)
        yd = sb("yd", [128, 256], BF16)
        cosT = sb("cosT", [128, 4, 32], F32)
        sinT = sb("sinT", [128, 4, 32], F32)
        sg = [sb("sg%d" % i, [128, 512], BF16) for i in range(2)]
        diag = sb("diag", [128, 2, 31, 128], BF16)
        wsT = sb("wsT", [128, 4, 128], BF16)
        wsm = sb("wsm", [128, 4, 128], BF16)
        identf = sb("identf", [128, 128], F32)
        identb = sb("identb", [128, 128], BF16)
        I4 = sb("I4", [128, 4, 128], BF16)
        tril = sb("tril", [128, 128], F32)
        cb = sb("cb", [128, 128], F32)
        pow2 = sb("pow2", [128, NBIS + 2], F32)
        lnB = sb("lnB", [128, 5, 256], F32)
        gfinB = sb("gfinB", [128, 1024], F32)
        gT = sb("gT", [128, 2, 8], F32)
        wcaT = sb("wcaT", [128, 2, 3], F32)
        wcfT = sb("wcfT", [128, 2, 31], F32)
        bsT = sb("bsT", [128, 4], F32)
        ssq = sb("ssq", [128, 4], F32)
        rstd = sb("rstd", [128, 4], F32)
        st6 = sb("st6", [128, 6], F32)
        mv = sb("mv", [128, 2], F32)
        lrs = sb("lrs", [128, 1], F32)
        amax = sb("amax", [128, 1], F32)
        Wt = sb("Wt", [128, NBIS + 2], F32)
        cntb = sb("cntb", [128, NBIS + 2], F32)
        mid = [sb("mid%d" % i, [128, 1], F32) for i in range(2)]
        tq = sb("tq", [128, 1], F32)
        thr = sb("thr", [128, 1], F32)
        score_t = big[:, 0:2 * SEQ].bitcast(F32)
        mb = big[:, 8192:8192 + SEQ]
        scoreB = wring_all[:, 0:2, :, :].rearrange("p a k n -> p (a k n)").bitcast(F32)[:, 0:SEQ]
        SC = [(score_t, ["score"]), (scoreB, [("wr", 0), ("wr", 1)])]
        qi32 = sb("qi32", [128, 256], F32)
        ki32 = sb("ki32", [128, 64], F32)
        aT = big[:, :].rearrange("p (k t) -> p k t", k=24)

        pT = [psum("pT%d" % i, [128, 8, 128], BF16) for i in range(2)]
        pM = [psum("pM%d" % i, [128, 512], F32) for i in range(4)]
        pO = psum("pO", [128, 512], F32)
        pD = psum("pD", [128, 512], F32)

        def K_ps(i):
            return ("pM", i)

        def act(out_, in_, func, r, w, **kw):
            S.op("act", lambda e: e.activation(out=out_, in_=in_, func=func, **kw), r, w)

        def tt(out_, in0, in1, op, r, w, eng="dve"):
            S.op(eng, lambda e: e.tensor_tensor(out=out_, in0=in0, in1=in1, op=op), r, w)

        def ts(out_, in0, s1, s2, op0, op1, r, w, eng="dve", **kw):
            if op1 is None:
                S.op(eng, lambda e: e.tensor_scalar(out=out_, in0=in0, scalar1=s1, scalar2=None, op0=op0, **kw), r, w)
            else:
                S.op(eng, lambda e: e.tensor_scalar(out=out_, in0=in0, scalar1=s1, scalar2=s2, op0=op0, op1=op1, **kw), r, w)

        def stt(out_, in0, scalar, in1, op0, op1, r, w, eng="dve"):
            S.op(eng, lambda e: e.scalar_tensor_tensor(out=out_, in0=in0, scalar=scalar, in1=in1, op0=op0, op1=op1), r, w)

        def cp(out_, in_, r, w, eng="dve"):
            S.op(eng, lambda e: e.tensor_copy(out=out_, in_=in_), r, w)

        def mm(out_, lhsT, rhs, start, stop, r, w):
            S.op("pe", lambda e: e.matmul(out_, lhsT=lhsT, rhs=rhs, start=start, stop=stop, skip_group_check=True), r, w)

        def tr(out_, in_, r, w):
            S.op("pe", lambda e: e.transpose(out=out_, in_=in_, identity=identb[:, :]), list(r) + ["identb"], w)

        def dma(eng, out_, in_, r, w, key, slow=False):
            if slow:
                S.op(eng, lambda e: e.dma_start(out=out_, in_=in_, allow_slow_non_contiguous=True), r, w, dma=key)
            else:
                S.op(eng, lambda e: e.dma_start(out=out_, in_=in_), r, w, dma=key)

        ntap = [0]

        def tap(name, ap, shape, rkeys):
            if not dbg:
                return
            d = nc.dram_tensor("dbg_" + name, list(shape), ap.dtype, kind="ExternalOutput").ap()
            dbg_outs[name] = d
            ntap[0] += 1
            dma("sp", d, ap, rkeys, ["dbg_" + name], ("dbg", ntap[0]))

        table = chunk_table()
        import os
        NCAST = int(os.environ.get("KNCAST", "1000"))
        for l in range(DEPTH):
            for ci, (r0, nr, cols) in enumerate(table):
                if ci >= NCAST:
                    continue
                for (src, c0, n, d0) in cols:
                    dma("pool", wsc[l, ci, 0:nr, d0:d0 + n], P[src][l, r0:r0 + nr, c0:c0 + n],
                        [], [("wsc", l, ci)], ("cast", l, ci))

        dma("sp", identf[:, :], c_ident, [], ["identf"], "c0")
        dma("sp", tril[:, :], c_tril, [], ["tril"], "c1")
        dma("sp", cb[:, :], c_cb, [], ["cb"], "c2")
        dma("sp", pow2[:, :], c_pow2, [], ["pow2"], "c5")
        dma("sp", gfinB[:, :], P["g_final"].partition_broadcast(128), [], ["gfinB"], "c6")
        cp(identb[:, :], identf[:, :], ["identf"], ["identb"])
        for h in range(4):
            ts(I4[:, h, :], identf[:, :], 30000.0, None, ALU.mult, None, ["identf"], ["I4"])
        Vv = V[:, :, :].rearrange("p b (h e) -> p b h e", h=4)
        S.op("dve", lambda e: e.memset(V[:, :, :], 1.0), [], ["V"])
        S.op("dve", lambda e: e.memset(qbd[:, :, :, :], 0.0), [], ["qbd"])

        ring_ctr = [0]

        grp_ctr = {"ffn": 0}

        def load_chunk(l, ci, nkc=8, group=None):
            if group == "ffn":
                slot = grp_ctr["ffn"] % 2
                grp_ctr["ffn"] += 1
            elif group == "front":
                slot = 2
            else:
                slot = ring_ctr[0] % 3
                ring_ctr[0] += 1
            ncol = max(d0 + n for (src_, c0, n, d0) in table[ci][2])
            src = wsc[l, ci].rearrange("(k p) n -> p k n", p=128)
            dma("sp", wring[slot][:, 0:nkc, 0:ncol], src[:, 0:nkc, 0:ncol], [("wsc", l, ci)], [("wr", slot)], ("wr", slot))
            return slot, wring[slot]

        tcount = [0]

        def transpose2(src_aps, src_keys, dst_ap, dst_keys, evac="dve"):
            i = tcount[0] % 2
            tcount[0] += 1
            n = len(src_aps)
            for j, a in enumerate(src_aps):
                tr(pT[i][:, j, :], a, src_keys, [("pT", i)])
            srcp = pT[i][:, 0:n, :] if n > 1 else pT[i][:, 0, :]
            if evac == "act":
                act(dst_ap, srcp, AF.Copy, [("pT", i)], dst_keys)
            else:
                cp(dst_ap, srcp, [("pT", i)], dst_keys)

        def rmsnorm_to_hT(gidx):
            S.op("dve", lambda e: e.memset(ssq[:, :], 0.0), [], [("ssq", i) for i in range(4)])
            for s in range(4):
                act(junk[:, :], xt[:, s, :], AF.Square, [("xt", s)], ["ropeA", ("ssq", s)], accum_out=ssq[:, s:s + 1])
            for s in range(4):
                ts(rstd[:, s:s + 1], ssq[:, s:s + 1], 1.0 / D, 1e-6, ALU.mult, ALU.add, [("ssq", s)], [("rstd", s)])
                act(rstd[:, s:s + 1], rstd[:, s:s + 1], AF.Sqrt, [("rstd", s)], [("rstd", s)])
                S.op("dve", lambda e, s=s: e.reciprocal(out=rstd[:, s:s + 1], in_=rstd[:, s:s + 1]),
                     [("rstd", s)], [("rstd", s)])
                act(hn[:, s, :], xt[:, s, :], AF.Copy, [("xt", s), ("rstd", s)], [("hy", s)], scale=rstd[:, s:s + 1])
            for kc in range(8):
                i = tcount[0] % 2
                tcount[0] += 1
                for s in range(4):
                    tr(pT[i][:, s, :], hn[:, s, kc * 128:(kc + 1) * 128], [("hy", s)], [("pT", i)])
                src = pT[i][:, 0:4, :].rearrange("p s t -> p (s t)")
                if kc % 2 == 0:
                    ts(hTb[:, kc, :], src, gT[:, gidx, kc:kc + 1], None, ALU.mult, None, [("pT", i), "gT"], [("hTb", kc)])
                else:
                    act(hTb[:, kc, :], src, AF.Copy, [("pT", i), "gT"], [("hTb", kc)], scale=gT[:, gidx, kc:kc + 1])

        def run(g, n=1):
            for _ in range(n):
                try:
                    next(g)
                except StopIteration:
                    return False
            return True

        pfc = [0]

        def nb():
            i = pfc[0] % 2
            pfc[0] += 1
            return (pD, "pD") if i == 0 else (pO, "pO")

        ssq2 = sb("ssq2", [128, 4], F32)
        rstd2 = sb("rstd2", [128, 4], F32)
        xstg = rbuf[:, :, :].rearrange("p a b -> p (a b)").bitcast(F32)
        RB_ALL = [("rbuf", j) for j in range(4)]
        XT_ALL = [("xt", s_) for s_ in range(4)]
        HT_ALL = [("hT", k) for k in range(8)]
        HTB_ALL = [("hTb", k) for k in range(8)]
        HY_ALL = [("hy", k) for k in range(4)]
        pmc = [0]

        def next_pm():
            i = pmc[0] % 4
            pmc[0] += 1
            return i

        def rope(dst, src_ps, src_keys, ngrp, blk, dst_keys):
            n = ngrp * 64
            if ngrp == 1:
                rk = list(src_keys) + ["cosT", "sinT"]
                cs = cosT[:, blk % 4, :]
                sn = sinT[:, blk % 4, :]
                tt(ropeA[:, 0:32], src_ps[:, 0:32], cs, ALU.mult, rk, ["ropeA"])
                tt(ropeA[:, 32:64], src_ps[:, 32:64], cs, ALU.mult, rk, ["ropeA"])
                tt(ropeB[:, 0:32], src_ps[:, 32:64], sn, ALU.mult, rk, ["ropeB"])
                tt(ropeB[:, 32:64], src_ps[:, 0:32], sn, ALU.mult, rk, ["ropeB"])
                tt(dst[:, 0:32], ropeA[:, 0:32], ropeB[:, 0:32], ALU.subtract, ["ropeA", "ropeB"], dst_keys)
                tt(dst[:, 32:64], ropeA[:, 32:64], ropeB[:, 32:64], ALU.add, ["ropeA", "ropeB"], dst_keys)
                return
            v = src_ps.rearrange("p (g h d) -> p g h d", g=ngrp, h=2)
            A = ropeA[:, 0:n].rearrange("p (g h d) -> p g h d", g=ngrp, h=2)
            B = ropeB[:, 0:n].rearrange("p (g h d) -> p g h d", g=ngrp, h=2)
            cs = cosT[:, blk % 4:blk % 4 + 1, :].to_broadcast([128, ngrp, 32])
            sn = sinT[:, blk % 4:blk % 4 + 1, :].to_broadcast([128, ngrp, 32])
            rk = list(src_keys) + ["cosT", "sinT"]
            tt(A[:, :, 0, :], v[:, :, 0, :], cs, ALU.mult, rk, ["ropeA"])
            tt(A[:, :, 1, :], v[:, :, 1, :], cs, ALU.mult, rk, ["ropeA"])
            tt(B[:, :, 0, :], v[:, :, 1, :], sn, ALU.mult, rk, ["ropeB"])
            tt(B[:, :, 1, :], v[:, :, 0, :], sn, ALU.mult, rk, ["ropeB"])
            dv = dst.rearrange("p (g h d) -> p g h d", g=ngrp, h=2)
            tt(dv[:, :, 0, :], A[:, :, 0, :], B[:, :, 0, :], ALU.subtract, ["ropeA", "ropeB"], dst_keys)
            tt(dv[:, :, 1, :], A[:, :, 1, :], B[:, :, 1, :], ALU.add, ["ropeA", "ropeB"], dst_keys)

        def layernorm_tok(src, src_keys, dst, dst_keys, gi, bi):
            S.op("dve", lambda e: e.bn_stats(out=st6[:, :], in_=src), src_keys, ["st6"])
            S.op("dve", lambda e: e.bn_aggr(out=mv[:, :], in_=st6[:, :]), ["st6"], ["mv"])
            ts(lrs[:, :], mv[:, 1:2], 1e-5, None, ALU.add, None, ["mv"], ["lrs"])
            act(lrs[:, :], lrs[:, :], AF.Sqrt, ["lrs"], ["lrs"])
            S.op("dve", lambda e: e.reciprocal(out=lrs[:, :], in_=lrs[:, :]), ["lrs"], ["lrs"])
            ts(vn32[:, :], src, mv[:, 0:1], lrs[:, 0:1], ALU.subtract, ALU.mult, list(src_keys) + ["mv", "lrs"], ["vn32"])
            tt(vn32[:, :], vn32[:, :], lnB[:, gi, :], ALU.mult, ["vn32", "lnB"], ["vn32"])
            tt(dst, vn32[:, :], lnB[:, bi, :], ALU.add, ["vn32", "lnB"], dst_keys)

        try:
          stage(1)
          for l in range(DEPTH):
              x_src = x_in if l == 0 else xs
              last = (l == DEPTH - 1)
              pk = ("lp", l)
              dma("sp", gT[:, 0, :], P["g_mix"][l].rearrange("(k p) -> p k", p=128), [], ["gT"], "lp0", slow=True)
              dma("sp", gT[:, 1, :], P["g_ffn"][l].rearrange("(k p) -> p k", p=128), [], ["gT"], "lp0", slow=True)
              for cc in range(2):
                  dma("sp", wcaT[:, cc, :], P["w_conv_a"][l][:, cc * 128:(cc + 1) * 128].rearrange("t p -> p t"),
                      [], ["wcaT"], "lp1", slow=True)
                  dma("sp", wcfT[:, cc, :], P["w_conf"][l][:, cc * 128:(cc + 1) * 128].rearrange("t p -> p t"),
                      [], ["wcfT"], "lp2", slow=True)
              dma("sp", bsT[:, :], P["b_s"][l].rearrange("h t -> t h"), [], ["bsT"], "lp3", slow=True)
              for i, nm in enumerate(["gmlp_ln_g", "gmlp_ln_b", "conf_ln_g", "conf_ln_b", "b_conf"]):
                  dma("sp", lnB[:, i, :], P[nm][l].partition_broadcast(128), [], ["lnB"], "lp4")
              dma("sp", wsf[:, :, :], P["w_s"][l].rearrange("h t s -> t h s"), [], ["qk32"], "lp5")
              for h in range(4):
                  tt(wsm[:, h, :], wsf[:, h, :], tril[:, :], ALU.mult, ["qk32", "tril"], ["wsm"])
              transpose2([wsm[:, h, :] for h in range(4)], ["wsm"], wsT[:, :, :], ["wsT"])
              for cc in range(2):
                  for tp in range(31):
                      ts(diag[:, cc, tp, :], identb[:, :], wcfT[:, cc, tp:tp + 1], None, ALU.mult, None,
                         ["identb", "wcfT"], ["diag"])
              stage(2)
              S.op("dve", lambda e: e.memset(ch[:, :, 0:2], 0.0), [], ["ch"])
              S.op("dve", lambda e: e.memset(dT[:, :, 0:32], 0.0), [], ["dT"])

              def gen_front(t):
                  dma("pool", cosT[:, :, :], c_cos[:, t * 4:(t + 1) * 4, :], [], ["cosT"], "c3")
                  dma("pool", sinT[:, :, :], c_sin[:, t * 4:(t + 1) * 4, :], [], ["sinT"], "c4")
                  S.op("dve", lambda e: e.memset(ssq2[:, :], 0.0), [], [("ssq2", i) for i in range(4)])
                  for s in range(4):
                      r0 = (t * 4 + s) * 128
                      dma("pool", xstg, x_src[r0:r0 + 128, :], ["xs"] if l > 0 else [], RB_ALL, "xstg")
                      act(junk[:, :], xstg, AF.Square, RB_ALL, ["ropeA", ("ssq2", s)], accum_out=ssq2[:, s:s + 1])
                      ts(rstd2[:, s:s + 1], ssq2[:, s:s + 1], 1.0 / D, 1e-6, ALU.mult, ALU.add, [("ssq2", s)], [("rstd2", s)])
                      act(rstd2[:, s:s + 1], rstd2[:, s:s + 1], AF.Sqrt, [("rstd2", s)], [("rstd2", s)])
                      S.op("dve", lambda e, s=s: e.reciprocal(out=rstd2[:, s:s + 1], in_=rstd2[:, s:s + 1]),
                           [("rstd2", s)], [("rstd2", s)])
                      act(hn[:, s, :], xstg, AF.Copy, RB_ALL + [("rstd2", s)], [("hy", s)], scale=rstd2[:, s:s + 1])
                      yield
                  for kc in range(8):
                      yield "sync"
                      i = tcount[0] % 2
                      tcount[0] += 1
                      for s in range(4):
                          tr(pT[i][:, s, :], hn[:, s, kc * 128:(kc + 1) * 128], [("hy", s)], [("pT", i)])
                      src = pT[i][:, 0:4, :].rearrange("p s t -> p (s t)")
                      if kc % 2 == 0:
                          ts(hT[:, kc, :], src, gT[:, 0, kc:kc + 1], None, ALU.mult, None, [("pT", i), "gT"], [("hT", kc)])
                      else:
                          act(hT[:, kc, :], src, AF.Copy, [("pT", i), "gT"], [("hT", kc)], scale=gT[:, 0, kc:kc + 1])
                      yield
                  slot, W = load_chunk(l, 0, group="front")
                  for m in range(4):
                      pb, pk = nb()
                      for kc in range(8):
                          mm(pb[:, :], W[:, kc, m * 128:(m + 1) * 128], hT[:, kc, :], kc == 0, kc == 7,
                             [("wr", slot)] + HT_ALL, [pk])
                      dst = bT if m < 2 else cT
                      act(dst[:, m % 2, :], pb[:, :], AF.Copy, [pk], ["bT" if m < 2 else "cT"])
                      yield
                  slot, W = load_chunk(l, 1, group="front")
                  for m in range(4):
                      pb, pk = nb()
                      for kc in range(8):
                          mm(pb[:, :], W[:, kc, m * 128:(m + 1) * 128], hT[:, kc, :], kc == 0, kc == 7,
                             [("wr", slot)] + HT_ALL, [pk])
                      if m < 2:
                          tt(ch[:, m, 2:514], pb[:, :], cT[:, m, :], ALU.mult, [pk, "cT"], ["ch"])
                      else:
                          act(daT[:, m - 2, :], pb[:, :], AF.Copy, [pk], ["daT"])
                      yield
                  for cc in range(2):
                      ts(accA[:, :], ch[:, cc, 2:514], wcaT[:, cc, 2:3], None, ALU.mult, None, ["ch", "wcaT"], ["accA"])
                      stt(accA[:, :], ch[:, cc, 1:513], wcaT[:, cc, 1:2], accA[:, :], ALU.mult, ALU.add,
                          ["ch", "wcaT", "accA"], ["accA"])
                      stt(accA[:, :], ch[:, cc, 0:512], wcaT[:, cc, 0:1], accA[:, :], ALU.mult, ALU.add,
                          ["ch", "wcaT", "accA"], ["accA"])
                      tt(yT[:, cc, :], accA[:, :], bT[:, cc, :], ALU.mult, ["accA", "bT"], [("hy", 0)])
                      yield
                  cp(ch[:, :, 0:2], ch[:, :, 512:514], ["ch"], ["ch"])
                  stage(4)
                  slot, W = load_chunk(l, 2, group="front")
                  for m in range(2):
                      pb, pk = nb()
                      for kc in range(8):
                          mm(pb[:, :], W[:, kc, m * 128:(m + 1) * 128], hT[:, kc, :], kc == 0, kc == 7,
                             [("wr", slot)] + HT_ALL, [pk])
                      act(sig[:, :], pb[:, :], AF.Sigmoid, [pk], ["sig"])
                      tt(dT[:, m, 32:544], daT[:, m, :], sig[:, :], ALU.mult, ["daT", "sig"], ["dT"])
                      yield
                  for s in range(4):
                      pb, pk = nb()
                      for kc in range(8):
                          mm(pb[:, 0:68], hT[:, kc, s * 128:(s + 1) * 128], W[:, kc, 256:324], kc == 0, kc == 7,
                             [("wr", slot)] + HT_ALL, [pk])
                      cp(kraw[:, s, :], pb[:, 0:64], [pk], ["kraw"])
                      cp(wraw[:, s, :], pb[:, 64:68], [pk], ["wraw"])
                      yield
                  stage(4.3)
                  ts(sgn[:, :, :], wraw[:, :, :], 0.0, 2.0, ALU.is_ge, ALU.mult, ["wraw"], ["sgn"])
                  ts(sgn[:, :, :], sgn[:, :, :], -1.0, None, ALU.add, None, ["sgn"], ["sgn"])
                  stt(wabs[:, :, :], wraw[:, :, :], 1.0 / 16.0, sgn[:, :, :], ALU.mult, ALU.mult, ["wraw", "sgn"], ["wabs"])
                  slot, W = load_chunk(l, 5, group="front")
                  for s in range(4):
                      blk = t * 4 + s
                      pb, pk = nb()
                      for kc in range(8):
                          mm(pb[:, :], hT[:, kc, s * 128:(s + 1) * 128], W[:, kc, :], kc == 0, kc == 7,
                             [("wr", slot)] + HT_ALL, [pk])
                      for h in range(4):
                          act(V[:, blk, h * 65:h * 65 + 64], pb[:, h * 64:(h + 1) * 64], AF.Copy,
                              [pk], ["V"])
                      stage(6.5)
                      yield
                      act(qk32[:, 0:256], pb[:, 256:512], AF.Copy, [pk], ["qk32"])
                      rope(qi32[:, :], qk32[:, 0:256], ["qk32"], 4, blk, ["qi32"])
                      for h in range(4):
                          ts(qib[:, h * 64:(h + 1) * 64], qi32[:, h * 64:(h + 1) * 64], wabs[:, s, h:h + 1], None,
                             ALU.mult, None, ["qi32", "wabs"], ["qib"])
                      stage(6.7)
                      yield
                      yield "sync"
                      transpose2([qib[:, 0:128], qib[:, 128:256]], ["qib"],
                                 qiT_all[:, :, s * 128:(s + 1) * 128], ["qiT_all"], evac="act")
                      stage(6.8)
                      rope(ki32[:, :], kraw[:, s, :], ["kraw"], 1, blk, ["ki32"])
                      stage(6.9)
                      yield
                      cp(kib[:, 0:64], ki32[:, :], ["ki32"], ["kib"])
                      cp(kib[:, 64:128], ki32[:, :], ["ki32"], ["kib"])
                      stage(6.91 + 0.02 * s)
                      yield "sync"
                      transpose2([kib[:, :]], ["kib"], kiT2[:, blk * 128:(blk + 1) * 128], ["kiT2"], evac="act")
                      stage(6.92 + 0.02 * s)
                      yield

              def gen_gateup(t):
                  for j in range(11):
                      slw, Wgu = load_chunk(l, 8 + j, group="ffn")
                      for m in range(2):
                          pg = next_pm()
                          pu = next_pm()
                          for kc in range(8):
                              mm(pM[pg][:, :], Wgu[:, kc, m * 128:(m + 1) * 128], hTb[:, kc, :], kc == 0, kc == 7,
                                 [("wr", slw)] + HTB_ALL, [K_ps(pg)])
                          for kc in range(8):
                              mm(pM[pu][:, :], Wgu[:, kc, 256 + m * 128:256 + (m + 1) * 128], hTb[:, kc, :], kc == 0, kc == 7,
                                 [("wr", slw)] + HTB_ALL, [K_ps(pu)])
                          ai = j * 2 + m
                          sgi = ai % 2
                          act(sg[sgi][:, :], pM[pg][:, :], AF.Silu, [K_ps(pg)], [("sg", sgi)])
                          tt(aT[:, ai, :], pM[pu][:, :], sg[sgi][:, :], ALU.mult, [K_ps(pu), ("sg", sgi)],
                             [("aT", ai), "score" if ai < 16 else "mb"])
                          yield

              def gen_down(t):
                  AT_ALL = [("aT", i) for i in range(22)] + ["score", "mb"]
                  for cg in range(2):
                      pis = [next_pm() for _ in range(4)]
                      for g in range(3):
                          nkc = 8 if g < 2 else 6
                          slot, W = load_chunk(l, 19 + cg * 3 + g, nkc, group="ffn")
                          for s in range(4):
                              for kc in range(nkc):
                                  kk = g * 8 + kc
                                  mm(pM[pis[s]][:, :], aT[:, kk, s * 128:(s + 1) * 128], W[:, kc, :],
                                     kk == 0, kk == 21, [("wr", slot)] + AT_ALL, [K_ps(pis[s])])
                          yield
                      for s in range(4):
                          tt(xt[:, s, cg * 512:(cg + 1) * 512], pM[pis[s]][:, :], xt[:, s, cg * 512:(cg + 1) * 512],
                             ALU.add, [K_ps(pis[s]), ("xt", s)], [("xt", s)])
                      yield
                  stage(10)
                  if last:
                      S.op("dve", lambda e: e.memset(ssq[:, :], 0.0), [], [("ssq", i) for i in range(4)])
                      for s in range(4):
                          act(junk[:, :], xt[:, s, :], AF.Square, [("xt", s)], ["ropeA", ("ssq", s)],
                              accum_out=ssq[:, s:s + 1])
                      for s in range(4):
                          ts(rstd[:, s:s + 1], ssq[:, s:s + 1], 1.0 / D, 1e-6, ALU.mult, ALU.add, [("ssq", s)], [("rstd", s)])
                          act(rstd[:, s:s + 1], rstd[:, s:s + 1], AF.Sqrt, [("rstd", s)], [("rstd", s)])
                          S.op("dve", lambda e, s=s: e.reciprocal(out=rstd[:, s:s + 1], in_=rstd[:, s:s + 1]),
                               [("rstd", s)], [("rstd", s)])
                          stt(xt[:, s, :], xt[:, s, :], rstd[:, s:s + 1], gfinB[:, :], ALU.mult, ALU.mult,
                              [("xt", s), ("rstd", s), "gfinB"], [("xt", s)])
                      dma("pool", out[t * 512:(t + 1) * 512, :].rearrange("(s p) d -> p s d", p=128), xt[:, :, :],
                          XT_ALL, ["out"], "ost")
                  else:
                      dma("pool", xs[t * 512:(t + 1) * 512, :].rearrange("(s p) d -> p s d", p=128), xt[:, :, :],
                          XT_ALL, ["xs"], "ost")

              def load_xt(t):
                  dma("pool", xt[:, :, :], x_src[t * 512:(t + 1) * 512, :].rearrange("(s p) d -> p s d", p=128),
                      ["xs"] if l > 0 else [], XT_ALL, "xt")

              gfr0 = gen_front(0)
              while run(gfr0):
                  pass
              load_xt(0)
              for t in range(NTILE):
                  def gen_filler():
                      stage(4.5)
                      cbanks = [(pD, "pD"), (pO, "pO"), (pM[2], K_ps(2)), (pM[3], K_ps(3))]
                      for s in range(4):
                          cbk, cky = cbanks[s]
                          for cc in range(2):
                              for tp in range(31):
                                  mm(cbk[:, cc * 128:(cc + 1) * 128], dT[:, cc, 2 + s * 128 + tp: 2 + s * 128 + tp + 128],
                                     diag[:, cc, tp, :], tp == 0, tp == 30, ["dT", "diag"], [cky])
                          yield
                      for s in range(4):
                          cbk, cky = cbanks[s]
                          tt(t1[:, :], cbk[:, 0:256], lnB[:, 4, :], ALU.add, [cky, "lnB"], ["t1"])
                          stage(4.75)
                          layernorm_tok(t1[:, :], ["t1"], t1[:, :], ["t1"], 2, 3)
                          yield
                          stage(4.8)
                          act(yd[:, :], t1[:, :], AF.Silu, ["t1"], ["yd"])
                          stage(4.85)
                          transpose2([yd[:, 0:128], yd[:, 128:256]], ["yd"],
                                     yT[:, 6:8, s * 128:(s + 1) * 128], [("hy", 3)], evac="act")
                          yield
                          stage(4.9 + 0.01 * s)
                      if os.environ.get("KHALO", "1") == "1":
                          cp(dT[:, :, 0:32], dT[:, :, 512:544], ["dT"], ["dT"])
                      elif os.environ.get("KHALO") == "2":
                          for cc in range(2):
                              act(dT[:, cc, 0:32], dT[:, cc, 512:544], AF.Copy, ["dT"], ["dT"])
                      stage(5)
                      slot, W = load_chunk(l, 3)
                      for s in range(4):
                          pi = next_pm()
                          for kc in range(8):
                              mm(pM[pi][:, :], hT[:, kc, s * 128:(s + 1) * 128], W[:, kc, :], kc == 0, kc == 7,
                                 [("wr", slot)] + HT_ALL, [K_ps(pi)])
                          act(ub[:, :], pM[pi][:, 0:256], AF.Copy, [K_ps(pi)], ["ub"])
                          act(v32[:, :], pM[pi][:, 256:512], AF.Copy, [K_ps(pi)], ["v32"])
                          layernorm_tok(v32[:, :], ["v32"], vnb[:, :], ["vnb"], 0, 1)
                          yield
                          stage(5.3)
                          gbk, gky = (pD, "pD") if s % 2 == 0 else (pO, "pO")
                          for h in range(4):
                              mm(gbk[:, h * 64:(h + 1) * 64], wsT[:, h, :], vnb[:, h * 64:(h + 1) * 64], True, True,
                                 ["wsT", "vnb"], [gky])
                          for h in range(4):
                              stt(yb[:, h * 64:(h + 1) * 64], gbk[:, h * 64:(h + 1) * 64], bsT[:, h:h + 1],
                                  ub[:, h * 64:(h + 1) * 64], ALU.add, ALU.mult, [gky, "bsT", "ub"], ["yb"])
                          transpose2([yb[:, 0:128], yb[:, 128:256]], ["yb"],
                                     yT[:, 2:4, s * 128:(s + 1) * 128], [("hy", 1)], evac="act")
                          yield
                      stage(6)
                      slot, W = load_chunk(l, 4)
                      for s in range(4):
                          blk = t * 4 + s
                          pi = next_pm()
                          for kc in range(8):
                              mm(pM[pi][:, :], hT[:, kc, s * 128:(s + 1) * 128], W[:, kc, :], kc == 0, kc == 7,
                                 [("wr", slot)] + HT_ALL, [K_ps(pi)])
                          act(qk32[:, :], pM[pi][:, :], AF.Copy, [K_ps(pi)], ["qk32"])
                          rope(qkr[:, :], qk32[:, :], ["qk32"], 8, blk, ["qkr"])
                          yield
                          i_ = tcount[0] % 2
                          tcount[0] += 1
                          tr(pT[i_][:, 0, :], qkr[:, 0:128], ["qkr"], [("pT", i_)])
                          tr(pT[i_][:, 1, :], qkr[:, 128:256], ["qkr"], [("pT", i_)])
                          act(qbd[0:64, :, s, 0:128], pT[i_][0:64, 0:2, :], AF.Copy, [("pT", i_)], ["qbd"])
                          act(qbd[64:128, :, s, 128:256], pT[i_][64:128, 0:2, :], AF.Copy, [("pT", i_)], ["qbd"])
                          transpose2([qkr[:, 256:384], qkr[:, 384:512]], ["qkr"],
                                     kT[:, :, blk * 128:(blk + 1) * 128], ["kT"], evac="act")
                          yield
                      return

                  stage(7)
                  I4f = I4[:, 0:2, :].rearrange("p h q -> p (h q)")

                  def gen_index(s):
                      qb = t * 4 + s
                      nk = (qb + 1) * 128
                      sc_, sk_ = SC[s % 2]
                      for j in range(4):
                          act(dsgn[:, j, :], identb[:, :], AF.Copy, ["identb", "sgn"], ["dsgn"], scale=sgn[:, s, j:j + 1])
                      nch = (nk + 511) // 512
                      for c in range(nch):
                          k0 = c * 512
                          n = min(512, nk - k0)
                          for half in range(2):
                              for jj in range(2):
                                  j = half * 2 + jj
                                  base = (j % 2) * 64
                                  mm(pM[2 + jj][:, 0:n], qiT_all[base:base + 64, j // 2, s * 128:(s + 1) * 128],
                                     kiT2[base:base + 64, k0:k0 + n], True, True, ["qiT_all", "kiT2"], [K_ps(2 + jj)])
                              for jj in range(2):
                                  j = half * 2 + jj
                                  act(rbuf[:, j, 0:n], pM[2 + jj][:, 0:n], AF.Relu, [K_ps(2 + jj)], [("rbuf", j)])
                          yield
                          for j in range(4):
                              mm(pD[:, 0:n], dsgn[:, j, :], rbuf[:, j, 0:n], j == 0, j == 3, ["dsgn", ("rbuf", j)], ["pD"])
                          act(sc_[:, k0:k0 + n], pD[:, 0:n], AF.Copy, ["pD"], sk_)
                          yield

                  def emit_bisect(s):
                      qb = t * 4 + s
                      nk = (qb + 1) * 128
                      score_t, skey = SC[s % 2]
                      jko = jk[:, 0:1].to_broadcast([128, nk])
                      if qb >= 2:
                          S.op("dve", lambda e, nk=nk: e.reduce_max(out=amax[:, :], in_=score_t[:, 0:nk], axis=AX.X,
                                                                   apply_absolute_value=True), skey, ["amax"])
                      tt(score_t[:, nk - 128:nk], score_t[:, nk - 128:nk], cb[:, :], ALU.add, skey + ["cb"], skey)
                      if qb >= 2:
                          ts(Wt[:, :], pow2[:, :], amax[:, 0:1], None, ALU.mult, None, ["pow2", "amax"], ["Wt"])
                          S.op("dve", lambda e: e.memset(cntb[:, :], 0.0), [], ["cntb"])
                          stt(mid[1][:, :], amax[:, :], -1.0, Wt[:, 1:2], ALU.mult, ALU.add, ["amax", "Wt"], ["mid1"])
                          for k in range(1, NBIS + 1):
                              mc = mid[k % 2]
                              mn = mid[(k + 1) % 2]
                              ts(jko, score_t[:, 0:nk], mc[:, 0:1], 0.0, ALU.is_ge, ALU.add,
                                 skey + ["mid%d" % (k % 2)], ["jk", "cntb"], accum_out=cntb[:, k:k + 1])
                              stt(tq[:, :], cntb[:, k:k + 1], 255.5, Wt[:, k:k + 1], ALU.is_ge, ALU.mult,
                                  ["cntb", "Wt"], ["tq"])
                              if k < NBIS:
                                  stt(mn[:, :], tq[:, :], Wt[:, k + 1:k + 2], mc[:, :], ALU.subtract, ALU.add,
                                      ["tq", "Wt", "mid%d" % (k % 2)], ["mid%d" % ((k + 1) % 2)])
                              else:
                                  stt(thr[:, :], tq[:, :], Wt[:, k:k + 1], mc[:, :], ALU.subtract, ALU.add,
                                      ["tq", "Wt", "mid%d" % (k % 2)], ["thr"])
                      else:
                          ts(thr[:, :], pow2[:, 0:1], 0.0, -1e29, ALU.mult, ALU.add, ["pow2"], ["thr"])

                  def emit_bisect_act(s, gf):
                      qb = t * 4 + s
                      nk = (qb + 1) * 128
                      score_t, skey = SC[s % 2]
                      S.op("dve", lambda e, nk=nk: e.reduce_max(out=amax[:, :], in_=score_t[:, 0:nk], axis=AX.X,
                                                               apply_absolute_value=True), skey, ["amax"])
                      tt(score_t[:, nk - 128:nk], score_t[:, nk - 128:nk], cb[:, :], ALU.add, skey + ["cb"], skey)
                      ts(Wt[:, :], pow2[:, :], amax[:, 0:1], None, ALU.mult, None, ["pow2", "amax"], ["Wt"])
                      S.op("dve", lambda e: e.memset(cntb[:, :], 0.0), [], ["cntb"])
                      stt(mid[1][:, :], amax[:, :], -1.0, Wt[:, 1:2], ALU.mult, ALU.add, ["amax", "Wt"], ["mid1"])
                      for k in range(1, NBIS + 1):
                          mc = mid[k % 2]
                          mn = mid[(k + 1) % 2]
                          S.op("act", lambda e, k=k, mc=mc, nk=nk: e.activation(
                              out=mb[:, 0:nk], in_=score_t[:, 0:nk], func=AF.Sign, bias=mc[:, 0:1], scale=-1.0,
                              accum_out=cntb[:, k:k + 1]), skey + ["mid%d" % (k % 2)], ["mb", "cntb"])
                          run(gf, 2)
                          stt(tq[:, :], cntb[:, k:k + 1], float(nk) - 511.0, Wt[:, k:k + 1], ALU.is_le, ALU.mult,
                              ["cntb", "Wt"], ["tq"])
                          if k < NBIS:
                              stt(mn[:, :], tq[:, :], Wt[:, k + 1:k + 2], mc[:, :], ALU.subtract, ALU.add,
                                  ["tq", "Wt", "mid%d" % (k % 2)], ["mid%d" % ((k + 1) % 2)])
                          else:
                              stt(thr[:, :], tq[:, :], Wt[:, k:k + 1], mc[:, :], ALU.subtract, ALU.add,
                                  ["tq", "Wt", "mid%d" % (k % 2)], ["thr"])

                  def emit_mask(s):
                      nk = (t * 4 + s + 1) * 128
                      score_t, skey = SC[s % 2]
                      ts(mb[:, 0:nk], score_t[:, 0:nk], thr[:, 0:1], -1.0, ALU.is_ge, ALU.add,
                         skey + ["thr"], ["mb"])

                  def gen_attn(s):
                      qb = t * 4 + s

                      def st_mm(kb):
                          pi = kb % 2
                          for p in range(2):
                              mm(pM[pi][:, p * 256:(p + 1) * 256], kT[:, p, kb * 128:(kb + 1) * 128], qbd[:, p, s, :],
                                 True, False, ["kT", "qbd"], [K_ps(pi)])
                              mm(pM[pi][:, p * 256:(p + 1) * 256], mb[:, kb * 128:(kb + 1) * 128], I4f,
                                 False, True, ["mb", "I4"], [K_ps(pi)])
                      st_mm(0)
                      for kb in range(qb + 1):
                          pi = kb % 2
                          if kb + 1 <= qb:
                              st_mm(kb + 1)
                          pt = PT[kb % 2]
                          act(pt[:, :], pM[pi][:, :], AF.Exp, [K_ps(pi)], [("PT", kb % 2)], scale=0.125)
                          for h in range(4):
                              mm(pO[:, h * 65:(h + 1) * 65], pt[:, h * 128:(h + 1) * 128], V[:, kb, h * 65:(h + 1) * 65],
                                 (kb == 0 and h == 0), kb == qb, [("PT", kb % 2), "V"], ["pO"])
                          yield

                  def emit_final(s):
                      act(o32[:, :], pO[:, 0:260], AF.Copy, ["pO"], ["o32"])
                      pOv = o32[:, :].rearrange("p (h e) -> p h e", h=4)
                      S.op("dve", lambda e, pOv=pOv: e.reciprocal(out=rec[:, :], in_=pOv[:, :, 64]), ["o32"], ["rec"])
                      for h in range(4):
                          ts(yc[:, h * 64:(h + 1) * 64], o32[:, h * 65:h * 65 + 64], rec[:, h:h + 1], None,
                             ALU.mult, None, ["o32", "rec"], ["yc"])
                      transpose2([yc[:, 0:128], yc[:, 128:256]], ["yc"],
                                 yT[:, 4:6, s * 128:(s + 1) * 128], [("hy", 2)], evac="act")

                  def run(g, n=1):
                      for _ in range(n):
                          try:
                              next(g)
                          except StopIteration:
                              return False
                      return True

                  g0 = gen_index(0)
                  while run(g0):
                      pass
                  gf = gen_filler()
                  if t * 4 >= 2:
                      emit_bisect_act(0, gf)
                  else:
                      emit_bisect(0)
                  while run(gf):
                      pass
                  emit_mask(0)
                  g1 = gen_index(1)
                  while run(g1):
                      pass
                  for s in range(4):
                      if s < 3:
                          emit_bisect(s + 1)
                      ga = gen_attn(s)
                      if s < 2:
                          gi = gen_index(s + 2)
                          alive = True
                          while alive:
                              alive = run(gi)
                              run(ga)
                      while run(ga):
                          pass
                      emit_final(s)
                      if s < 3:
                          emit_mask(s + 1)
                  if dbg and l == 0 and t == 0:
                      tap("yT", yT[:, :, :], [128, 8, 512], HY_ALL)
                  stage(8)
                  for cg in range(2):
                      slot, W = load_chunk(l, 6 + cg)
                      for s in range(4):
                          pi = next_pm()
                          for kc in range(8):
                              mm(pM[pi][:, :], yT[:, kc, s * 128:(s + 1) * 128], W[:, kc, :], kc == 0, kc == 7,
                                 [("wr", slot)] + HY_ALL, [K_ps(pi)])
                          tt(xt[:, s, cg * 512:(cg + 1) * 512], pM[pi][:, :], xt[:, s, cg * 512:(cg + 1) * 512], ALU.add,
                             [K_ps(pi), ("xt", s)], [("xt", s)])
                  if dbg and l == 0 and t == 0:
                      tap("xmid", xt[:, :, :], [128, 4, 1024], XT_ALL)
                  stage(9)
                  rmsnorm_to_hT(1)
                  AT_ALL = [("aT", i) for i in range(22)] + ["score", "mb"]
                  def gen_ffn(t):
                      yield from gen_gateup(t)
                      yield from gen_down(t)
                  gd = gen_ffn(t)
                  if t + 1 < NTILE:
                      gfr = gen_front(t + 1)
                      a1 = a2_ = True
                      while a1 or a2_:
                          if a1:
                              a1 = run(gd)
                          k_ = 0
                          while a2_ and k_ < 3:
                              try:
                                  r_ = next(gfr)
                              except StopIteration:
                                  a2_ = False
                                  break
                              k_ += 1
                              if r_ == "sync" and a1:
                                  break
                      load_xt(t + 1)
                  else:
                      while run(gd):
                          pass
        except _Stop:
            dma("pool", out[0:512, :].rearrange("(s p) d -> p s d", p=128), xt[:, :, :],
                [("xt", s) for s in range(4)], ["out"], "ost")
        S.final_wait("pool", ["out"] + ["dbg_" + k for k in dbg_outs])
        S.final_wait("sp", ["out"] + ["dbg_" + k for k in dbg_outs])
        S.emit()
    return nc


def make_consts(SEQ):
    NB = SEQ // 128
    f32 = np.float32
    ident = np.eye(128, dtype=f32)
    tril = np.tril(np.ones((128, 128), dtype=f32))
    cbm = np.where(np.arange(128)[None, :] <= np.arange(128)[:, None], 0.0, -1e30).astype(f32)
    inv_freq = (f32(10000.0) ** (-(np.arange(0, 64, 2, dtype=f32)) / f32(64))).astype(f32)
    ang = (np.arange(SEQ, dtype=f32)[:, None] * inv_freq[None, :]).astype(f32)
    cos = np.cos(ang).astype(f32).reshape(NB, 128, 32).transpose(1, 0, 2)
    sin = np.sin(ang).astype(f32).reshape(NB, 128, 32).transpose(1, 0, 2)
    pw = (2.002 * 2.0 ** (-np.arange(NBIS + 2, dtype=np.float64))).astype(f32)
    pow2 = np.broadcast_to(pw[None, :], (128, NBIS + 2))
    return {"c_ident": ident, "c_tril": tril, "c_cb": cbm, "c_cos": np.ascontiguousarray(cos),
            "c_sin": np.ascontiguousarray(sin), "c_pow2": np.ascontiguousarray(pow2)}


_NC_CACHE = {}


def kernel(**inputs):
    x = np.asarray(inputs["x"], dtype=np.float32)
    B, SEQ, _ = x.shape
    DEPTH = inputs["w_in"].shape[0]
    key = (SEQ, DEPTH)
    if key not in _NC_CACHE:
        _NC_CACHE[key] = build(SEQ, DEPTH)
    nc = _NC_CACHE[key]
    consts = make_consts(SEQ)
    params = {k: np.ascontiguousarray(np.asarray(v, dtype=np.float32)) for k, v in inputs.items() if k != "x"}
    n = 8
    in_maps = []
    for c in range(n):
        m = {"x": np.ascontiguousarray(x[c % B])}
        m.update(params)
        m.update(consts)
        in_maps.append(m)
    res = run_bass_kernel_spmd(nc, in_maps, core_ids=list(range(n)))
    outp = np.stack([np.asarray(res.results[b]["out"], dtype=np.float32) for b in range(B)], axis=0)
    return outp
```

```python
import numpy as np
from contextlib import ExitStack
import concourse.bass as bass
import concourse.mybir as mybir
from concourse.bass_utils import run_bass_kernel_spmd

F32 = mybir.dt.float32
BF16 = mybir.dt.bfloat16
ALU = mybir.AluOpType
AF = mybir.ActivationFunctionType
AX = mybir.AxisListType

D = 1024
G = 256
HID = 2816
INC = 2884
NBIS = 15
ENGS = ("pe", "act", "dve", "pool", "sp")


class Sched:
    def __init__(self, nc, stack):
        self.nc = nc
        self.stack = stack
        self.q = {e: [] for e in ENGS}
        self.sem = {e: stack.enter_context(nc.semaphore("sem_" + e)) for e in ENGS}
        self.cnt = {e: 0 for e in ENGS}
        self.waited = {e: {} for e in ENGS}
        self.last_write = {}
        self.readers = {}
        self.dsem = {}
        self.dcnt = {}

    def _dsem(self, key):
        if key not in self.dsem:
            self.dsem[key] = self.stack.enter_context(self.nc.semaphore("dsem_%d" % len(self.dsem)))
            self.dcnt[key] = 0
        return self.dsem[key]

    def op(self, eng, fn, reads=(), writes=(), dma=None):
        deps = []
        for r in reads:
            t = self.last_write.get(r)
            if t is not None:
                deps.append(t)
        for w in writes:
            t = self.last_write.get(w)
            if t is not None:
                deps.append(t)
            deps.extend(self.readers.get(w, ()))
        wd = self.waited[eng]
        m = {}
        for (sem, val, deng) in deps:
            if deng == "pe" and eng == "pe" and dma is None:
                continue
            k = id(sem)
            if wd.get(k, 0) >= val:
                continue
            if k not in m or m[k][1] < val:
                m[k] = (sem, val)
        for k, (sem, val) in m.items():
            wd[k] = val
        waits = list(m.values())
        if dma is not None:
            sem = self._dsem(dma)
            self.dcnt[dma] += 1
            tok = (sem, 16 * self.dcnt[dma], "dma")
            inc = 16
        else:
            self.cnt[eng] += 1
            tok = (self.sem[eng], self.cnt[eng], eng)
            inc = 1
        self.q[eng].append((waits, fn, tok[0], inc))
        for w in writes:
            self.last_write[w] = tok
            self.readers[w] = []
        for r in reads:
            if r in writes:
                continue
            lst = self.readers.setdefault(r, [])
            lst[:] = [x for x in lst if not (x[0] is tok[0])]
            lst.append(tok)
        return tok

    def final_wait(self, eng, keys):
        waits = []
        for k in keys:
            t = self.last_write.get(k)
            if t is not None:
                waits.append((t[0], t[1]))
        self.q[eng].append((waits, None, None, 0))

    def emit(self):
        nc = self.nc
        import os
        if os.environ.get("KDUMP"):
            names = {id(v): k for k, v in self.sem.items()}
            names.update({id(v): "d:" + str(k) for k, v in self.dsem.items()})
            for e in ENGS:
                print("ENG", e, "n=", len(self.q[e]), "cnt=", self.cnt[e])
                for (waits, fn, sem, inc) in self.q[e][-6:]:
                    print("   waits", [(names.get(id(s_), "?"), v) for (s_, v) in waits], "inc", names.get(id(sem)), inc)
        with nc.Block() as block:
            def run(eng_name):
                def body(e):
                    for (waits, fn, sem, inc) in self.q[eng_name]:
                        for (s, v) in waits:
                            e.wait_ge(s, v)
                        if fn is not None:
                            fn(e).then_inc(sem, inc)
                return body
            block.tensor(run("pe"))
            block.scalar(run("act"))
            block.vector(run("dve"))
            block.gpsimd(run("pool"))
            block.sync(run("sp"))


C_B, C_C, C_H, C_U, C_V, C_Q, C_K, C_VV, C_QI, C_KI, C_WI, C_DA, C_DG = (
    0, 256, 512, 768, 1024, 1280, 1536, 1792, 2048, 2304, 2368, 2372, 2628)
def chunk_table():
    t = []
    t.append((0, 1024, [("w_in", C_B, 512, 0)]))
    t.append((0, 1024, [("w_in", C_H, 256, 0), ("w_in", C_DA, 256, 256)]))
    t.append((0, 1024, [("w_in", C_DG, 256, 0), ("w_in", C_KI, 68, 256)]))
    t.append((0, 1024, [("w_in", C_U, 512, 0)]))
    t.append((0, 1024, [("w_in", C_Q, 512, 0)]))
    t.append((0, 1024, [("w_in", C_VV, 512, 0)]))
    t.append((0, 1024, [("w_out", 0, 512, 0)]))
    t.append((0, 1024, [("w_out", 512, 512, 0)]))
    for j in range(11):
        t.append((0, 1024, [("w_gate", j * 256, 256, 0), ("w_up", j * 256, 256, 256)]))
    for cg in range(2):
        for g in range(3):
            nr = 1024 if g < 2 else 768
            t.append((g * 1024, nr, [("w_down", cg * 512, 512, 0)]))
    return t


NCHUNK = 25


class _Stop(Exception):
    pass


def build(SEQ, DEPTH, dbg=False, stop=None):
    nc = bass.Bass("TRN2", target_bir_lowering=False)

    def stage(n):
        if stop is not None and n > stop:
            raise _Stop()
    NTILE = SEQ // 512
    NB = SEQ // 128

    def din(name, shape):
        return nc.dram_tensor(name, list(shape), F32, kind="ExternalInput").ap()

    x_in = din("x", [SEQ, D])
    P = {}
    for name, shape in [("g_mix", [DEPTH, D]), ("w_in", [DEPTH, D, INC]), ("w_conv_a", [DEPTH, 3, G]),
                        ("gmlp_ln_g", [DEPTH, G]), ("gmlp_ln_b", [DEPTH, G]), ("w_s", [DEPTH, 4, 128, 128]),
                        ("b_s", [DEPTH, 4, 128]), ("w_conf", [DEPTH, 31, G]), ("b_conf", [DEPTH, G]),
                        ("conf_ln_g", [DEPTH, G]), ("conf_ln_b", [DEPTH, G]), ("w_out", [DEPTH, D, D]),
                        ("g_ffn", [DEPTH, D]), ("w_gate", [DEPTH, D, HID]), ("w_up", [DEPTH, D, HID]),
                        ("w_down", [DEPTH, HID, D]), ("g_final", [D])]:
        P[name] = din(name, shape)
    c_ident = din("c_ident", [128, 128])
    c_tril = din("c_tril", [128, 128])
    c_cb = din("c_cb", [128, 128])
    c_cos = din("c_cos", [128, NB, 32])
    c_sin = din("c_sin", [128, NB, 32])
    c_pow2 = din("c_pow2", [128, NBIS + 2])
    out = nc.dram_tensor("out", [SEQ, D], F32, kind="ExternalOutput").ap()
    wsc = nc.dram_tensor("wsc", [DEPTH, NCHUNK, 1024, 512], BF16, kind="Internal").ap()
    xs = nc.dram_tensor("xs", [SEQ, D], F32, kind="Internal").ap()
    dbg_outs = {}

    with ExitStack() as st:
        S = Sched(nc, st)

        def sb(name, shape, dt):
            return st.enter_context(nc.sbuf_tensor(name, list(shape), dt))

        def psum(name, shape, dt):
            return st.enter_context(nc.psum_tensor(name, list(shape), dt))

        wring_all = sb("wring", [128, 3, 8, 512], BF16)
        wring = [wring_all[:, i, :, :] for i in range(3)]
        xt = sb("xt", [128, 4, 1024], F32)
        hy = sb("hy", [128, 4096], BF16)
        hn = hy[:, :].rearrange("p (s d) -> p s d", s=4)
        yT = hy[:, :].rearrange("p (k t) -> p k t", k=8)
        hT = sb("hT", [128, 8, 512], BF16)
        hTb = sb("hTb", [128, 8, 512], BF16)
        bT = sb("bT", [128, 2, 512], BF16)
        cT = sb("cT", [128, 2, 512], BF16)
        ch = sb("ch", [128, 2, 514], BF16)
        accA = sb("accA", [128, 512], F32)
        daT = sb("daT", [128, 2, 512], BF16)
        sig = sb("sig", [128, 512], BF16)
        dT = sb("dT", [128, 2, 544], BF16)
        kraw = sb("kraw", [128, 4, 64], F32)
        wraw = sb("wraw", [128, 4, 4], F32)
        wabs = sb("wabs", [128, 4, 4], F32)
        sgn = sb("sgn", [128, 4, 4], F32)
        ub = sb("ub", [128, 256], BF16)
        vn32 = sb("vn32", [128, 256], F32)
        vnb = sb("vnb", [128, 256], BF16)
        v32 = sb("v32", [128, 256], F32)
        qk32 = sb("qk32", [128, 512], F32)
        o32 = sb("o32", [128, 260], F32)
        wsf = qk32[:, :].rearrange("p (h s) -> p h s", h=4)
        yb = sb("yb", [128, 256], BF16)
        ropeA = sb("ropeA", [128, 512], F32)
        ropeB = sb("ropeB", [128, 512], F32)
        junk = ropeA[:, :].bitcast(BF16)
        qkr = sb("qkr", [128, 512], BF16)
        qbd = sb("qbd", [128, 2, 4, 256], BF16)
        jk = sb("jk", [128, 4], BF16)
        qiT_all = sb("qiT_all", [128, 2, 512], BF16)
        qib = sb("qib", [128, 256], BF16)
        kib = sb("kib", [128, 128], BF16)
        dsgn = sb("dsgn", [128, 4, 128], BF16)
        kT = sb("kT", [128, 2, SEQ], BF16)
        kiT2 = sb("kiT2", [128, SEQ], BF16)
        V = sb("V", [128, NB, 260], BF16)
        big = sb("big", [128, 12288], BF16)
        rbuf = sb("rbuf", [128, 4, 512], BF16)
        PT = [sb("PT%d" % i, [128, 512], BF16) for i in range(2)]
        yc = sb("yc", [128, 256], BF16)
        rec = sb("rec", [128, 4], F32)
        t1 = sb("t1", [128, 256], F32)
        yd = sb("yd", [128, 256], BF16)
        cosT = sb("cosT", [128, 4, 32], F32)
        sinT = sb("sinT", [128, 4, 32], F32)
        sg = [sb("sg%d" % i, [128, 512], BF16) for i in range(2)]
        diag = sb("diag", [128, 2, 31, 128], BF16)
        wsT = sb("wsT", [128, 4, 128], BF16)
        wsm = sb("wsm", [128, 4, 128], BF16)
        identf = sb("identf", [128, 128], F32)
        identb = sb("identb", [128, 128], BF16)
        I4 = sb("I4", [128, 4, 128], BF16)
        tril = sb("tril", [128, 128], F32)
        cb = sb("cb", [128, 128], F32)
        pow2 = sb("pow2", [128, NBIS + 2], F32)
        lnB = sb("lnB", [128, 5, 256], F32)
        gfinB = sb("gfinB", [128, 1024], F32)
        gT = sb("gT", [128, 2, 8], F32)
        wcaT = sb("wcaT", [128, 2, 3], F32)
        wcfT = sb("wcfT", [128, 2, 31], F32)
        bsT = sb("bsT", [128, 4], F32)
        ssq = sb("ssq", [128, 4], F32)
        rstd = sb("rstd", [128, 4], F32)
        st6 = sb("st6", [128, 6], F32)
        mv = sb("mv", [128, 2], F32)
        lrs = sb("lrs", [128, 1], F32)
        amax = sb("amax", [128, 1], F32)
        Wt = sb("Wt", [128, NBIS + 2], F32)
        cntb = sb("cntb", [128, NBIS + 2], F32)
        mid = [sb("mid%d" % i, [128, 1], F32) for i in range(2)]
        tq = sb("tq", [128, 1], F32)
        thr = sb("thr", [128, 1], F32)
        score_t = big[:, 0:2 * SEQ].bitcast(F32)
        mb = big[:, 8192:8192 + SEQ]
        scoreB = wring_all[:, 0:2, :, :].rearrange("p a k n -> p (a k n)").bitcast(F32)[:, 0:SEQ]
        SC = [(score_t, ["score"]), (scoreB, [("wr", 0), ("wr", 1)])]
        qi32 = sb("qi32", [128, 256], F32)
        ki32 = sb("ki32", [128, 64], F32)
        aT = big[:, :].rearrange("p (k t) -> p k t", k=24)

        pT = [psum("pT%d" % i, [128, 8, 128], BF16) for i in range(2)]
        pTf = [pT[i][:, :, :].rearrange("p a b -> p (a b)").bitcast(F32) for i in range(2)]
        pM = [psum("pM%d" % i, [128, 512], F32) for i in range(4)]
        pO = psum("pO", [128, 512], F32)
        pD = psum("pD", [128, 512], F32)

        def K_ps(i):
            return ("pM", i)

        def act(out_, in_, func, r, w, **kw):
            S.op("act", lambda e: e.activation(out=out_, in_=in_, func=func, **kw), r, w)

        def tt(out_, in0, in1, op, r, w, eng="dve"):
            S.op(eng, lambda e: e.tensor_tensor(out=out_, in0=in0, in1=in1, op=op), r, w)

        def ts(out_, in0, s1, s2, op0, op1, r, w, eng="dve", **kw):
            if op1 is None:
                S.op(eng, lambda e: e.tensor_scalar(out=out_, in0=in0, scalar1=s1, scalar2=None, op0=op0, **kw), r, w)
            else:
                S.op(eng, lambda e: e.tensor_scalar(out=out_, in0=in0, scalar1=s1, scalar2=s2, op0=op0, op1=op1, **kw), r, w)

        def stt(out_, in0, scalar, in1, op0, op1, r, w, eng="dve"):
            S.op(eng, lambda e: e.scalar_tensor_tensor(out=out_, in0=in0, scalar=scalar, in1=in1, op0=op0, op1=op1), r, w)

        def cp(out_, in_, r, w, eng="dve"):
            S.op(eng, lambda e: e.tensor_copy(out=out_, in_=in_), r, w)

        def mm(out_, lhsT, rhs, start, stop, r, w):
            S.op("pe", lambda e: e.matmul(out_, lhsT=lhsT, rhs=rhs, start=start, stop=stop, skip_group_check=True), r, w)

        def tr(out_, in_, r, w):
            S.op("pe", lambda e: e.transpose(out=out_, in_=in_, identity=identb[:, :]), list(r) + ["identb"], w)

        def dma(eng, out_, in_, r, w, key, slow=False):
            if slow:
                S.op(eng, lambda e: e.dma_start(out=out_, in_=in_, allow_slow_non_contiguous=True), r, w, dma=key)
            else:
                S.op(eng, lambda e: e.dma_start(out=out_, in_=in_), r, w, dma=key)

        ntap = [0]

        def tap(name, ap, shape, rkeys):
            if not dbg:
                return
            d = nc.dram_tensor("dbg_" + name, list(shape), ap.dtype, kind="ExternalOutput").ap()
            dbg_outs[name] = d
            ntap[0] += 1
            dma("sp", d, ap, rkeys, ["dbg_" + name], ("dbg", ntap[0]))

        table = chunk_table()
        import os
        NCAST = int(os.environ.get("KNCAST", "1000"))
        for l in range(DEPTH):
            for ci, (r0, nr, cols) in enumerate(table):
                if ci >= NCAST:
                    continue
                for (src, c0, n, d0) in cols:
                    dma("pool", wsc[l, ci, 0:nr, d0:d0 + n], P[src][l, r0:r0 + nr, c0:c0 + n],
                        [], [("wsc", l, ci)], ("cast", l, ci))

        dma("sp", identf[:, :], c_ident, [], ["identf"], "c0")
        dma("sp", tril[:, :], c_tril, [], ["tril"], "c1")
        dma("sp", cb[:, :], c_cb, [], ["cb"], "c2")
        dma("sp", pow2[:, :], c_pow2, [], ["pow2"], "c5")
        dma("sp", gfinB[:, :], P["g_final"].partition_broadcast(128), [], ["gfinB"], "c6")
        cp(identb[:, :], identf[:, :], ["identf"], ["identb"])
        for h in range(4):
            ts(I4[:, h, :], identf[:, :], 30000.0, None, ALU.mult, None, ["identf"], ["I4"])
        Vv = V[:, :, :].rearrange("p b (h e) -> p b h e", h=4)
        S.op("dve", lambda e: e.memset(V[:, :, :], 1.0), [], ["V"])
        S.op("dve", lambda e: e.memset(qbd[:, :, :, :], 0.0), [], ["qbd"])

        ring_ctr = [0]

        grp_ctr = {"ffn": 0}

        def load_chunk(l, ci, nkc=8, group=None):
            if group == "ffn":
                slot = grp_ctr["ffn"] % 2
                grp_ctr["ffn"] += 1
            elif group == "front":
                slot = 2
            else:
                slot = ring_ctr[0] % 3
                ring_ctr[0] += 1
            ncol = max(d0 + n for (src_, c0, n, d0) in table[ci][2])
            src = wsc[l, ci].rearrange("(k p) n -> p k n", p=128)
            dma("sp", wring[slot][:, 0:nkc, 0:ncol], src[:, 0:nkc, 0:ncol], [("wsc", l, ci)], [("wr", slot)], ("wr", slot))
            return slot, wring[slot]

        tcount = [0]

        def transpose2(src_aps, src_keys, dst_ap, dst_keys, evac="dve"):
            i = tcount[0] % 2
            tcount[0] += 1
            n = len(src_aps)
            for j, a in enumerate(src_aps):
                tr(pT[i][:, j, :], a, src_keys, [("pT", i)])
            srcp = pT[i][:, 0:n, :] if n > 1 else pT[i][:, 0, :]
            if evac == "act":
                act(dst_ap, srcp, AF.Copy, [("pT", i)], dst_keys)
            else:
                cp(dst_ap, srcp, [("pT", i)], dst_keys)

        def rmsnorm_to_hT(gidx):
            S.op("dve", lambda e: e.memset(ssq[:, :], 0.0), [], [("ssq", i) for i in range(4)])
            for s in range(4):
                act(junk[:, :], xt[:, s, :], AF.Square, [("xt", s)], ["ropeA", ("ssq", s)], accum_out=ssq[:, s:s + 1])
            for s in range(4):
                ts(rstd[:, s:s + 1], ssq[:, s:s + 1], 1.0 / D, 1e-6, ALU.mult, ALU.add, [("ssq", s)], [("rstd", s)])
                act(rstd[:, s:s + 1], rstd[:, s:s + 1], AF.Sqrt, [("rstd", s)], [("rstd", s)])
                S.op("dve", lambda e, s=s: e.reciprocal(out=rstd[:, s:s + 1], in_=rstd[:, s:s + 1]),
                     [("rstd", s)], [("rstd", s)])
                act(hn[:, s, :], xt[:, s, :], AF.Copy, [("xt", s), ("rstd", s)], [("hy", s)], scale=rstd[:, s:s + 1])
            for kc in range(8):
                i = tcount[0] % 2
                tcount[0] += 1
                for s in range(4):
                    tr(pT[i][:, s, :], hn[:, s, kc * 128:(kc + 1) * 128], [("hy", s)], [("pT", i)])
                src = pT[i][:, 0:4, :].rearrange("p s t -> p (s t)")
                if kc % 2 == 0:
                    ts(hTb[:, kc, :], src, gT[:, gidx, kc:kc + 1], None, ALU.mult, None, [("pT", i), "gT"], [("hTb", kc)])
                else:
                    act(hTb[:, kc, :], src, AF.Copy, [("pT", i), "gT"], [("hTb", kc)], scale=gT[:, gidx, kc:kc + 1])

        def run(g, n=1):
            for _ in range(n):
                try:
                    next(g)
                except StopIteration:
                    return False
            return True

        pfc = [0]

        def nb():
            i = pfc[0] % 2
            pfc[0] += 1
            return (pD, "pD") if i == 0 else (pO, "pO")

        ssq2 = sb("ssq2", [128, 4], F32)
        rstd2 = sb("rstd2", [128, 4], F32)
        xstg = rbuf[:, :, :].rearrange("p a b -> p (a b)").bitcast(F32)
        RB_ALL = [("rbuf", j) for j in range(4)]
        XT_ALL = [("xt", s_) for s_ in range(4)]
        HT_ALL = [("hT", k) for k in range(8)]
        HTB_ALL = [("hTb", k) for k in range(8)]
        HY_ALL = [("hy", k) for k in range(4)]
        pmc = [0]

        def next_pm():
            i = pmc[0] % 4
            pmc[0] += 1
            return i

        def rope(dst, src_ps, src_keys, ngrp, blk, dst_keys):
            n = ngrp * 64
            if ngrp == 1:
                rk = list(src_keys) + ["cosT", "sinT"]
                cs = cosT[:, blk % 4, :]
                sn = sinT[:, blk % 4, :]
                tt(ropeA[:, 0:32], src_ps[:, 0:32], cs, ALU.mult, rk, ["ropeA"])
                tt(ropeA[:, 32:64], src_ps[:, 32:64], cs, ALU.mult, rk, ["ropeA"])
                tt(ropeB[:, 0:32], src_ps[:, 32:64], sn, ALU.mult, rk, ["ropeB"])
                tt(ropeB[:, 32:64], src_ps[:, 0:32], sn, ALU.mult, rk, ["ropeB"])
                tt(dst[:, 0:32], ropeA[:, 0:32], ropeB[:, 0:32], ALU.subtract, ["ropeA", "ropeB"], dst_keys)
                tt(dst[:, 32:64], ropeA[:, 32:64], ropeB[:, 32:64], ALU.add, ["ropeA", "ropeB"], dst_keys)
                return
            v = src_ps.rearrange("p (g h d) -> p g h d", g=ngrp, h=2)
            A = ropeA[:, 0:n].rearrange("p (g h d) -> p g h d", g=ngrp, h=2)
            B = ropeB[:, 0:n].rearrange("p (g h d) -> p g h d", g=ngrp, h=2)
            cs = cosT[:, blk % 4:blk % 4 + 1, :].to_broadcast([128, ngrp, 32])
            sn = sinT[:, blk % 4:blk % 4 + 1, :].to_broadcast([128, ngrp, 32])
            rk = list(src_keys) + ["cosT", "sinT"]
            tt(A[:, :, 0, :], v[:, :, 0, :], cs, ALU.mult, rk, ["ropeA"])
            tt(A[:, :, 1, :], v[:, :, 1, :], cs, ALU.mult, rk, ["ropeA"])
            tt(B[:, :, 0, :], v[:, :, 1, :], sn, ALU.mult, rk, ["ropeB"])
            tt(B[:, :, 1, :], v[:, :, 0, :], sn, ALU.mult, rk, ["ropeB"])
            dv = dst.rearrange("p (g h d) -> p g h d", g=ngrp, h=2)
            tt(dv[:, :, 0, :], A[:, :, 0, :], B[:, :, 0, :], ALU.subtract, ["ropeA", "ropeB"], dst_keys)
            tt(dv[:, :, 1, :], A[:, :, 1, :], B[:, :, 1, :], ALU.add, ["ropeA", "ropeB"], dst_keys)

        def layernorm_tok(src, src_keys, dst, dst_keys, gi, bi):
            S.op("dve", lambda e: e.bn_stats(out=st6[:, :], in_=src), src_keys, ["st6"])
            S.op("dve", lambda e: e.bn_aggr(out=mv[:, :], in_=st6[:, :]), ["st6"], ["mv"])
            ts(lrs[:, :], mv[:, 1:2], 1e-5, None, ALU.add, None, ["mv"], ["lrs"])
            act(lrs[:, :], lrs[:, :], AF.Sqrt, ["lrs"], ["lrs"])
            S.op("dve", lambda e: e.reciprocal(out=lrs[:, :], in_=lrs[:, :]), ["lrs"], ["lrs"])
            ts(vn32[:, :], src, mv[:, 0:1], lrs[:, 0:1], ALU.subtract, ALU.mult, list(src_keys) + ["mv", "lrs"], ["vn32"])
            tt(vn32[:, :], vn32[:, :], lnB[:, gi, :], ALU.mult, ["vn32", "lnB"], ["vn32"])
            tt(dst, vn32[:, :], lnB[:, bi, :], ALU.add, ["vn32", "lnB"], dst_keys)

        try:
          stage(1)
          for l in range(DEPTH):
              x_src = x_in if l == 0 else xs
              last = (l == DEPTH - 1)
              pk = ("lp", l)
              dma("sp", gT[:, 0, :], P["g_mix"][l].rearrange("(k p) -> p k", p=128), [], ["gT"], "lp0", slow=True)
              dma("sp", gT[:, 1, :], P["g_ffn"][l].rearrange("(k p) -> p k", p=128), [], ["gT"], "lp0", slow=True)
              for cc in range(2):
                  dma("sp", wcaT[:, cc, :], P["w_conv_a"][l][:, cc * 128:(cc + 1) * 128].rearrange("t p -> p t"),
                      [], ["wcaT"], "lp1", slow=True)
                  dma("sp", wcfT[:, cc, :], P["w_conf"][l][:, cc * 128:(cc + 1) * 128].rearrange("t p -> p t"),
                      [], ["wcfT"], "lp2", slow=True)
              dma("sp", bsT[:, :], P["b_s"][l].rearrange("h t -> t h"), [], ["bsT"], "lp3", slow=True)
              for i, nm in enumerate(["gmlp_ln_g", "gmlp_ln_b", "conf_ln_g", "conf_ln_b", "b_conf"]):
                  dma("sp", lnB[:, i, :], P[nm][l].partition_broadcast(128), [], ["lnB"], "lp4")
              dma("sp", wsf[:, :, :], P["w_s"][l].rearrange("h t s -> t h s"), [], ["qk32"], "lp5")
              for h in range(4):
                  tt(wsm[:, h, :], wsf[:, h, :], tril[:, :], ALU.mult, ["qk32", "tril"], ["wsm"])
              transpose2([wsm[:, h, :] for h in range(4)], ["wsm"], wsT[:, :, :], ["wsT"])
              for cc in range(2):
                  for tp in range(31):
                      ts(diag[:, cc, tp, :], identb[:, :], wcfT[:, cc, tp:tp + 1], None, ALU.mult, None,
                         ["identb", "wcfT"], ["diag"])
              stage(2)
              S.op("dve", lambda e: e.memset(ch[:, :, 0:2], 0.0), [], ["ch"])
              S.op("dve", lambda e: e.memset(dT[:, :, 0:32], 0.0), [], ["dT"])

              def gen_front(t):
                  dma("pool", cosT[:, :, :], c_cos[:, t * 4:(t + 1) * 4, :], [], ["cosT"], "c3")
                  dma("pool", sinT[:, :, :], c_sin[:, t * 4:(t + 1) * 4, :], [], ["sinT"], "c4")
                  S.op("dve", lambda e: e.memset(ssq2[:, :], 0.0), [], [("ssq2", i) for i in range(4)])
                  for s in range(4):
                      r0 = (t * 4 + s) * 128
                      dma("pool", xstg, x_src[r0:r0 + 128, :], ["xs"] if l > 0 else [], RB_ALL, "xstg")
                      act(junk[:, :], xstg, AF.Square, RB_ALL, ["ropeA", ("ssq2", s)], accum_out=ssq2[:, s:s + 1])
                      ts(rstd2[:, s:s + 1], ssq2[:, s:s + 1], 1.0 / D, 1e-6, ALU.mult, ALU.add, [("ssq2", s)], [("rstd2", s)])
                      act(rstd2[:, s:s + 1], rstd2[:, s:s + 1], AF.Sqrt, [("rstd2", s)], [("rstd2", s)])
                      S.op("dve", lambda e, s=s: e.reciprocal(out=rstd2[:, s:s + 1], in_=rstd2[:, s:s + 1]),
                           [("rstd2", s)], [("rstd2", s)])
                      act(hn[:, s, :], xstg, AF.Copy, RB_ALL + [("rstd2", s)], [("hy", s)], scale=rstd2[:, s:s + 1])
                      yield
                  for kc in range(8):
                      yield "sync"
                      i = tcount[0] % 2
                      tcount[0] += 1
                      for s in range(4):
                          tr(pT[i][:, s, :], hn[:, s, kc * 128:(kc + 1) * 128], [("hy", s)], [("pT", i)])
                      src = pT[i][:, 0:4, :].rearrange("p s t -> p (s t)")
                      if kc % 2 == 0:
                          ts(hT[:, kc, :], src, gT[:, 0, kc:kc + 1], None, ALU.mult, None, [("pT", i), "gT"], [("hT", kc)])
                      else:
                          act(hT[:, kc, :], src, AF.Copy, [("pT", i), "gT"], [("hT", kc)], scale=gT[:, 0, kc:kc + 1])
                      yield
                  slot, W = load_chunk(l, 0, group="front")
                  for m in range(4):
                      pb, pk = nb()
                      for kc in range(8):
                          mm(pb[:, :], W[:, kc, m * 128:(m + 1) * 128], hT[:, kc, :], kc == 0, kc == 7,
                             [("wr", slot)] + HT_ALL, [pk])
                      dst = bT if m < 2 else cT
                      act(dst[:, m % 2, :], pb[:, :], AF.Copy, [pk], ["bT" if m < 2 else "cT"])
                      yield
                  slot, W = load_chunk(l, 1, group="front")
                  for m in range(4):
                      pb, pk = nb()
                      for kc in range(8):
                          mm(pb[:, :], W[:, kc, m * 128:(m + 1) * 128], hT[:, kc, :], kc == 0, kc == 7,
                             [("wr", slot)] + HT_ALL, [pk])
                      if m < 2:
                          tt(ch[:, m, 2:514], pb[:, :], cT[:, m, :], ALU.mult, [pk, "cT"], ["ch"])
                      else:
                          act(daT[:, m - 2, :], pb[:, :], AF.Copy, [pk], ["daT"])
                      yield
                  for cc in range(2):
                      ts(accA[:, :], ch[:, cc, 2:514], wcaT[:, cc, 2:3], None, ALU.mult, None, ["ch", "wcaT"], ["accA"])
                      stt(accA[:, :], ch[:, cc, 1:513], wcaT[:, cc, 1:2], accA[:, :], ALU.mult, ALU.add,
                          ["ch", "wcaT", "accA"], ["accA"])
                      stt(accA[:, :], ch[:, cc, 0:512], wcaT[:, cc, 0:1], accA[:, :], ALU.mult, ALU.add,
                          ["ch", "wcaT", "accA"], ["accA"])
                      tt(yT[:, cc, :], accA[:, :], bT[:, cc, :], ALU.mult, ["accA", "bT"], [("hy", 0)])
                      yield
                  cp(ch[:, :, 0:2], ch[:, :, 512:514], ["ch"], ["ch"])
                  stage(4)
                  slot, W = load_chunk(l, 2, group="front")
                  for m in range(2):
                      pb, pk = nb()
                      for kc in range(8):
                          mm(pb[:, :], W[:, kc, m * 128:(m + 1) * 128], hT[:, kc, :], kc == 0, kc == 7,
                             [("wr", slot)] + HT_ALL, [pk])
                      act(sig[:, :], pb[:, :], AF.Sigmoid, [pk], ["sig"])
                      tt(dT[:, m, 32:544], daT[:, m, :], sig[:, :], ALU.mult, ["daT", "sig"], ["dT"])
                      yield
                  for s in range(4):
                      pb, pk = nb()
                      for kc in range(8):
                          mm(pb[:, 0:68], hT[:, kc, s * 128:(s + 1) * 128], W[:, kc, 256:324], kc == 0, kc == 7,
                             [("wr", slot)] + HT_ALL, [pk])
                      cp(kraw[:, s, :], pb[:, 0:64], [pk], ["kraw"])
                      cp(wraw[:, s, :], pb[:, 64:68], [pk], ["wraw"])
                      yield
                  stage(4.3)
                  ts(sgn[:, :, :], wraw[:, :, :], 0.0, 2.0, ALU.is_ge, ALU.mult, ["wraw"], ["sgn"])
                  ts(sgn[:, :, :], sgn[:, :, :], -1.0, None, ALU.add, None, ["sgn"], ["sgn"])
                  stt(wabs[:, :, :], wraw[:, :, :], 1.0 / 16.0, sgn[:, :, :], ALU.mult, ALU.mult, ["wraw", "sgn"], ["wabs"])
                  slot, W = load_chunk(l, 5, group="front")
                  for s in range(4):
                      blk = t * 4 + s
                      pb, pk = nb()
                      for kc in range(8):
                          mm(pb[:, :], hT[:, kc, s * 128:(s + 1) * 128], W[:, kc, :], kc == 0, kc == 7,
                             [("wr", slot)] + HT_ALL, [pk])
                      for h in range(4):
                          act(V[:, blk, h * 65:h * 65 + 64], pb[:, h * 64:(h + 1) * 64], AF.Copy,
                              [pk], ["V"])
                      stage(6.5)
                      yield
                      act(qk32[:, 0:256], pb[:, 256:512], AF.Copy, [pk], ["qk32"])
                      rope(qi32[:, :], qk32[:, 0:256], ["qk32"], 4, blk, ["qi32"])
                      for h in range(4):
                          ts(qib[:, h * 64:(h + 1) * 64], qi32[:, h * 64:(h + 1) * 64], wabs[:, s, h:h + 1], None,
                             ALU.mult, None, ["qi32", "wabs"], ["qib"])
                      stage(6.7)
                      yield
                      yield "sync"
                      transpose2([qib[:, 0:128], qib[:, 128:256]], ["qib"],
                                 qiT_all[:, :, s * 128:(s + 1) * 128], ["qiT_all"], evac="act")
                      stage(6.8)
                      rope(ki32[:, :], kraw[:, s, :], ["kraw"], 1, blk, ["ki32"])
                      stage(6.9)
                      yield
                      cp(kib[:, 0:64], ki32[:, :], ["ki32"], ["kib"])
                      cp(kib[:, 64:128], ki32[:, :], ["ki32"], ["kib"])
                      stage(6.91 + 0.02 * s)
                      yield "sync"
                      transpose2([kib[:, :]], ["kib"], kiT2[:, blk * 128:(blk + 1) * 128], ["kiT2"], evac="act")
                      stage(6.92 + 0.02 * s)
                      yield

              def gen_gateup(t):
                  for j in range(11):
                      slw, Wgu = load_chunk(l, 8 + j, group="ffn")
                      for m in range(2):
                          pg = next_pm()
                          pu = next_pm()
                          for kc in range(8):
                              mm(pM[pg][:, :], Wgu[:, kc, m * 128:(m + 1) * 128], hTb[:, kc, :], kc == 0, kc == 7,
                                 [("wr", slw)] + HTB_ALL, [K_ps(pg)])
                          for kc in range(8):
                              mm(pM[pu][:, :], Wgu[:, kc, 256 + m * 128:256 + (m + 1) * 128], hTb[:, kc, :], kc == 0, kc == 7,
                                 [("wr", slw)] + HTB_ALL, [K_ps(pu)])
                          ai = j * 2 + m
                          sgi = ai % 2
                          act(sg[sgi][:, :], pM[pg][:, :], AF.Silu, [K_ps(pg)], [("sg", sgi)])
                          tt(aT[:, ai, :], pM[pu][:, :], sg[sgi][:, :], ALU.mult, [K_ps(pu), ("sg", sgi)],
                             [("aT", ai), "score" if ai < 16 else "mb"])
                          yield

              def gen_down(t):
                  AT_ALL = [("aT", i) for i in range(22)] + ["score", "mb"]
                  for cg in range(2):
                      pis = [next_pm() for _ in range(4)]
                      for g in range(3):
                          nkc = 8 if g < 2 else 6
                          slot, W = load_chunk(l, 19 + cg * 3 + g, nkc, group="ffn")
                          for s in range(4):
                              for kc in range(nkc):
                                  kk = g * 8 + kc
                                  mm(pM[pis[s]][:, :], aT[:, kk, s * 128:(s + 1) * 128], W[:, kc, :],
                                     kk == 0, kk == 21, [("wr", slot)] + AT_ALL, [K_ps(pis[s])])
                          yield
                      for s in range(4):
                          tt(xt[:, s, cg * 512:(cg + 1) * 512], pM[pis[s]][:, :], xt[:, s, cg * 512:(cg + 1) * 512],
                             ALU.add, [K_ps(pis[s]), ("xt", s)], [("xt", s)])
                      yield
                  stage(10)
                  if last:
                      S.op("dve", lambda e: e.memset(ssq[:, :], 0.0), [], [("ssq", i) for i in range(4)])
                      for s in range(4):
                          act(junk[:, :], xt[:, s, :], AF.Square, [("xt", s)], ["ropeA", ("ssq", s)],
                              accum_out=ssq[:, s:s + 1])
                      for s in range(4):
                          ts(rstd[:, s:s + 1], ssq[:, s:s + 1], 1.0 / D, 1e-6, ALU.mult, ALU.add, [("ssq", s)], [("rstd", s)])
                          act(rstd[:, s:s + 1], rstd[:, s:s + 1], AF.Sqrt, [("rstd", s)], [("rstd", s)])
                          S.op("dve", lambda e, s=s: e.reciprocal(out=rstd[:, s:s + 1], in_=rstd[:, s:s + 1]),
                               [("rstd", s)], [("rstd", s)])
                          stt(xt[:, s, :], xt[:, s, :], rstd[:, s:s + 1], gfinB[:, :], ALU.mult, ALU.mult,
                              [("xt", s), ("rstd", s), "gfinB"], [("xt", s)])
                      dma("pool", out[t * 512:(t + 1) * 512, :].rearrange("(s p) d -> p s d", p=128), xt[:, :, :],
                          XT_ALL, ["out"], "ost")
                  else:
                      dma("pool", xs[t * 512:(t + 1) * 512, :].rearrange("(s p) d -> p s d", p=128), xt[:, :, :],
                          XT_ALL, ["xs"], "ost")

              def load_xt(t):
                  dma("pool", xt[:, :, :], x_src[t * 512:(t + 1) * 512, :].rearrange("(s p) d -> p s d", p=128),
                      ["xs"] if l > 0 else [], XT_ALL, "xt")

              gfr0 = gen_front(0)
              while run(gfr0):
                  pass
              load_xt(0)
              for t in range(NTILE):
                  def gen_filler():
                      stage(4.5)
                      cbanks = [(pD, "pD"), (pO, "pO"), (pM[2], K_ps(2)), (pM[3], K_ps(3))]
                      for s in range(4):
                          cbk, cky = cbanks[s]
                          for cc in range(2):
                              for tp in range(31):
                                  mm(cbk[:, cc * 128:(cc + 1) * 128], dT[:, cc, 2 + s * 128 + tp: 2 + s * 128 + tp + 128],
                                     diag[:, cc, tp, :], tp == 0, tp == 30, ["dT", "diag"], [cky])
                          yield
                      for s in range(4):
                          cbk, cky = cbanks[s]
                          tt(t1[:, :], cbk[:, 0:256], lnB[:, 4, :], ALU.add, [cky, "lnB"], ["t1"])
                          stage(4.75)
                          layernorm_tok(t1[:, :], ["t1"], t1[:, :], ["t1"], 2, 3)
                          yield
                          stage(4.8)
                          act(yd[:, :], t1[:, :], AF.Silu, ["t1"], ["yd"])
                          stage(4.85)
                          transpose2([yd[:, 0:128], yd[:, 128:256]], ["yd"],
                                     yT[:, 6:8, s * 128:(s + 1) * 128], [("hy", 3)], evac="act")
                          yield
                          stage(4.9 + 0.01 * s)
                      if os.environ.get("KHALO", "1") == "1":
                          cp(dT[:, :, 0:32], dT[:, :, 512:544], ["dT"], ["dT"])
                      elif os.environ.get("KHALO") == "2":
                          for cc in range(2):
                              act(dT[:, cc, 0:32], dT[:, cc, 512:544], AF.Copy, ["dT"], ["dT"])
                      stage(5)
                      slot, W = load_chunk(l, 3, group="front")
                      for s in range(4):
                          pi = next_pm()
                          for kc in range(8):
                              mm(pM[pi][:, :], hT[:, kc, s * 128:(s + 1) * 128], W[:, kc, :], kc == 0, kc == 7,
                                 [("wr", slot)] + HT_ALL, [K_ps(pi)])
                          act(ub[:, :], pM[pi][:, 0:256], AF.Copy, [K_ps(pi)], ["ub"])
                          act(v32[:, :], pM[pi][:, 256:512], AF.Copy, [K_ps(pi)], ["v32"])
                          layernorm_tok(v32[:, :], ["v32"], vnb[:, :], ["vnb"], 0, 1)
                          yield
                          stage(5.3)
                          gbk, gky = (pD, "pD") if s % 2 == 0 else (pO, "pO")
                          for h in range(4):
                              mm(gbk[:, h * 64:(h + 1) * 64], wsT[:, h, :], vnb[:, h * 64:(h + 1) * 64], True, True,
                                 ["wsT", "vnb"], [gky])
                          for h in range(4):
                              stt(yb[:, h * 64:(h + 1) * 64], gbk[:, h * 64:(h + 1) * 64], bsT[:, h:h + 1],
                                  ub[:, h * 64:(h + 1) * 64], ALU.add, ALU.mult, [gky, "bsT", "ub"], ["yb"])
                          transpose2([yb[:, 0:128], yb[:, 128:256]], ["yb"],
                                     yT[:, 2:4, s * 128:(s + 1) * 128], [("hy", 1)], evac="act")
                          yield
                      stage(6)
                      slot, W = load_chunk(l, 4, group="front")
                      for s in range(4):
                          blk = t * 4 + s
                          pi = next_pm()
                          for kc in range(8):
                              mm(pM[pi][:, :], hT[:, kc, s * 128:(s + 1) * 128], W[:, kc, :], kc == 0, kc == 7,
                                 [("wr", slot)] + HT_ALL, [K_ps(pi)])
                          act(qk32[:, :], pM[pi][:, :], AF.Copy, [K_ps(pi)], ["qk32"])
                          rope(qkr[:, :], qk32[:, :], ["qk32"], 8, blk, ["qkr"])
                          yield
                          i_ = tcount[0] % 2
                          tcount[0] += 1
                          tr(pT[i_][:, 0, :], qkr[:, 0:128], ["qkr"], [("pT", i_)])
                          tr(pT[i_][:, 1, :], qkr[:, 128:256], ["qkr"], [("pT", i_)])
                          act(qbd[0:64, :, s, 0:128], pT[i_][0:64, 0:2, :], AF.Copy, [("pT", i_)], ["qbd"])
                          act(qbd[64:128, :, s, 128:256], pT[i_][64:128, 0:2, :], AF.Copy, [("pT", i_)], ["qbd"])
                          transpose2([qkr[:, 256:384], qkr[:, 384:512]], ["qkr"],
                                     kT[:, :, blk * 128:(blk + 1) * 128], ["kT"], evac="act")
                          yield
                      return

                  stage(7)
                  I4f = I4[:, 0:2, :].rearrange("p h q -> p (h q)")

                  def gen_index(s):
                      qb = t * 4 + s
                      nk = (qb + 1) * 128
                      sc_, sk_ = SC[s % 2]
                      for j in range(4):
                          act(dsgn[:, j, :], identb[:, :], AF.Copy, ["identb", "sgn"], ["dsgn"], scale=sgn[:, s, j:j + 1])
                      nch = (nk + 511) // 512
                      for c in range(nch):
                          k0 = c * 512
                          n = min(512, nk - k0)
                          xb = [(pM[2], K_ps(2)), (pM[3], K_ps(3)), (pTf[0], ("pT", 0)), (pTf[1], ("pT", 1))]
                          for j in range(4):
                              base = (j % 2) * 64
                              mm(xb[j][0][:, 0:n], qiT_all[base:base + 64, j // 2, s * 128:(s + 1) * 128],
                                 kiT2[base:base + 64, k0:k0 + n], True, True, ["qiT_all", "kiT2"], [xb[j][1]])
                          for j in range(4):
                              act(rbuf[:, j, 0:n], xb[j][0][:, 0:n], AF.Relu, [xb[j][1]], [("rbuf", j)])
                          yield
                          for j in range(4):
                              mm(pD[:, 0:n], dsgn[:, j, :], rbuf[:, j, 0:n], j == 0, j == 3, ["dsgn", ("rbuf", j)], ["pD"])
                          act(sc_[:, k0:k0 + n], pD[:, 0:n], AF.Copy, ["pD"], sk_)
                          yield

                  def emit_bisect(s):
                      qb = t * 4 + s
                      nk = (qb + 1) * 128
                      score_t, skey = SC[s % 2]
                      jko = jk[:, 0:1].to_broadcast([128, nk])
                      if qb >= 2:
                          S.op("dve", lambda e, nk=nk: e.reduce_max(out=amax[:, :], in_=score_t[:, 0:nk], axis=AX.X,
                                                                   apply_absolute_value=True), skey, ["amax"])
                      tt(score_t[:, nk - 128:nk], score_t[:, nk - 128:nk], cb[:, :], ALU.add, skey + ["cb"], skey)
                      if qb >= 2:
                          ts(Wt[:, :], pow2[:, :], amax[:, 0:1], None, ALU.mult, None, ["pow2", "amax"], ["Wt"])
                          S.op("dve", lambda e: e.memset(cntb[:, :], 0.0), [], ["cntb"])
                          stt(mid[1][:, :], amax[:, :], -1.0, Wt[:, 1:2], ALU.mult, ALU.add, ["amax", "Wt"], ["mid1"])
                          for k in range(1, NBIS + 1):
                              mc = mid[k % 2]
                              mn = mid[(k + 1) % 2]
                              ts(jko, score_t[:, 0:nk], mc[:, 0:1], 0.0, ALU.is_ge, ALU.add,
                                 skey + ["mid%d" % (k % 2)], ["jk", "cntb"], accum_out=cntb[:, k:k + 1])
                              stt(tq[:, :], cntb[:, k:k + 1], 255.5, Wt[:, k:k + 1], ALU.is_ge, ALU.mult,
                                  ["cntb", "Wt"], ["tq"])
                              if k < NBIS:
                                  stt(mn[:, :], tq[:, :], Wt[:, k + 1:k + 2], mc[:, :], ALU.subtract, ALU.add,
                                      ["tq", "Wt", "mid%d" % (k % 2)], ["mid%d" % ((k + 1) % 2)])
                              else:
                                  stt(thr[:, :], tq[:, :], Wt[:, k:k + 1], mc[:, :], ALU.subtract, ALU.add,
                                      ["tq", "Wt", "mid%d" % (k % 2)], ["thr"])
                      else:
                          ts(thr[:, :], pow2[:, 0:1], 0.0, -1e29, ALU.mult, ALU.add, ["pow2"], ["thr"])

                  def emit_bisect_act(s, gf):
                      qb = t * 4 + s
                      nk = (qb + 1) * 128
                      score_t, skey = SC[s % 2]
                      S.op("dve", lambda e, nk=nk: e.reduce_max(out=amax[:, :], in_=score_t[:, 0:nk], axis=AX.X,
                                                               apply_absolute_value=True), skey, ["amax"])
                      tt(score_t[:, nk - 128:nk], score_t[:, nk - 128:nk], cb[:, :], ALU.add, skey + ["cb"], skey)
                      ts(Wt[:, :], pow2[:, :], amax[:, 0:1], None, ALU.mult, None, ["pow2", "amax"], ["Wt"])
                      S.op("dve", lambda e: e.memset(cntb[:, :], 0.0), [], ["cntb"])
                      stt(mid[1][:, :], amax[:, :], -1.0, Wt[:, 1:2], ALU.mult, ALU.add, ["amax", "Wt"], ["mid1"])
                      for k in range(1, NBIS + 1):
                          mc = mid[k % 2]
                          mn = mid[(k + 1) % 2]
                          S.op("act", lambda e, k=k, mc=mc, nk=nk: e.activation(
                              out=mb[:, 0:nk], in_=score_t[:, 0:nk], func=AF.Sign, bias=mc[:, 0:1], scale=-1.0,
                              accum_out=cntb[:, k:k + 1]), skey + ["mid%d" % (k % 2)], ["mb", "cntb"])
                          run(gf, 2)
                          stt(tq[:, :], cntb[:, k:k + 1], float(nk) - 511.0, Wt[:, k:k + 1], ALU.is_le, ALU.mult,
                              ["cntb", "Wt"], ["tq"])
                          if k < NBIS:
                              stt(mn[:, :], tq[:, :], Wt[:, k + 1:k + 2], mc[:, :], ALU.subtract, ALU.add,
                                  ["tq", "Wt", "mid%d" % (k % 2)], ["mid%d" % ((k + 1) % 2)])
                          else:
                              stt(thr[:, :], tq[:, :], Wt[:, k:k + 1], mc[:, :], ALU.subtract, ALU.add,
                                  ["tq", "Wt", "mid%d" % (k % 2)], ["thr"])

                  def emit_mask(s):
                      nk = (t * 4 + s + 1) * 128
                      score_t, skey = SC[s % 2]
                      ts(mb[:, 0:nk], score_t[:, 0:nk], thr[:, 0:1], -1.0, ALU.is_ge, ALU.add,
                         skey + ["thr"], ["mb"])

                  def gen_attn(s):
                      qb = t * 4 + s

                      def st_mm(kb):
                          pi = kb % 2
                          for p in range(2):
                              mm(pM[pi][:, p * 256:(p + 1) * 256], kT[:, p, kb * 128:(kb + 1) * 128], qbd[:, p, s, :],
                                 True, False, ["kT", "qbd"], [K_ps(pi)])
                              mm(pM[pi][:, p * 256:(p + 1) * 256], mb[:, kb * 128:(kb + 1) * 128], I4f,
                                 False, True, ["mb", "I4"], [K_ps(pi)])
                      st_mm(0)
                      for kb in range(qb + 1):
                          pi = kb % 2
                          if kb + 1 <= qb:
                              st_mm(kb + 1)
                          pt = PT[kb % 2]
                          act(pt[:, :], pM[pi][:, :], AF.Exp, [K_ps(pi)], [("PT", kb % 2)], scale=0.125)
                          for h in range(4):
                              mm(pO[:, h * 65:(h + 1) * 65], pt[:, h * 128:(h + 1) * 128], V[:, kb, h * 65:(h + 1) * 65],
                                 (kb == 0 and h == 0), kb == qb, [("PT", kb % 2), "V"], ["pO"])
                          yield

                  def emit_final(s):
                      act(o32[:, :], pO[:, 0:260], AF.Copy, ["pO"], ["o32"])
                      pOv = o32[:, :].rearrange("p (h e) -> p h e", h=4)
                      S.op("dve", lambda e, pOv=pOv: e.reciprocal(out=rec[:, :], in_=pOv[:, :, 64]), ["o32"], ["rec"])
                      for h in range(4):
                          ts(yc[:, h * 64:(h + 1) * 64], o32[:, h * 65:h * 65 + 64], rec[:, h:h + 1], None,
                             ALU.mult, None, ["o32", "rec"], ["yc"])
                      transpose2([yc[:, 0:128], yc[:, 128:256]], ["yc"],
                                 yT[:, 4:6, s * 128:(s + 1) * 128], [("hy", 2)], evac="act")

                  def run(g, n=1):
                      for _ in range(n):
                          try:
                              next(g)
                          except StopIteration:
                              return False
                      return True

                  g0 = gen_index(0)
                  while run(g0):
                      pass
                  g1 = gen_index(1)
                  while run(g1):
                      pass
                  gf = gen_filler()
                  if t * 4 >= 2:
                      emit_bisect_act(0, gf)
                  else:
                      emit_bisect(0)
                  while run(gf):
                      pass
                  emit_mask(0)
                  for s in range(4):
                      if s < 3:
                          emit_bisect(s + 1)
                      ga = gen_attn(s)
                      if s < 2:
                          gi = gen_index(s + 2)
                          alive = True
                          while alive:
                              alive = run(gi)
                              run(ga)
                      while run(ga):
                          pass
                      emit_final(s)
                      if s < 3:
                          emit_mask(s + 1)
                  if dbg and l == 0 and t == 0:
                      tap("yT", yT[:, :, :], [128, 8, 512], HY_ALL)
                  stage(8)
                  for cg in range(2):
                      slot, W = load_chunk(l, 6 + cg)
                      for s in range(4):
                          pi = next_pm()
                          for kc in range(8):
                              mm(pM[pi][:, :], yT[:, kc, s * 128:(s + 1) * 128], W[:, kc, :], kc == 0, kc == 7,
                                 [("wr", slot)] + HY_ALL, [K_ps(pi)])
                          tt(xt[:, s, cg * 512:(cg + 1) * 512], pM[pi][:, :], xt[:, s, cg * 512:(cg + 1) * 512], ALU.add,
                             [K_ps(pi), ("xt", s)], [("xt", s)])
                  if dbg and l == 0 and t == 0:
                      tap("xmid", xt[:, :, :], [128, 4, 1024], XT_ALL)
                  stage(9)
                  rmsnorm_to_hT(1)
                  AT_ALL = [("aT", i) for i in range(22)] + ["score", "mb"]
                  def gen_ffn(t):
                      yield from gen_gateup(t)
                      yield from gen_down(t)
                  gd = gen_ffn(t)
                  if t + 1 < NTILE:
                      gfr = gen_front(t + 1)
                      a1 = a2_ = True
                      while a1 or a2_:
                          if a1:
                              a1 = run(gd)
                          k_ = 0
                          while a2_ and k_ < 3:
                              try:
                                  r_ = next(gfr)
                              except StopIteration:
                                  a2_ = False
                                  break
                              k_ += 1
                              if r_ == "sync" and a1:
                                  break
                      load_xt(t + 1)
                  else:
                      while run(gd):
                          pass
        except _Stop:
            dma("pool", out[0:512, :].rearrange("(s p) d -> p s d", p=128), xt[:, :, :],
                [("xt", s) for s in range(4)], ["out"], "ost")
        S.final_wait("pool", ["out"] + ["dbg_" + k for k in dbg_outs])
        S.final_wait("sp", ["out"] + ["dbg_" + k for k in dbg_outs])
        S.emit()
    return nc


def make_consts(SEQ):
    NB = SEQ // 128
    f32 = np.float32
    ident = np.eye(128, dtype=f32)
    tril = np.tril(np.ones((128, 128), dtype=f32))
    cbm = np.where(np.arange(128)[None, :] <= np.arange(128)[:, None], 0.0, -1e30).astype(f32)
    inv_freq = (f32(10000.0) ** (-(np.arange(0, 64, 2, dtype=f32)) / f32(64))).astype(f32)
    ang = (np.arange(SEQ, dtype=f32)[:, None] * inv_freq[None, :]).astype(f32)
    cos = np.cos(ang).astype(f32).reshape(NB, 128, 32).transpose(1, 0, 2)
    sin = np.sin(ang).astype(f32).reshape(NB, 128, 32).transpose(1, 0, 2)
    pw = (2.002 * 2.0 ** (-np.arange(NBIS + 2, dtype=np.float64))).astype(f32)
    pow2 = np.broadcast_to(pw[None, :], (128, NBIS + 2))
    return {"c_ident": ident, "c_tril": tril, "c_cb": cbm, "c_cos": np.ascontiguousarray(cos),
            "c_sin": np.ascontiguousarray(sin), "c_pow2": np.ascontiguousarray(pow2)}


_NC_CACHE = {}


def kernel(**inputs):
    x = np.asarray(inputs["x"], dtype=np.float32)
    B, SEQ, _ = x.shape
    DEPTH = inputs["w_in"].shape[0]
    key = (SEQ, DEPTH)
    if key not in _NC_CACHE:
        _NC_CACHE[key] = build(SEQ, DEPTH)
    nc = _NC_CACHE[key]
    consts = make_consts(SEQ)
    params = {k: np.ascontiguousarray(np.asarray(v, dtype=np.float32)) for k, v in inputs.items() if k != "x"}
    n = 8
    in_maps = []
    for c in range(n):
        m = {"x": np.ascontiguousarray(x[c % B])}
        m.update(params)
        m.update(consts)
        in_maps.append(m)
    res = run_bass_kernel_spmd(nc, in_maps, core_ids=list(range(n)))
    outp = np.stack([np.asarray(res.results[b]["out"], dtype=np.float32) for b in range(B)], axis=0)
    return outp
```

```python
import numpy as np
from contextlib import ExitStack
import concourse.bass as bass
import concourse.mybir as mybir
from concourse.bass_utils import run_bass_kernel_spmd

F32 = mybir.dt.float32
BF16 = mybir.dt.bfloat16
ALU = mybir.AluOpType
AF = mybir.ActivationFunctionType
AX = mybir.AxisListType

D = 1024
G = 256
HID = 2816
INC = 2884
NBIS = 15
ENGS = ("pe", "act", "dve", "pool", "sp")


class Sched:
    def __init__(self, nc, stack):
        self.nc = nc
        self.stack = stack
        self.q = {e: [] for e in ENGS}
        self.sem = {e: stack.enter_context(nc.semaphore("sem_" + e)) for e in ENGS}
        self.cnt = {e: 0 for e in ENGS}
        self.waited = {e: {} for e in ENGS}
        self.last_write = {}
        self.readers = {}
        self.dsem = {}
        self.dcnt = {}

    def _dsem(self, key):
        if key not in self.dsem:
            self.dsem[key] = self.stack.enter_context(self.nc.semaphore("dsem_%d" % len(self.dsem)))
            self.dcnt[key] = 0
        return self.dsem[key]

    def op(self, eng, fn, reads=(), writes=(), dma=None):
        deps = []
        for r in reads:
            t = self.last_write.get(r)
            if t is not None:
                deps.append(t)
        for w in writes:
            t = self.last_write.get(w)
            if t is not None:
                deps.append(t)
            deps.extend(self.readers.get(w, ()))
        wd = self.waited[eng]
        m = {}
        for (sem, val, deng) in deps:
            if deng == "pe" and eng == "pe" and dma is None:
                continue
            k = id(sem)
            if wd.get(k, 0) >= val:
                continue
            if k not in m or m[k][1] < val:
                m[k] = (sem, val)
        for k, (sem, val) in m.items():
            wd[k] = val
        waits = list(m.values())
        if dma is not None:
            sem = self._dsem(dma)
            self.dcnt[dma] += 1
            tok = (sem, 16 * self.dcnt[dma], "dma")
            inc = 16
        else:
            self.cnt[eng] += 1
            tok = (self.sem[eng], self.cnt[eng], eng)
            inc = 1
        self.q[eng].append((waits, fn, tok[0], inc))
        for w in writes:
            self.last_write[w] = tok
            self.readers[w] = []
        for r in reads:
            if r in writes:
                continue
            lst = self.readers.setdefault(r, [])
            lst[:] = [x for x in lst if not (x[0] is tok[0])]
            lst.append(tok)
        return tok

    def final_wait(self, eng, keys):
        waits = []
        for k in keys:
            t = self.last_write.get(k)
            if t is not None:
                waits.append((t[0], t[1]))
        self.q[eng].append((waits, None, None, 0))

    def emit(self):
        nc = self.nc
        import os
        if os.environ.get("KDUMP"):
            names = {id(v): k for k, v in self.sem.items()}
            names.update({id(v): "d:" + str(k) for k, v in self.dsem.items()})
            for e in ENGS:
                print("ENG", e, "n=", len(self.q[e]), "cnt=", self.cnt[e])
                for (waits, fn, sem, inc) in self.q[e][-6:]:
                    print("   waits", [(names.get(id(s_), "?"), v) for (s_, v) in waits], "inc", names.get(id(sem)), inc)
        with nc.Block() as block:
            def run(eng_name):
                def body(e):
                    for (waits, fn, sem, inc) in self.q[eng_name]:
                        for (s, v) in waits:
                            e.wait_ge(s, v)
                        if fn is not None:
                            fn(e).then_inc(sem, inc)
                return body
            block.tensor(run("pe"))
            block.scalar(run("act"))
            block.vector(run("dve"))
            block.gpsimd(run("pool"))
            block.sync(run("sp"))


C_B, C_C, C_H, C_U, C_V, C_Q, C_K, C_VV, C_QI, C_KI, C_WI, C_DA, C_DG = (
    0, 256, 512, 768, 1024, 1280, 1536, 1792, 2048, 2304, 2368, 2372, 2628)
def chunk_table():
    t = []
    t.append((0, 1024, [("w_in", C_B, 512, 0)]))
    t.append((0, 1024, [("w_in", C_H, 256, 0), ("w_in", C_DA, 256, 256)]))
    t.append((0, 1024, [("w_in", C_DG, 256, 0), ("w_in", C_KI, 68, 256)]))
    t.append((0, 1024, [("w_in", C_U, 512, 0)]))
    t.append((0, 1024, [("w_in", C_Q, 512, 0)]))
    t.append((0, 1024, [("w_in", C_VV, 512, 0)]))
    t.append((0, 1024, [("w_out", 0, 512, 0)]))
    t.append((0, 1024, [("w_out", 512, 512, 0)]))
    for j in range(11):
        t.append((0, 1024, [("w_gate", j * 256, 256, 0), ("w_up", j * 256, 256, 256)]))
    for cg in range(2):
        for g in range(3):
            nr = 1024 if g < 2 else 768
            t.append((g * 1024, nr, [("w_down", cg * 512, 512, 0)]))
    return t


NCHUNK = 25


class _Stop(Exception):
    pass


def build(SEQ, DEPTH, dbg=False, stop=None):
    nc = bass.Bass("TRN2", target_bir_lowering=False)

    def stage(n):
        if stop is not None and n > stop:
            raise _Stop()
    NTILE = SEQ // 512
    NB = SEQ // 128

    def din(name, shape):
        return nc.dram_tensor(name, list(shape), F32, kind="ExternalInput").ap()

    x_in = din("x", [SEQ, D])
    P = {}
    for name, shape in [("g_mix", [DEPTH, D]), ("w_in", [DEPTH, D, INC]), ("w_conv_a", [DEPTH, 3, G]),
                        ("gmlp_ln_g", [DEPTH, G]), ("gmlp_ln_b", [DEPTH, G]), ("w_s", [DEPTH, 4, 128, 128]),
                        ("b_s", [DEPTH, 4, 128]), ("w_conf", [DEPTH, 31, G]), ("b_conf", [DEPTH, G]),
                        ("conf_ln_g", [DEPTH, G]), ("conf_ln_b", [DEPTH, G]), ("w_out", [DEPTH, D, D]),
                        ("g_ffn", [DEPTH, D]), ("w_gate", [DEPTH, D, HID]), ("w_up", [DEPTH, D, HID]),
                        ("w_down", [DEPTH, HID, D]), ("g_final", [D])]:
        P[name] = din(name, shape)
    c_ident = din("c_ident", [128, 128])
    c_tril = din("c_tril", [128, 128])
    c_cb = din("c_cb", [128, 128])
    c_cos = din("c_cos", [128, NB, 32])
    c_sin = din("c_sin", [128, NB, 32])
    c_pow2 = din("c_pow2", [128, NBIS + 2])
    out = nc.dram_tensor("out", [SEQ, D], F32, kind="ExternalOutput").ap()
    wsc = nc.dram_tensor("wsc", [DEPTH, NCHUNK, 1024, 512], BF16, kind="Internal").ap()
    xs = nc.dram_tensor("xs", [SEQ, D], F32, kind="Internal").ap()
    dbg_outs = {}

    with ExitStack() as st:
        S = Sched(nc, st)

        def sb(name, shape, dt):
            return st.enter_context(nc.sbuf_tensor(name, list(shape), dt))

        def psum(name, shape, dt):
            return st.enter_context(nc.psum_tensor(name, list(shape), dt))

        wring_all = sb("wring", [128, 3, 8, 512], BF16)
        wring = [wring_all[:, i, :, :] for i in range(3)]
        xt = sb("xt", [128, 4, 1024], F32)
        hy = sb("hy", [128, 4096], BF16)
        hn = hy[:, :].rearrange("p (s d) -> p s d", s=4)
        yT = hy[:, :].rearrange("p (k t) -> p k t", k=8)
        hT = sb("hT", [128, 8, 512], BF16)
        hTb = sb("hTb", [128, 8, 512], BF16)
        bT = sb("bT", [128, 2, 512], BF16)
        cT = sb("cT", [128, 2, 512], BF16)
        ch = sb("ch", [128, 2, 514], BF16)
        accA = sb("accA", [128, 512], F32)
        daT = sb("daT", [128, 2, 512], BF16)
        sig = sb("sig", [128, 512], BF16)
        dT = sb("dT", [128, 2, 544], BF16)
        kraw = sb("kraw", [128, 4, 64], F32)
        wraw = sb("wraw", [128, 4, 4], F32)
        wabs = sb("wabs", [128, 4, 4], F32)
        sgn = sb("sgn", [128, 4, 4], F32)
        ub = sb("ub", [128, 256], BF16)
        vn32 = sb("vn32", [128, 256], F32)
        vnb = sb("vnb", [128, 256], BF16)
        v32 = sb("v32", [128, 256], F32)
        qk32 = sb("qk32", [128, 512], F32)
        o32 = sb("o32", [128, 260], F32)
        wsf = qk32[:, :].rearrange("p (h s) -> p h s", h=4)
        yb = sb("yb", [128, 256], BF16)
        ropeA = sb("ropeA", [128, 512], F32)
        ropeB = sb("ropeB", [128, 512], F32)
        junk = ropeA[:, :].bitcast(BF16)
        qkr = sb("qkr", [128, 512], BF16)
        qbd = sb("qbd", [128, 2, 4, 256], BF16)
        jk = sb("jk", [128, 4], BF16)
        qiT_all = sb("qiT_all", [128, 2, 512], BF16)
        qib = sb("qib", [128, 256], BF16)
        kib = sb("kib", [128, 128], BF16)
        dsgn = sb("dsgn", [128, 4, 128], BF16)
        kT = sb("kT", [128, 2, SEQ], BF16)
        kiT2 = sb("kiT2", [128, SEQ], BF16)
        V = sb("V", [128, NB, 260], BF16)
        big = sb("big", [128, 12288], BF16)
        rbuf = sb("rbuf", [128, 4, 512], BF16)
        PT = [sb("PT%d" % i, [128, 512], BF16) for i in range(2)]
        yc = sb("yc", [128, 256], BF16)
        rec = sb("rec", [128, 4], F32)
        t1 = sb("t1", [128, 256], F32)
        yd = sb("yd", [128, 256], BF16)
        cosT = sb("cosT", [128, 4, 32], F32)
        sinT = sb("sinT", [128, 4, 32], F32)
        sg = [sb("sg%d" % i, [128, 512], BF16) for i in range(2)]
        diag = sb("diag", [128, 2, 31, 128], BF16)
        wsT = sb("wsT", [128, 4, 128], BF16)
        wsm = sb("wsm", [128, 4, 128], BF16)
        identf = sb("identf", [128, 128], F32)
        identb = sb("identb", [128, 128], BF16)
        I4 = sb("I4", [128, 4, 128], BF16)
        tril = sb("tril", [128, 128], F32)
        cb = sb("cb", [128, 128], F32)
        pow2 = sb("pow2", [128, NBIS + 2], F32)
        lnB = sb("lnB", [128, 5, 256], F32)
        gfinB = sb("gfinB", [128, 1024], F32)
        gT = sb("gT", [128, 2, 8], F32)
        wcaT = sb("wcaT", [128, 2, 3], F32)
        wcfT = sb("wcfT", [128, 2, 31], F32)
        bsT = sb("bsT", [128, 4], F32)
        ssq = sb("ssq", [128, 4], F32)
        rstd = sb("rstd", [128, 4], F32)
        st6 = sb("st6", [128, 6], F32)
        mv = sb("mv", [128, 2], F32)
        lrs = sb("lrs", [128, 1], F32)
        amax = sb("amax", [128, 1], F32)
        Wt = sb("Wt", [128, NBIS + 2], F32)
        cntb = sb("cntb", [128, NBIS + 2], F32)
        mid = [sb("mid%d" % i, [128, 1], F32) for i in range(2)]
        tq = sb("tq", [128, 1], F32)
        thr = sb("thr", [128, 1], F32)
        amaxB = sb("amaxB", [128, 1], F32)
        WtB = sb("WtB", [128, NBIS + 2], F32)
        cntbB = sb("cntbB", [128, NBIS + 2], F32)
        midB = [sb("midB%d" % i, [128, 1], F32) for i in range(2)]
        tqB = sb("tqB", [128, 1], F32)
        thrB = sb("thrB", [128, 1], F32)
        score_t = big[:, 0:2 * SEQ].bitcast(F32)
        mb = big[:, 8192:8192 + SEQ]
        scoreB = wring_all[:, 0:2, :, :].rearrange("p a k n -> p (a k n)").bitcast(F32)[:, 0:SEQ]
        SC = [(score_t, ["score"]), (scoreB, [("wr", 0), ("wr", 1)])]
        qi32 = sb("qi32", [128, 256], F32)
        ki32 = sb("ki32", [128, 64], F32)
        aT = big[:, :].rearrange("p (k t) -> p k t", k=24)

        pT = [psum("pT%d" % i, [128, 8, 128], BF16) for i in range(2)]
        pTf = [pT[i][:, :, :].rearrange("p a b -> p (a b)").bitcast(F32) for i in range(2)]
        pM = [psum("pM%d" % i, [128, 512], F32) for i in range(4)]
        pO = psum("pO", [128, 512], F32)
        pD = psum("pD", [128, 512], F32)

        def K_ps(i):
            return ("pM", i)

        def act(out_, in_, func, r, w, **kw):
            S.op("act", lambda e: e.activation(out=out_, in_=in_, func=func, **kw), r, w)

        def tt(out_, in0, in1, op, r, w, eng="dve"):
            S.op(eng, lambda e: e.tensor_tensor(out=out_, in0=in0, in1=in1, op=op), r, w)

        def ts(out_, in0, s1, s2, op0, op1, r, w, eng="dve", **kw):
            if op1 is None:
                S.op(eng, lambda e: e.tensor_scalar(out=out_, in0=in0, scalar1=s1, scalar2=None, op0=op0, **kw), r, w)
            else:
                S.op(eng, lambda e: e.tensor_scalar(out=out_, in0=in0, scalar1=s1, scalar2=s2, op0=op0, op1=op1, **kw), r, w)

        def stt(out_, in0, scalar, in1, op0, op1, r, w, eng="dve"):
            S.op(eng, lambda e: e.scalar_tensor_tensor(out=out_, in0=in0, scalar=scalar, in1=in1, op0=op0, op1=op1), r, w)

        def cp(out_, in_, r, w, eng="dve"):
            S.op(eng, lambda e: e.tensor_copy(out=out_, in_=in_), r, w)

        def mm(out_, lhsT, rhs, start, stop, r, w):
            S.op("pe", lambda e: e.matmul(out_, lhsT=lhsT, rhs=rhs, start=start, stop=stop, skip_group_check=True), r, w)

        def tr(out_, in_, r, w):
            S.op("pe", lambda e: e.transpose(out=out_, in_=in_, identity=identb[:, :]), list(r) + ["identb"], w)

        def dma(eng, out_, in_, r, w, key, slow=False):
            if slow:
                S.op(eng, lambda e: e.dma_start(out=out_, in_=in_, allow_slow_non_contiguous=True), r, w, dma=key)
            else:
                S.op(eng, lambda e: e.dma_start(out=out_, in_=in_), r, w, dma=key)

        ntap = [0]

        def tap(name, ap, shape, rkeys):
            if not dbg:
                return
            d = nc.dram_tensor("dbg_" + name, list(shape), ap.dtype, kind="ExternalOutput").ap()
            dbg_outs[name] = d
            ntap[0] += 1
            dma("sp", d, ap, rkeys, ["dbg_" + name], ("dbg", ntap[0]))

        table = chunk_table()
        import os
        NCAST = int(os.environ.get("KNCAST", "1000"))
        for l in range(DEPTH):
            for ci, (r0, nr, cols) in enumerate(table):
                if ci >= NCAST:
                    continue
                for (src, c0, n, d0) in cols:
                    dma("pool", wsc[l, ci, 0:nr, d0:d0 + n], P[src][l, r0:r0 + nr, c0:c0 + n],
                        [], [("wsc", l, ci)], ("cast", l, ci))

        dma("sp", identf[:, :], c_ident, [], ["identf"], "c0")
        dma("sp", tril[:, :], c_tril, [], ["tril"], "c1")
        dma("sp", cb[:, :], c_cb, [], ["cb"], "c2")
        dma("sp", pow2[:, :], c_pow2, [], ["pow2"], "c5")
        dma("sp", gfinB[:, :], P["g_final"].partition_broadcast(128), [], ["gfinB"], "c6")
        cp(identb[:, :], identf[:, :], ["identf"], ["identb"])
        for h in range(4):
            ts(I4[:, h, :], identf[:, :], 30000.0, None, ALU.mult, None, ["identf"], ["I4"])
        Vv = V[:, :, :].rearrange("p b (h e) -> p b h e", h=4)
        S.op("dve", lambda e: e.memset(V[:, :, :], 1.0), [], ["V"])
        S.op("dve", lambda e: e.memset(qbd[:, :, :, :], 0.0), [], ["qbd"])

        ring_ctr = [0]

        grp_ctr = {"ffn": 0}

        def load_chunk(l, ci, nkc=8, group=None):
            if group == "ffn":
                slot = grp_ctr["ffn"] % 2
                grp_ctr["ffn"] += 1
            elif group == "front":
                slot = 2
            else:
                slot = ring_ctr[0] % 3
                ring_ctr[0] += 1
            ncol = max(d0 + n for (src_, c0, n, d0) in table[ci][2])
            src = wsc[l, ci].rearrange("(k p) n -> p k n", p=128)
            dma("sp", wring[slot][:, 0:nkc, 0:ncol], src[:, 0:nkc, 0:ncol], [("wsc", l, ci)], [("wr", slot)], ("wr", slot))
            return slot, wring[slot]

        tcount = [0]

        def transpose2(src_aps, src_keys, dst_ap, dst_keys, evac="dve"):
            i = tcount[0] % 2
            tcount[0] += 1
            n = len(src_aps)
            for j, a in enumerate(src_aps):
                tr(pT[i][:, j, :], a, src_keys, [("pT", i)])
            srcp = pT[i][:, 0:n, :] if n > 1 else pT[i][:, 0, :]
            if evac == "act":
                act(dst_ap, srcp, AF.Copy, [("pT", i)], dst_keys)
            else:
                cp(dst_ap, srcp, [("pT", i)], dst_keys)

        def rmsnorm_to_hT(gidx):
            S.op("dve", lambda e: e.memset(ssq[:, :], 0.0), [], [("ssq", i) for i in range(4)])
            for s in range(4):
                act(junk[:, :], xt[:, s, :], AF.Square, [("xt", s)], ["ropeA", ("ssq", s)], accum_out=ssq[:, s:s + 1])
            for s in range(4):
                ts(rstd[:, s:s + 1], ssq[:, s:s + 1], 1.0 / D, 1e-6, ALU.mult, ALU.add, [("ssq", s)], [("rstd", s)])
                act(rstd[:, s:s + 1], rstd[:, s:s + 1], AF.Sqrt, [("rstd", s)], [("rstd", s)])
                S.op("dve", lambda e, s=s: e.reciprocal(out=rstd[:, s:s + 1], in_=rstd[:, s:s + 1]),
                     [("rstd", s)], [("rstd", s)])
                act(hn[:, s, :], xt[:, s, :], AF.Copy, [("xt", s), ("rstd", s)], [("hy", s)], scale=rstd[:, s:s + 1])
            for kc in range(8):
                i = tcount[0] % 2
                tcount[0] += 1
                for s in range(4):
                    tr(pT[i][:, s, :], hn[:, s, kc * 128:(kc + 1) * 128], [("hy", s)], [("pT", i)])
                src = pT[i][:, 0:4, :].rearrange("p s t -> p (s t)")
                if kc % 2 == 0:
                    ts(hTb[:, kc, :], src, gT[:, gidx, kc:kc + 1], None, ALU.mult, None, [("pT", i), "gT"], [("hTb", kc)])
                else:
                    act(hTb[:, kc, :], src, AF.Copy, [("pT", i), "gT"], [("hTb", kc)], scale=gT[:, gidx, kc:kc + 1])

        def run(g, n=1):
            for _ in range(n):
                try:
                    next(g)
                except StopIteration:
                    return False
            return True

        pfc = [0]

        def nb():
            i = pfc[0] % 2
            pfc[0] += 1
            return (pD, "pD") if i == 0 else (pO, "pO")

        ssq2 = sb("ssq2", [128, 4], F32)
        rstd2 = sb("rstd2", [128, 4], F32)
        xstg = rbuf[:, :, :].rearrange("p a b -> p (a b)").bitcast(F32)
        RB_ALL = [("rbuf", j) for j in range(4)]
        XT_ALL = [("xt", s_) for s_ in range(4)]
        HT_ALL = [("hT", k) for k in range(8)]
        HTB_ALL = [("hTb", k) for k in range(8)]
        HY_ALL = [("hy", k) for k in range(4)]
        pmc = [0]

        def next_pm():
            i = pmc[0] % 4
            pmc[0] += 1
            return i

        def rope(dst, src_ps, src_keys, ngrp, blk, dst_keys):
            n = ngrp * 64
            if ngrp == 1:
                rk = list(src_keys) + ["cosT", "sinT"]
                cs = cosT[:, blk % 4, :]
                sn = sinT[:, blk % 4, :]
                tt(ropeA[:, 0:32], src_ps[:, 0:32], cs, ALU.mult, rk, ["ropeA"])
                tt(ropeA[:, 32:64], src_ps[:, 32:64], cs, ALU.mult, rk, ["ropeA"])
                tt(ropeB[:, 0:32], src_ps[:, 32:64], sn, ALU.mult, rk, ["ropeB"])
                tt(ropeB[:, 32:64], src_ps[:, 0:32], sn, ALU.mult, rk, ["ropeB"])
                tt(dst[:, 0:32], ropeA[:, 0:32], ropeB[:, 0:32], ALU.subtract, ["ropeA", "ropeB"], dst_keys)
                tt(dst[:, 32:64], ropeA[:, 32:64], ropeB[:, 32:64], ALU.add, ["ropeA", "ropeB"], dst_keys)
                return
            v = src_ps.rearrange("p (g h d) -> p g h d", g=ngrp, h=2)
            A = ropeA[:, 0:n].rearrange("p (g h d) -> p g h d", g=ngrp, h=2)
            B = ropeB[:, 0:n].rearrange("p (g h d) -> p g h d", g=ngrp, h=2)
            cs = cosT[:, blk % 4:blk % 4 + 1, :].to_broadcast([128, ngrp, 32])
            sn = sinT[:, blk % 4:blk % 4 + 1, :].to_broadcast([128, ngrp, 32])
            rk = list(src_keys) + ["cosT", "sinT"]
            tt(A[:, :, 0, :], v[:, :, 0, :], cs, ALU.mult, rk, ["ropeA"])
            tt(A[:, :, 1, :], v[:, :, 1, :], cs, ALU.mult, rk, ["ropeA"])
            tt(B[:, :, 0, :], v[:, :, 1, :], sn, ALU.mult, rk, ["ropeB"])
            tt(B[:, :, 1, :], v[:, :, 0, :], sn, ALU.mult, rk, ["ropeB"])
            dv = dst.rearrange("p (g h d) -> p g h d", g=ngrp, h=2)
            tt(dv[:, :, 0, :], A[:, :, 0, :], B[:, :, 0, :], ALU.subtract, ["ropeA", "ropeB"], dst_keys)
            tt(dv[:, :, 1, :], A[:, :, 1, :], B[:, :, 1, :], ALU.add, ["ropeA", "ropeB"], dst_keys)

        def layernorm_tok(src, src_keys, dst, dst_keys, gi, bi):
            S.op("dve", lambda e: e.bn_stats(out=st6[:, :], in_=src), src_keys, ["st6"])
            S.op("dve", lambda e: e.bn_aggr(out=mv[:, :], in_=st6[:, :]), ["st6"], ["mv"])
            ts(lrs[:, :], mv[:, 1:2], 1e-5, None, ALU.add, None, ["mv"], ["lrs"])
            act(lrs[:, :], lrs[:, :], AF.Sqrt, ["lrs"], ["lrs"])
            S.op("dve", lambda e: e.reciprocal(out=lrs[:, :], in_=lrs[:, :]), ["lrs"], ["lrs"])
            ts(vn32[:, :], src, mv[:, 0:1], lrs[:, 0:1], ALU.subtract, ALU.mult, list(src_keys) + ["mv", "lrs"], ["vn32"])
            tt(vn32[:, :], vn32[:, :], lnB[:, gi, :], ALU.mult, ["vn32", "lnB"], ["vn32"])
            tt(dst, vn32[:, :], lnB[:, bi, :], ALU.add, ["vn32", "lnB"], dst_keys)

        try:
          stage(1)
          for l in range(DEPTH):
              x_src = x_in if l == 0 else xs
              last = (l == DEPTH - 1)
              pk = ("lp", l)
              dma("sp", gT[:, 0, :], P["g_mix"][l].rearrange("(k p) -> p k", p=128), [], ["gT"], "lp0", slow=True)
              dma("sp", gT[:, 1, :], P["g_ffn"][l].rearrange("(k p) -> p k", p=128), [], ["gT"], "lp0", slow=True)
              for cc in range(2):
                  dma("sp", wcaT[:, cc, :], P["w_conv_a"][l][:, cc * 128:(cc + 1) * 128].rearrange("t p -> p t"),
                      [], ["wcaT"], "lp1", slow=True)
                  dma("sp", wcfT[:, cc, :], P["w_conf"][l][:, cc * 128:(cc + 1) * 128].rearrange("t p -> p t"),
                      [], ["wcfT"], "lp2", slow=True)
              dma("sp", bsT[:, :], P["b_s"][l].rearrange("h t -> t h"), [], ["bsT"], "lp3", slow=True)
              for i, nm in enumerate(["gmlp_ln_g", "gmlp_ln_b", "conf_ln_g", "conf_ln_b", "b_conf"]):
                  dma("sp", lnB[:, i, :], P[nm][l].partition_broadcast(128), [], ["lnB"], "lp4")
              dma("sp", wsf[:, :, :], P["w_s"][l].rearrange("h t s -> t h s"), [], ["qk32"], "lp5")
              for h in range(4):
                  tt(wsm[:, h, :], wsf[:, h, :], tril[:, :], ALU.mult, ["qk32", "tril"], ["wsm"])
              transpose2([wsm[:, h, :] for h in range(4)], ["wsm"], wsT[:, :, :], ["wsT"])
              for cc in range(2):
                  for tp in range(31):
                      ts(diag[:, cc, tp, :], identb[:, :], wcfT[:, cc, tp:tp + 1], None, ALU.mult, None,
                         ["identb", "wcfT"], ["diag"])
              stage(2)
              S.op("dve", lambda e: e.memset(ch[:, :, 0:2], 0.0), [], ["ch"])
              S.op("dve", lambda e: e.memset(dT[:, :, 0:32], 0.0), [], ["dT"])

              def gen_front(t):
                  dma("pool", cosT[:, :, :], c_cos[:, t * 4:(t + 1) * 4, :], [], ["cosT"], "c3")
                  dma("pool", sinT[:, :, :], c_sin[:, t * 4:(t + 1) * 4, :], [], ["sinT"], "c4")
                  S.op("dve", lambda e: e.memset(ssq2[:, :], 0.0), [], [("ssq2", i) for i in range(4)])
                  for s in range(4):
                      r0 = (t * 4 + s) * 128
                      dma("pool", xstg, x_src[r0:r0 + 128, :], ["xs"] if l > 0 else [], RB_ALL, "xstg")
                      act(junk[:, :], xstg, AF.Square, RB_ALL, ["ropeA", ("ssq2", s)], accum_out=ssq2[:, s:s + 1])
                      ts(rstd2[:, s:s + 1], ssq2[:, s:s + 1], 1.0 / D, 1e-6, ALU.mult, ALU.add, [("ssq2", s)], [("rstd2", s)])
                      act(rstd2[:, s:s + 1], rstd2[:, s:s + 1], AF.Sqrt, [("rstd2", s)], [("rstd2", s)])
                      S.op("dve", lambda e, s=s: e.reciprocal(out=rstd2[:, s:s + 1], in_=rstd2[:, s:s + 1]),
                           [("rstd2", s)], [("rstd2", s)])
                      act(hn[:, s, :], xstg, AF.Copy, RB_ALL + [("rstd2", s)], [("hy", s)], scale=rstd2[:, s:s + 1])
                      yield
                  for kc in range(8):
                      yield "sync"
                      i = tcount[0] % 2
                      tcount[0] += 1
                      for s in range(4):
                          tr(pT[i][:, s, :], hn[:, s, kc * 128:(kc + 1) * 128], [("hy", s)], [("pT", i)])
                      src = pT[i][:, 0:4, :].rearrange("p s t -> p (s t)")
                      if kc % 2 == 0:
                          ts(hT[:, kc, :], src, gT[:, 0, kc:kc + 1], None, ALU.mult, None, [("pT", i), "gT"], [("hT", kc)])
                      else:
                          act(hT[:, kc, :], src, AF.Copy, [("pT", i), "gT"], [("hT", kc)], scale=gT[:, 0, kc:kc + 1])
                      yield
                  slot, W = load_chunk(l, 0, group="front")
                  for m in range(4):
                      pb, pk = nb()
                      for kc in range(8):
                          mm(pb[:, :], W[:, kc, m * 128:(m + 1) * 128], hT[:, kc, :], kc == 0, kc == 7,
                             [("wr", slot)] + HT_ALL, [pk])
                      dst = bT if m < 2 else cT
                      act(dst[:, m % 2, :], pb[:, :], AF.Copy, [pk], ["bT" if m < 2 else "cT"])
                      yield
                  slot, W = load_chunk(l, 1, group="front")
                  for m in range(4):
                      pb, pk = nb()
                      for kc in range(8):
                          mm(pb[:, :], W[:, kc, m * 128:(m + 1) * 128], hT[:, kc, :], kc == 0, kc == 7,
                             [("wr", slot)] + HT_ALL, [pk])
                      if m < 2:
                          tt(ch[:, m, 2:514], pb[:, :], cT[:, m, :], ALU.mult, [pk, "cT"], ["ch"])
                      else:
                          act(daT[:, m - 2, :], pb[:, :], AF.Copy, [pk], ["daT"])
                      yield
                  for cc in range(2):
                      ts(accA[:, :], ch[:, cc, 2:514], wcaT[:, cc, 2:3], None, ALU.mult, None, ["ch", "wcaT"], ["accA"])
                      stt(accA[:, :], ch[:, cc, 1:513], wcaT[:, cc, 1:2], accA[:, :], ALU.mult, ALU.add,
                          ["ch", "wcaT", "accA"], ["accA"])
                      stt(accA[:, :], ch[:, cc, 0:512], wcaT[:, cc, 0:1], accA[:, :], ALU.mult, ALU.add,
                          ["ch", "wcaT", "accA"], ["accA"])
                      tt(yT[:, cc, :], accA[:, :], bT[:, cc, :], ALU.mult, ["accA", "bT"], [("hy", 0)])
                      yield
                  cp(ch[:, :, 0:2], ch[:, :, 512:514], ["ch"], ["ch"])
                  stage(4)
                  slot, W = load_chunk(l, 2, group="front")
                  for m in range(2):
                      pb, pk = nb()
                      for kc in range(8):
                          mm(pb[:, :], W[:, kc, m * 128:(m + 1) * 128], hT[:, kc, :], kc == 0, kc == 7,
                             [("wr", slot)] + HT_ALL, [pk])
                      act(sig[:, :], pb[:, :], AF.Sigmoid, [pk], ["sig"])
                      tt(dT[:, m, 32:544], daT[:, m, :], sig[:, :], ALU.mult, ["daT", "sig"], ["dT"])
                      yield
                  for s in range(4):
                      pb, pk = nb()
                      for kc in range(8):
                          mm(pb[:, 0:68], hT[:, kc, s * 128:(s + 1) * 128], W[:, kc, 256:324], kc == 0, kc == 7,
                             [("wr", slot)] + HT_ALL, [pk])
                      cp(kraw[:, s, :], pb[:, 0:64], [pk], ["kraw"])
                      cp(wraw[:, s, :], pb[:, 64:68], [pk], ["wraw"])
                      yield
                  stage(4.3)
                  ts(sgn[:, :, :], wraw[:, :, :], 0.0, 2.0, ALU.is_ge, ALU.mult, ["wraw"], ["sgn"])
                  ts(sgn[:, :, :], sgn[:, :, :], -1.0, None, ALU.add, None, ["sgn"], ["sgn"])
                  stt(wabs[:, :, :], wraw[:, :, :], 1.0 / 16.0, sgn[:, :, :], ALU.mult, ALU.mult, ["wraw", "sgn"], ["wabs"])
                  slot, W = load_chunk(l, 5, group="front")
                  for s in range(4):
                      blk = t * 4 + s
                      pb, pk = nb()
                      for kc in range(8):
                          mm(pb[:, :], hT[:, kc, s * 128:(s + 1) * 128], W[:, kc, :], kc == 0, kc == 7,
                             [("wr", slot)] + HT_ALL, [pk])
                      for h in range(4):
                          act(V[:, blk, h * 65:h * 65 + 64], pb[:, h * 64:(h + 1) * 64], AF.Copy,
                              [pk], ["V"])
                      stage(6.5)
                      yield
                      act(qk32[:, 0:256], pb[:, 256:512], AF.Copy, [pk], ["qk32"])
                      rope(qi32[:, :], qk32[:, 0:256], ["qk32"], 4, blk, ["qi32"])
                      for h in range(4):
                          ts(qib[:, h * 64:(h + 1) * 64], qi32[:, h * 64:(h + 1) * 64], wabs[:, s, h:h + 1], None,
                             ALU.mult, None, ["qi32", "wabs"], ["qib"])
                      stage(6.7)
                      yield
                      yield "sync"
                      transpose2([qib[:, 0:128], qib[:, 128:256]], ["qib"],
                                 qiT_all[:, :, s * 128:(s + 1) * 128], ["qiT_all"], evac="act")
                      stage(6.8)
                      rope(ki32[:, :], kraw[:, s, :], ["kraw"], 1, blk, ["ki32"])
                      stage(6.9)
                      yield
                      cp(kib[:, 0:64], ki32[:, :], ["ki32"], ["kib"])
                      cp(kib[:, 64:128], ki32[:, :], ["ki32"], ["kib"])
                      stage(6.91 + 0.02 * s)
                      yield "sync"
                      transpose2([kib[:, :]], ["kib"], kiT2[:, blk * 128:(blk + 1) * 128], ["kiT2"], evac="act")
                      stage(6.92 + 0.02 * s)
                      yield

              def gen_gateup(t):
                  for j in range(11):
                      slw, Wgu = load_chunk(l, 8 + j, group="ffn")
                      for m in range(2):
                          pg = next_pm()
                          pu = next_pm()
                          for kc in range(8):
                              mm(pM[pg][:, :], Wgu[:, kc, m * 128:(m + 1) * 128], hTb[:, kc, :], kc == 0, kc == 7,
                                 [("wr", slw)] + HTB_ALL, [K_ps(pg)])
                          for kc in range(8):
                              mm(pM[pu][:, :], Wgu[:, kc, 256 + m * 128:256 + (m + 1) * 128], hTb[:, kc, :], kc == 0, kc == 7,
                                 [("wr", slw)] + HTB_ALL, [K_ps(pu)])
                          ai = j * 2 + m
                          sgi = ai % 2
                          act(sg[sgi][:, :], pM[pg][:, :], AF.Silu, [K_ps(pg)], [("sg", sgi)])
                          tt(aT[:, ai, :], pM[pu][:, :], sg[sgi][:, :], ALU.mult, [K_ps(pu), ("sg", sgi)],
                             [("aT", ai), "score" if ai < 16 else "mb"])
                          yield

              def gen_down(t):
                  AT_ALL = [("aT", i) for i in range(22)] + ["score", "mb"]
                  for cg in range(2):
                      pis = [next_pm() for _ in range(4)]
                      for g in range(3):
                          nkc = 8 if g < 2 else 6
                          slot, W = load_chunk(l, 19 + cg * 3 + g, nkc, group="ffn")
                          for s in range(4):
                              for kc in range(nkc):
                                  kk = g * 8 + kc
                                  mm(pM[pis[s]][:, :], aT[:, kk, s * 128:(s + 1) * 128], W[:, kc, :],
                                     kk == 0, kk == 21, [("wr", slot)] + AT_ALL, [K_ps(pis[s])])
                          yield
                      for s in range(4):
                          tt(xt[:, s, cg * 512:(cg + 1) * 512], pM[pis[s]][:, :], xt[:, s, cg * 512:(cg + 1) * 512],
                             ALU.add, [K_ps(pis[s]), ("xt", s)], [("xt", s)])
                      yield
                  stage(10)
                  if last:
                      S.op("dve", lambda e: e.memset(ssq[:, :], 0.0), [], [("ssq", i) for i in range(4)])
                      for s in range(4):
                          act(junk[:, :], xt[:, s, :], AF.Square, [("xt", s)], ["ropeA", ("ssq", s)],
                              accum_out=ssq[:, s:s + 1])
                      for s in range(4):
                          ts(rstd[:, s:s + 1], ssq[:, s:s + 1], 1.0 / D, 1e-6, ALU.mult, ALU.add, [("ssq", s)], [("rstd", s)])
                          act(rstd[:, s:s + 1], rstd[:, s:s + 1], AF.Sqrt, [("rstd", s)], [("rstd", s)])
                          S.op("dve", lambda e, s=s: e.reciprocal(out=rstd[:, s:s + 1], in_=rstd[:, s:s + 1]),
                               [("rstd", s)], [("rstd", s)])
                          stt(xt[:, s, :], xt[:, s, :], rstd[:, s:s + 1], gfinB[:, :], ALU.mult, ALU.mult,
                              [("xt", s), ("rstd", s), "gfinB"], [("xt", s)])
                      dma("pool", out[t * 512:(t + 1) * 512, :].rearrange("(s p) d -> p s d", p=128), xt[:, :, :],
                          XT_ALL, ["out"], "ost")
                  else:
                      dma("pool", xs[t * 512:(t + 1) * 512, :].rearrange("(s p) d -> p s d", p=128), xt[:, :, :],
                          XT_ALL, ["xs"], "ost")

              def load_xt(t):
                  dma("pool", xt[:, :, :], x_src[t * 512:(t + 1) * 512, :].rearrange("(s p) d -> p s d", p=128),
                      ["xs"] if l > 0 else [], XT_ALL, "xt")

              gfr0 = gen_front(0)
              while run(gfr0):
                  pass
              load_xt(0)
              for t in range(NTILE):
                  def gen_filler():
                      stage(4.5)
                      cbanks = [(pD, "pD"), (pO, "pO"), (pM[2], K_ps(2)), (pM[3], K_ps(3))]
                      for s in range(4):
                          cbk, cky = cbanks[s]
                          for cc in range(2):
                              for tp in range(31):
                                  mm(cbk[:, cc * 128:(cc + 1) * 128], dT[:, cc, 2 + s * 128 + tp: 2 + s * 128 + tp + 128],
                                     diag[:, cc, tp, :], tp == 0, tp == 30, ["dT", "diag"], [cky])
                          yield
                      for s in range(4):
                          cbk, cky = cbanks[s]
                          tt(t1[:, :], cbk[:, 0:256], lnB[:, 4, :], ALU.add, [cky, "lnB"], ["t1"])
                          stage(4.75)
                          layernorm_tok(t1[:, :], ["t1"], t1[:, :], ["t1"], 2, 3)
                          yield
                          stage(4.8)
                          act(yd[:, :], t1[:, :], AF.Silu, ["t1"], ["yd"])
                          stage(4.85)
                          transpose2([yd[:, 0:128], yd[:, 128:256]], ["yd"],
                                     yT[:, 6:8, s * 128:(s + 1) * 128], [("hy", 3)], evac="act")
                          yield
                          stage(4.9 + 0.01 * s)
                      if os.environ.get("KHALO", "1") == "1":
                          cp(dT[:, :, 0:32], dT[:, :, 512:544], ["dT"], ["dT"])
                      elif os.environ.get("KHALO") == "2":
                          for cc in range(2):
                              act(dT[:, cc, 0:32], dT[:, cc, 512:544], AF.Copy, ["dT"], ["dT"])
                      stage(5)
                      slot, W = load_chunk(l, 3, group="front")
                      for s in range(4):
                          pi = next_pm()
                          for kc in range(8):
                              mm(pM[pi][:, :], hT[:, kc, s * 128:(s + 1) * 128], W[:, kc, :], kc == 0, kc == 7,
                                 [("wr", slot)] + HT_ALL, [K_ps(pi)])
                          act(ub[:, :], pM[pi][:, 0:256], AF.Copy, [K_ps(pi)], ["ub"])
                          act(v32[:, :], pM[pi][:, 256:512], AF.Copy, [K_ps(pi)], ["v32"])
                          layernorm_tok(v32[:, :], ["v32"], vnb[:, :], ["vnb"], 0, 1)
                          yield
                          stage(5.3)
                          gbk, gky = (pD, "pD") if s % 2 == 0 else (pO, "pO")
                          for h in range(4):
                              mm(gbk[:, h * 64:(h + 1) * 64], wsT[:, h, :], vnb[:, h * 64:(h + 1) * 64], True, True,
                                 ["wsT", "vnb"], [gky])
                          for h in range(4):
                              stt(yb[:, h * 64:(h + 1) * 64], gbk[:, h * 64:(h + 1) * 64], bsT[:, h:h + 1],
                                  ub[:, h * 64:(h + 1) * 64], ALU.add, ALU.mult, [gky, "bsT", "ub"], ["yb"])
                          transpose2([yb[:, 0:128], yb[:, 128:256]], ["yb"],
                                     yT[:, 2:4, s * 128:(s + 1) * 128], [("hy", 1)], evac="act")
                          yield
                      stage(6)
                      slot, W = load_chunk(l, 4, group="front")
                      for s in range(4):
                          blk = t * 4 + s
                          pi = next_pm()
                          for kc in range(8):
                              mm(pM[pi][:, :], hT[:, kc, s * 128:(s + 1) * 128], W[:, kc, :], kc == 0, kc == 7,
                                 [("wr", slot)] + HT_ALL, [K_ps(pi)])
                          act(qk32[:, :], pM[pi][:, :], AF.Copy, [K_ps(pi)], ["qk32"])
                          rope(qkr[:, :], qk32[:, :], ["qk32"], 8, blk, ["qkr"])
                          yield
                          i_ = tcount[0] % 2
                          tcount[0] += 1
                          tr(pT[i_][:, 0, :], qkr[:, 0:128], ["qkr"], [("pT", i_)])
                          tr(pT[i_][:, 1, :], qkr[:, 128:256], ["qkr"], [("pT", i_)])
                          act(qbd[0:64, :, s, 0:128], pT[i_][0:64, 0:2, :], AF.Copy, [("pT", i_)], ["qbd"])
                          act(qbd[64:128, :, s, 128:256], pT[i_][64:128, 0:2, :], AF.Copy, [("pT", i_)], ["qbd"])
                          transpose2([qkr[:, 256:384], qkr[:, 384:512]], ["qkr"],
                                     kT[:, :, blk * 128:(blk + 1) * 128], ["kT"], evac="act")
                          yield
                      return

                  stage(7)
                  I4f = I4[:, 0:2, :].rearrange("p h q -> p (h q)")

                  def gen_index(s):
                      qb = t * 4 + s
                      nk = (qb + 1) * 128
                      sc_, sk_ = SC[s % 2]
                      for j in range(4):
                          act(dsgn[:, j, :], identb[:, :], AF.Copy, ["identb", "sgn"], ["dsgn"], scale=sgn[:, s, j:j + 1])
                      nch = (nk + 511) // 512
                      for c in range(nch):
                          k0 = c * 512
                          n = min(512, nk - k0)
                          xb = [(pM[2], K_ps(2)), (pM[3], K_ps(3)), (pTf[0], ("pT", 0)), (pTf[1], ("pT", 1))]
                          for j in range(4):
                              base = (j % 2) * 64
                              mm(xb[j][0][:, 0:n], qiT_all[base:base + 64, j // 2, s * 128:(s + 1) * 128],
                                 kiT2[base:base + 64, k0:k0 + n], True, True, ["qiT_all", "kiT2"], [xb[j][1]])
                          for j in range(4):
                              act(rbuf[:, j, 0:n], xb[j][0][:, 0:n], AF.Relu, [xb[j][1]], [("rbuf", j)])
                          yield
                          for j in range(4):
                              mm(pD[:, 0:n], dsgn[:, j, :], rbuf[:, j, 0:n], j == 0, j == 3, ["dsgn", ("rbuf", j)], ["pD"])
                          act(sc_[:, k0:k0 + n], pD[:, 0:n], AF.Copy, ["pD"], sk_)
                          yield

                  def emit_bisect(s):
                      qb = t * 4 + s
                      nk = (qb + 1) * 128
                      score_t, skey = SC[s % 2]
                      jko = jk[:, 0:1].to_broadcast([128, nk])
                      if qb >= 2:
                          S.op("dve", lambda e, nk=nk: e.reduce_max(out=amax[:, :], in_=score_t[:, 0:nk], axis=AX.X,
                                                                   apply_absolute_value=True), skey, ["amax"])
                      tt(score_t[:, nk - 128:nk], score_t[:, nk - 128:nk], cb[:, :], ALU.add, skey + ["cb"], skey)
                      if qb >= 2:
                          ts(Wt[:, :], pow2[:, :], amax[:, 0:1], None, ALU.mult, None, ["pow2", "amax"], ["Wt"])
                          S.op("dve", lambda e: e.memset(cntb[:, :], 0.0), [], ["cntb"])
                          stt(mid[1][:, :], amax[:, :], -1.0, Wt[:, 1:2], ALU.mult, ALU.add, ["amax", "Wt"], ["mid1"])
                          for k in range(1, NBIS + 1):
                              mc = mid[k % 2]
                              mn = mid[(k + 1) % 2]
                              ts(jko, score_t[:, 0:nk], mc[:, 0:1], 0.0, ALU.is_ge, ALU.add,
                                 skey + ["mid%d" % (k % 2)], ["jk", "cntb"], accum_out=cntb[:, k:k + 1])
                              stt(tq[:, :], cntb[:, k:k + 1], 255.5, Wt[:, k:k + 1], ALU.is_ge, ALU.mult,
                                  ["cntb", "Wt"], ["tq"])
                              if k < NBIS:
                                  stt(mn[:, :], tq[:, :], Wt[:, k + 1:k + 2], mc[:, :], ALU.subtract, ALU.add,
                                      ["tq", "Wt", "mid%d" % (k % 2)], ["mid%d" % ((k + 1) % 2)])
                              else:
                                  stt(thr[:, :], tq[:, :], Wt[:, k:k + 1], mc[:, :], ALU.subtract, ALU.add,
                                      ["tq", "Wt", "mid%d" % (k % 2)], ["thr"])
                      else:
                          ts(thr[:, :], pow2[:, 0:1], 0.0, -1e29, ALU.mult, ALU.add, ["pow2"], ["thr"])

                  def gen_bisectB(s):
                      qb = t * 4 + s
                      nk = (qb + 1) * 128
                      score_t, skey = SC[s % 2]
                      jko = jk[:, 0:1].to_broadcast([128, nk])
                      if qb >= 2:
                          S.op("dve", lambda e, nk=nk: e.reduce_max(out=amaxB[:, :], in_=score_t[:, 0:nk], axis=AX.X,
                                                                   apply_absolute_value=True), skey, ["amaxB"])
                      tt(score_t[:, nk - 128:nk], score_t[:, nk - 128:nk], cb[:, :], ALU.add, skey + ["cb"], skey)
                      if qb >= 2:
                          ts(WtB[:, :], pow2[:, :], amaxB[:, 0:1], None, ALU.mult, None, ["pow2", "amaxB"], ["WtB"])
                          S.op("dve", lambda e: e.memset(cntbB[:, :], 0.0), [], ["cntbB"])
                          stt(midB[1][:, :], amaxB[:, :], -1.0, WtB[:, 1:2], ALU.mult, ALU.add, ["amaxB", "WtB"], ["midB1"])
                          yield
                          for k in range(1, NBIS + 1):
                              mc = midB[k % 2]
                              mn = midB[(k + 1) % 2]
                              ts(jko, score_t[:, 0:nk], mc[:, 0:1], 0.0, ALU.is_ge, ALU.add,
                                 skey + ["midB%d" % (k % 2)], ["jk", "cntbB"], accum_out=cntbB[:, k:k + 1])
                              stt(tqB[:, :], cntbB[:, k:k + 1], 255.5, WtB[:, k:k + 1], ALU.is_ge, ALU.mult,
                                  ["cntbB", "WtB"], ["tqB"])
                              if k < NBIS:
                                  stt(mn[:, :], tqB[:, :], WtB[:, k + 1:k + 2], mc[:, :], ALU.subtract, ALU.add,
                                      ["tqB", "WtB", "midB%d" % (k % 2)], ["midB%d" % ((k + 1) % 2)])
                              else:
                                  stt(thrB[:, :], tqB[:, :], WtB[:, k:k + 1], mc[:, :], ALU.subtract, ALU.add,
                                      ["tqB", "WtB", "midB%d" % (k % 2)], ["thrB"])
                              yield
                      else:
                          ts(thrB[:, :], pow2[:, 0:1], 0.0, -1e29, ALU.mult, ALU.add, ["pow2"], ["thrB"])
                          yield

                  def emit_bisect_act(s, gf, gb=None):
                      qb = t * 4 + s
                      nk = (qb + 1) * 128
                      score_t, skey = SC[s % 2]
                      S.op("dve", lambda e, nk=nk: e.reduce_max(out=amax[:, :], in_=score_t[:, 0:nk], axis=AX.X,
                                                               apply_absolute_value=True), skey, ["amax"])
                      tt(score_t[:, nk - 128:nk], score_t[:, nk - 128:nk], cb[:, :], ALU.add, skey + ["cb"], skey)
                      ts(Wt[:, :], pow2[:, :], amax[:, 0:1], None, ALU.mult, None, ["pow2", "amax"], ["Wt"])
                      S.op("dve", lambda e: e.memset(cntb[:, :], 0.0), [], ["cntb"])
                      stt(mid[1][:, :], amax[:, :], -1.0, Wt[:, 1:2], ALU.mult, ALU.add, ["amax", "Wt"], ["mid1"])
                      for k in range(1, NBIS + 1):
                          mc = mid[k % 2]
                          mn = mid[(k + 1) % 2]
                          S.op("act", lambda e, k=k, mc=mc, nk=nk: e.activation(
                              out=mb[:, 0:nk], in_=score_t[:, 0:nk], func=AF.Sign, bias=mc[:, 0:1], scale=-1.0,
                              accum_out=cntb[:, k:k + 1]), skey + ["mid%d" % (k % 2)], ["mb", "cntb"])
                          if gb is not None:
                              run(gb)
                          run(gf, 2)
                          stt(tq[:, :], cntb[:, k:k + 1], float(nk) - 511.0, Wt[:, k:k + 1], ALU.is_le, ALU.mult,
                              ["cntb", "Wt"], ["tq"])
                          if k < NBIS:
                              stt(mn[:, :], tq[:, :], Wt[:, k + 1:k + 2], mc[:, :], ALU.subtract, ALU.add,
                                  ["tq", "Wt", "mid%d" % (k % 2)], ["mid%d" % ((k + 1) % 2)])
                          else:
                              stt(thr[:, :], tq[:, :], Wt[:, k:k + 1], mc[:, :], ALU.subtract, ALU.add,
                                  ["tq", "Wt", "mid%d" % (k % 2)], ["thr"])

                  def emit_maskB(s):
                      nk = (t * 4 + s + 1) * 128
                      score_t, skey = SC[s % 2]
                      ts(mb[:, 0:nk], score_t[:, 0:nk], thrB[:, 0:1], -1.0, ALU.is_ge, ALU.add,
                         skey + ["thrB"], ["mb"])

                  def emit_mask(s):
                      nk = (t * 4 + s + 1) * 128
                      score_t, skey = SC[s % 2]
                      ts(mb[:, 0:nk], score_t[:, 0:nk], thr[:, 0:1], -1.0, ALU.is_ge, ALU.add,
                         skey + ["thr"], ["mb"])

                  def gen_attn(s):
                      qb = t * 4 + s

                      def st_mm(kb):
                          pi = kb % 2
                          for p in range(2):
                              mm(pM[pi][:, p * 256:(p + 1) * 256], kT[:, p, kb * 128:(kb + 1) * 128], qbd[:, p, s, :],
                                 True, False, ["kT", "qbd"], [K_ps(pi)])
                              mm(pM[pi][:, p * 256:(p + 1) * 256], mb[:, kb * 128:(kb + 1) * 128], I4f,
                                 False, True, ["mb", "I4"], [K_ps(pi)])
                      st_mm(0)
                      for kb in range(qb + 1):
                          pi = kb % 2
                          if kb + 1 <= qb:
                              st_mm(kb + 1)
                          pt = PT[kb % 2]
                          act(pt[:, :], pM[pi][:, :], AF.Exp, [K_ps(pi)], [("PT", kb % 2)], scale=0.125)
                          for h in range(4):
                              mm(pO[:, h * 65:(h + 1) * 65], pt[:, h * 128:(h + 1) * 128], V[:, kb, h * 65:(h + 1) * 65],
                                 (kb == 0 and h == 0), kb == qb, [("PT", kb % 2), "V"], ["pO"])
                          yield

                  def emit_final(s):
                      act(o32[:, :], pO[:, 0:260], AF.Copy, ["pO"], ["o32"])
                      pOv = o32[:, :].rearrange("p (h e) -> p h e", h=4)
                      S.op("dve", lambda e, pOv=pOv: e.reciprocal(out=rec[:, :], in_=pOv[:, :, 64]), ["o32"], ["rec"])
                      for h in range(4):
                          ts(yc[:, h * 64:(h + 1) * 64], o32[:, h * 65:h * 65 + 64], rec[:, h:h + 1], None,
                             ALU.mult, None, ["o32", "rec"], ["yc"])
                      transpose2([yc[:, 0:128], yc[:, 128:256]], ["yc"],
                                 yT[:, 4:6, s * 128:(s + 1) * 128], [("hy", 2)], evac="act")

                  def run(g, n=1):
                      for _ in range(n):
                          try:
                              next(g)
                          except StopIteration:
                              return False
                      return True

                  g0 = gen_index(0)
                  while run(g0):
                      pass
                  g1 = gen_index(1)
                  while run(g1):
                      pass
                  gf = gen_filler()
                  gb1 = gen_bisectB(1)
                  if t * 4 >= 2:
                      emit_bisect_act(0, gf, gb1)
                  else:
                      emit_bisect(0)
                  while run(gb1):
                      pass
                  while run(gf):
                      pass
                  emit_mask(0)
                  for s in range(4):
                      if 1 <= s < 3:
                          emit_bisect(s + 1)
                      ga = gen_attn(s)
                      if s < 2:
                          gi = gen_index(s + 2)
                          alive = True
                          while alive:
                              alive = run(gi)
                              run(ga)
                      while run(ga):
                          pass
                      emit_final(s)
                      if s == 0:
                          emit_maskB(1)
                      elif s < 3:
                          emit_mask(s + 1)
                  if dbg and l == 0 and t == 0:
                      tap("yT", yT[:, :, :], [128, 8, 512], HY_ALL)
                  stage(8)
                  for cg in range(2):
                      slot, W = load_chunk(l, 6 + cg)
                      for s in range(4):
                          pi = next_pm()
                          for kc in range(8):
                              mm(pM[pi][:, :], yT[:, kc, s * 128:(s + 1) * 128], W[:, kc, :], kc == 0, kc == 7,
                                 [("wr", slot)] + HY_ALL, [K_ps(pi)])
                          tt(xt[:, s, cg * 512:(cg + 1) * 512], pM[pi][:, :], xt[:, s, cg * 512:(cg + 1) * 512], ALU.add,
                             [K_ps(pi), ("xt", s)], [("xt", s)])
                  if dbg and l == 0 and t == 0:
                      tap("xmid", xt[:, :, :], [128, 4, 1024], XT_ALL)
                  stage(9)
                  rmsnorm_to_hT(1)
                  AT_ALL = [("aT", i) for i in range(22)] + ["score", "mb"]
                  def gen_ffn(t):
                      yield from gen_gateup(t)
                      yield from gen_down(t)
                  gd = gen_ffn(t)
                  if t + 1 < NTILE:
                      gfr = gen_front(t + 1)
                      a1 = a2_ = True
                      while a1 or a2_:
                          if a1:
                              a1 = run(gd)
                          k_ = 0
                          while a2_ and k_ < 3:
                              try:
                                  r_ = next(gfr)
                              except StopIteration:
                                  a2_ = False
                                  break
                              k_ += 1
                              if r_ == "sync" and a1:
                                  break
                      load_xt(t + 1)
                  else:
                      while run(gd):
                          pass
        except _Stop:
            dma("pool", out[0:512, :].rearrange("(s p) d -> p s d", p=128), xt[:, :, :],
                [("xt", s) for s in range(4)], ["out"], "ost")
        S.final_wait("pool", ["out"] + ["dbg_" + k for k in dbg_outs])
        S.final_wait("sp", ["out"] + ["dbg_" + k for k in dbg_outs])
        S.emit()
    return nc


def make_consts(SEQ):
    NB = SEQ // 128
    f32 = np.float32
    ident = np.eye(128, dtype=f32)
    tril = np.tril(np.ones((128, 128), dtype=f32))
    cbm = np.where(np.arange(128)[None, :] <= np.arange(128)[:, None], 0.0, -1e30).astype(f32)
    inv_freq = (f32(10000.0) ** (-(np.arange(0, 64, 2, dtype=f32)) / f32(64))).astype(f32)
    ang = (np.arange(SEQ, dtype=f32)[:, None] * inv_freq[None, :]).astype(f32)
    cos = np.cos(ang).astype(f32).reshape(NB, 128, 32).transpose(1, 0, 2)
    sin = np.sin(ang).astype(f32).reshape(NB, 128, 32).transpose(1, 0, 2)
    pw = (2.002 * 2.0 ** (-np.arange(NBIS + 2, dtype=np.float64))).astype(f32)
    pow2 = np.broadcast_to(pw[None, :], (128, NBIS + 2))
    return {"c_ident": ident, "c_tril": tril, "c_cb": cbm, "c_cos": np.ascontiguousarray(cos),
            "c_sin": np.ascontiguousarray(sin), "c_pow2": np.ascontiguousarray(pow2)}


_NC_CACHE = {}


def kernel(**inputs):
    x = np.asarray(inputs["x"], dtype=np.float32)
    B, SEQ, _ = x.shape
    DEPTH = inputs["w_in"].shape[0]
    key = (SEQ, DEPTH)
    if key not in _NC_CACHE:
        _NC_CACHE[key] = build(SEQ, DEPTH)
    nc = _NC_CACHE[key]
    consts = make_consts(SEQ)
    params = {k: np.ascontiguousarray(np.asarray(v, dtype=np.float32)) for k, v in inputs.items() if k != "x"}
    n = 8
    in_maps = []
    for c in range(n):
        m = {"x": np.ascontiguousarray(x[c % B])}
        m.update(params)
        m.update(consts)
        in_maps.append(m)
    res = run_bass_kernel_spmd(nc, in_maps, core_ids=list(range(n)))
    outp = np.stack([np.asarray(res.results[b]["out"], dtype=np.float32) for b in range(B)], axis=0)
    return outp
```
